# Optimizing a Trainium2 kernel written in Bass

```python
import jax, jax.numpy as jnp
from jax import lax
import numpy as np

D_MODEL = 1024
BATCH = 2
SEQ = 8192
DEPTH = 1

N_ATTN_HEADS = 8
ATTN_HEAD_DIM = 64
ATTN_WIDTH = N_ATTN_HEADS * ATTN_HEAD_DIM
N_SGU_GROUPS = 8
SGU_WIDTH = D_MODEL // 2
SGU_GROUP_DIM = SGU_WIDTH // N_SGU_GROUPS
CHUNK = 128
Q_BLOCK = 128
D_FF = 4 * D_MODEL
N_BRANCH = 2
EPS = 1e-6
IN_SPLITS = (2 * SGU_WIDTH, ATTN_WIDTH, ATTN_WIDTH, ATTN_WIDTH, N_ATTN_HEADS, N_BRANCH * D_MODEL)
IN_WIDTH = sum(IN_SPLITS)
IN_OFFSETS = tuple(int(o) for o in np.cumsum(IN_SPLITS)[:-1])

kernel_name = "hybrid_gmlp_fox_gated_block"


def rmsnorm(x, g):
    x32 = x.astype(jnp.float32)
    y = x32 * lax.rsqrt(jnp.mean(x32 * x32, axis=-1, keepdims=True) + EPS)
    return y.astype(x.dtype) * g


def layernorm(x, g, b):
    x32 = x.astype(jnp.float32)
    mu = jnp.mean(x32, axis=-1, keepdims=True)
    xc = x32 - mu
    y = xc * lax.rsqrt(jnp.mean(xc * xc, axis=-1, keepdims=True) + EPS)
    return y.astype(x.dtype) * g + b


def chunked_sgu(z, g_sgu, b_sgu, w_spatial, b_spatial):
    B, S, _ = z.shape
    u, v = z[..., :SGU_WIDTH], z[..., SGU_WIDTH:]
    v = layernorm(v, g_sgu, b_sgu)
    vc = v.reshape(B, S // CHUNK, CHUNK, N_SGU_GROUPS, SGU_GROUP_DIM)
    causal = jnp.tril(jnp.ones((CHUNK, CHUNK), dtype=bool))
    ws = jnp.where(causal[None], w_spatial, jnp.zeros_like(w_spatial))
    s = jnp.einsum('gts,bcsgd->bctgd', ws, vc)
    s = s + b_spatial.T[None, None, :, :, None]
    return u * s.reshape(B, S, SGU_WIDTH)


def forgetting_attention(q, k, v, cum):
    B, H, S, d = q.shape
    nb = S // Q_BLOCK
    qb = q.reshape(B, H, nb, Q_BLOCK, d).transpose(2, 0, 1, 3, 4)
    cb = cum.reshape(B, H, nb, Q_BLOCK).transpose(2, 0, 1, 3)
    key_pos = jnp.arange(S)
    scale = d ** -0.5

    def one_block(args):
        q_blk, c_blk, i = args
        s = jnp.einsum('bhqd,bhkd->bhqk', q_blk, k).astype(jnp.float32) * scale
        s = s + c_blk[..., :, None] - cum[..., None, :]
        q_pos = i * Q_BLOCK + jnp.arange(Q_BLOCK)
        mask = key_pos[None, :] <= q_pos[:, None]
        s = jnp.where(mask, s, -jnp.inf)
        p = jax.nn.softmax(s, axis=-1)
        return jnp.einsum('bhqk,bhkd->bhqd', p.astype(v.dtype), v)

    out = lax.map(one_block, (qb, cb, jnp.arange(nb)))
    return out.transpose(1, 2, 0, 3, 4).reshape(B, H, S, d)


def setup_inputs(seed: int = 0) -> dict:
    key = jax.random.key(seed)
    ks = jax.random.split(key, 20)
    L = DEPTH

    def nrm(k, shape, scale):
        return jax.random.normal(k, shape, jnp.float32) * scale

    def gain(k, shape):
        return 1.0 + 0.05 * jax.random.normal(k, shape, jnp.float32)

    return {
        "x": jax.random.normal(ks[0], (BATCH, SEQ, D_MODEL), jnp.float32),
        "g_mix_pre": gain(ks[1], (L, D_MODEL)),
        "w_in": nrm(ks[2], (L, D_MODEL, IN_WIDTH), D_MODEL ** -0.5),
        "b_forget": 2.0 + 0.5 * jax.random.normal(ks[3], (L, N_ATTN_HEADS), jnp.float32),
        "g_sgu": gain(ks[4], (L, SGU_WIDTH)),
        "b_sgu": nrm(ks[5], (L, SGU_WIDTH), 0.02),
        "w_spatial": nrm(ks[6], (L, N_SGU_GROUPS, CHUNK, CHUNK), CHUNK ** -0.5),
        "b_spatial": 1.0 + 0.1 * jax.random.normal(ks[7], (L, N_SGU_GROUPS, CHUNK), jnp.float32),
        "w_branch_sgu": nrm(ks[8], (L, SGU_WIDTH, D_MODEL), SGU_WIDTH ** -0.5),
        "w_branch_attn": nrm(ks[9], (L, ATTN_WIDTH, D_MODEL), ATTN_WIDTH ** -0.5),
        "w_out": nrm(ks[10], (L, D_MODEL, D_MODEL), D_MODEL ** -0.5),
        "g_mix_post": gain(ks[11], (L, D_MODEL)),
        "g_ffn_pre": gain(ks[12], (L, D_MODEL)),
        "w_up": nrm(ks[13], (L, D_MODEL, D_FF), D_MODEL ** -0.5),
        "w_down": nrm(ks[14], (L, D_FF, D_MODEL), D_FF ** -0.5),
        "g_ffn_post": gain(ks[15], (L, D_MODEL)),
    }


def reference(x, g_mix_pre, w_in, b_forget, g_sgu, b_sgu, w_spatial, b_spatial,
              w_branch_sgu, w_branch_attn, w_out, g_mix_post, g_ffn_pre, w_up, w_down,
              g_ffn_post):
    B, S, _ = x.shape
    h = x
    for l in range(DEPTH):
        xn = rmsnorm(h, g_mix_pre[l])
        proj = xn @ w_in[l]
        z_sgu, q, k, v, f_logit, gate_logit = jnp.split(proj, IN_OFFSETS, axis=-1)

        y_sgu = chunked_sgu(jax.nn.gelu(z_sgu), g_sgu[l], b_sgu[l], w_spatial[l], b_spatial[l])

        def heads(t):
            return t.reshape(B, S, N_ATTN_HEADS, ATTN_HEAD_DIM).transpose(0, 2, 1, 3)
        log_f = jax.nn.log_sigmoid((f_logit + b_forget[l]).astype(jnp.float32))
        cum = jnp.cumsum(log_f, axis=1).transpose(0, 2, 1)
        y_attn = forgetting_attention(heads(q), heads(k), heads(v), cum)
        y_attn = y_attn.transpose(0, 2, 1, 3).reshape(B, S, ATTN_WIDTH)

        gates = jax.nn.sigmoid(gate_logit)
        merged = (gates[..., :D_MODEL] * (y_sgu @ w_branch_sgu[l])
                  + gates[..., D_MODEL:] * (y_attn @ w_branch_attn[l]))
        h = h + rmsnorm(merged @ w_out[l], g_mix_post[l])

        xn2 = rmsnorm(h, g_ffn_pre[l])
        hid = jnp.square(jax.nn.relu(xn2 @ w_up[l]))
        h = h + rmsnorm(hid @ w_down[l], g_ffn_post[l])
    return h
```

```python
from contextlib import ExitStack
import math
import numpy as np
import concourse.bass as bass
import concourse.mybir as mybir
from concourse.bass_utils import run_bass_kernel_spmd

F32 = mybir.dt.float32
BF16 = mybir.dt.bfloat16
AF = mybir.ActivationFunctionType
ALU = mybir.AluOpType

D = 1024
NEG = -1.0e5
EPS = 1e-6


class Buf:
    def __init__(self, name=""):
        self.name = name
        self.w = None
        self.r = {}


class Tracker:
    def __init__(self, nc, stack):
        self.nc = nc
        self.stack = stack
        self.engs = {"pe": nc.tensor, "act": nc.scalar, "dve": nc.vector,
                     "pool": nc.gpsimd, "sp": nc.sync}
        self.semobj = {}
        self.cnt = {}
        for k in ["pe", "act", "dve", "pool"]:
            self.semobj[k] = stack.enter_context(nc.semaphore("s_" + k))
            self.cnt[k] = 0
        self.waited = {k: {} for k in self.engs}
        self.shared = {"c0", "c1", "c2", "c3"}
        self.nwaits = 0
        self.nops = 0

    def _wait(self, eng, ev):
        if ev is None:
            return
        key, val = ev
        if key in self.shared:
            val = self.cnt[key]
        if key == eng and eng == "pe":
            return
        if self.waited[eng].get(key, 0) >= val:
            return
        self.engs[eng].wait_ge(self.semobj[key], val)
        self.waited[eng][key] = val
        self.nwaits += 1

    def _deps(self, eng, reads, writes):
        for b in reads:
            self._wait(eng, b.w)
        for b in writes:
            self._wait(eng, b.w)
            for k, v in b.r.items():
                self._wait(eng, (k, v))

    def _record(self, ev, reads, writes):
        k, v = ev
        for b in reads:
            if b.r.get(k, 0) < v:
                b.r[k] = v
        for b in writes:
            b.w = ev
            b.r = {}

    def op(self, eng, fn, reads=(), writes=(), signal=True):
        self._deps(eng, reads, writes)
        ins = fn(self.engs[eng])
        self.nops += 1
        if signal:
            self.cnt[eng] += 1
            ins.then_inc(self.semobj[eng], 1)
            ev = (eng, self.cnt[eng])
        else:
            ev = (eng, self.cnt[eng] + 1)
        self._record(ev, reads, writes)
        return ev

    def barrier(self):
        for eng in self.engs:
            for k in list(self.semobj.keys()):
                if self.cnt[k] > 0:
                    self._wait(eng, (k, self.cnt[k]))

    def dma(self, q, fn, reads, writes, sem):
        if sem not in self.semobj:
            self.semobj[sem] = self.stack.enter_context(self.nc.semaphore("d_" + sem))
            self.cnt[sem] = 0
        self._deps(q, reads, writes)
        ins = fn(self.engs[q])
        self.nops += 1
        self.cnt[sem] += 16
        ins.then_inc(self.semobj[sem], 16)
        ev = (sem, self.cnt[sem])
        self._record(ev, reads, writes)
        return ev


def build_nc(NT, dbg=False):
    NO = NT // 4
    NGA = NT // 4
    NGO = NO // 4
    S = NT * 128
    SO = NO * 128
    nc = bass.Bass("TRN2", target_bir_lowering=False)

    def din(name, shape):
        return nc.dram_tensor(name, shape, F32, kind="ExternalInput").ap()

    xf = din("xf", [S, D])
    xo = din("xo", [SO, D])
    wkvf = din("wkvf", [D, 1032])
    wq = din("wq", [D, 512])
    wu = din("wu", [D, 512])
    wv2 = din("wv2", [D, 512])
    wg = din("wg", [D, 2048])
    wbs = din("wbs", [512, D])
    wba = din("wba", [512, D])
    wout = din("wout", [D, D])
    wup = din("wup", [D, 4096])
    wdn = din("wdn", [4096, D])
    g_pre = din("g_pre", [1, D])
    g_post = din("g_post", [1, D])
    g_fpre = din("g_fpre", [1, D])
    g_fpost = din("g_fpost", [1, D])
    g_sgu = din("g_sgu", [1, 512])
    b_sgu = din("b_sgu", [1, 512])
    b_fg = din("b_fg", [1, 8])
    wsT = din("wsT", [128, 8 * 128])
    bsp4 = din("bsp4", [128, 4 * 512])
    c_id = din("c_id", [128, 128])
    c_tri = din("c_tri", [128, 128])
    c_one = din("c_one", [128, 128])
    c_sel = din("c_sel", [128, 4])
    c_madd = din("c_madd", [128, 4 * 128])
    kscr = nc.dram_tensor("kscr", [512, S], BF16).ap()
    h1scr = nc.dram_tensor("h1scr", [SO, D], F32).ap()
    out = nc.dram_tensor("out", [SO, D], F32, kind="ExternalOutput").ap()
    dbgo = {}
    if dbg:
        dbgo["d_cum"] = nc.dram_tensor("d_cum", [128, NT * 8], F32, kind="ExternalOutput").ap()
        dbgo["d_yatt"] = nc.dram_tensor("d_yatt", [128, NO * 512], BF16, kind="ExternalOutput").ap()
        dbgo["d_h1"] = nc.dram_tensor("d_h1", [SO, D], F32, kind="ExternalOutput").ap()

    with ExitStack() as st0:
        T = Tracker(nc, st0)

        def sb(stk, name, shape, dt):
            return stk.enter_context(nc.sbuf_tensor(name, shape, dt))

        def sbr(stk, name, shape, dt, n):
            return [(sb(stk, f"{name}{i}", shape, dt), Buf(f"{name}{i}")) for i in range(n)]

        def wload(dst, src, buf, sem, kc):
            T.dma("pool", lambda e: e.dma_start(out=dst, in_=src.rearrange("(kc p) n -> p kc n", p=128)),
                  [], [buf], sem)

        wsrc = {"wu": (wu, [D, 512]), "wv2": (wv2, [D, 512]), "wg": (wg, [D, 2048]), "wbs": (wbs, [512, D]),
                "wba": (wba, [512, D]), "wout": (wout, [D, D]), "wup": (wup, [D, 4096]), "wdn": (wdn, [4096, D])}
        wscr = {k: nc.dram_tensor("scr_" + k, [128, shp[0] // 128 * shp[1]], BF16).ap() for k, (_, shp) in wsrc.items()}
        b_wscr = {k: Buf("scr_" + k) for k in wsrc}

        def precast(k):
            src, shp = wsrc[k]
            rows, n = shp
            KC = rows // 128
            if k == "wup":
                dstv = wscr[k].rearrange("p (pc kc n) -> p pc kc n", pc=8, kc=8)
                for pc in range(8):
                    T.dma("pool", lambda e, pc=pc: e.dma_start(
                        out=dstv[:, pc, :, :], in_=src[:, pc * 512:(pc + 1) * 512].rearrange("(kc p) n -> p kc n", p=128)),
                        [], [b_wscr[k]], "pc_" + k)
                return
            dstv = wscr[k].rearrange("p (kc n) -> p kc n", kc=KC)
            step = 4 if KC >= 4 else KC
            for c0 in range(0, KC, step):
                T.dma("pool", lambda e, c0=c0: e.dma_start(
                    out=dstv[:, c0:c0 + step, :],
                    in_=src[c0 * 128:(c0 + step) * 128, :].rearrange("(kc p) n -> p kc n", p=128)),
                    [], [b_wscr[k]], "pc_" + k)

        wl_n = [0]

        def wload2(dst, k, buf, nsplit=2):
            KC = dst.shape[1]
            srcv = wscr[k].rearrange("p (kc n) -> p kc n", kc=KC)
            step = max(1, KC // nsplit)
            for c0 in range(0, KC, step):
                q = "sp" if wl_n[0] % 2 == 0 else "act"
                wl_n[0] += 1
                T.dma(q, lambda e, c0=c0: e.dma_start(out=dst[:, c0:c0 + step, :], in_=srcv[:, c0:c0 + step, :]),
                      [b_wscr[k]], [buf], f"w_{k}")

        b_kscr = Buf("kscr")
        b_h1scr = Buf("h1scr")
        pf = []
        pb = []
        ps_stack = [None]
        ps_gen = [0]

        def psum_std():
            stk = ExitStack()
            ps_stack[0] = stk
            k = ps_gen[0]
            ps_gen[0] += 1
            pf[:] = [(stk.enter_context(nc.psum_tensor(f"pf{k}_{i}", [128, 512], F32)), Buf(f"pf{i}")) for i in range(6)]
            pb[:] = [(stk.enter_context(nc.psum_tensor(f"pb{k}_{i}", [128, 1024], BF16)), Buf(f"pb{i}")) for i in range(2)]

        def psum_free():
            ps_stack[0].close()
            ps_stack[0] = None

        psum_std()

        identb = sb(st0, "identb", [128, 128], BF16); b_identb = Buf()
        triU = sb(st0, "triU", [128, 128], F32); b_triU = Buf()
        onesf = sb(st0, "onesf", [128, 128], F32); b_onesf = Buf()
        gbc = sb(st0, "gbc", [128, D], F32); b_gbc = Buf()
        bfbc = sb(st0, "bfbc", [128, 8], F32); b_bfbc = Buf()
        sel = sb(st0, "sel", [128, 4], F32); b_sel = Buf()
        madd = sb(st0, "madd", [128, 4, 128], F32); b_madd = Buf()
        cumsp = sb(st0, "cumsp", [128, NT, 8], F32); b_cumsp = Buf()
        carry = sb(st0, "carry", [128, 8], F32); b_carry = Buf()
        junk = sb(st0, "junk", [128, D], BF16); b_junk = Buf()
        stat = sbr(st0, "stat", [128, 8], F32, 4)

        T.dma("pool", lambda e: e.dma_start(out=identb[:], in_=c_id), [], [b_identb], "c0")
        T.dma("sp", lambda e: e.dma_start(out=triU[:], in_=c_tri), [], [b_triU], "c1")
        T.dma("sp", lambda e: e.dma_start(out=onesf[:], in_=c_one), [], [b_onesf], "c1")
        T.dma("sp", lambda e: e.dma_start(out=gbc[:], in_=g_pre.partition_broadcast(128)), [], [b_gbc], "c1")
        T.dma("sp", lambda e: e.dma_start(out=bfbc[:], in_=b_fg.partition_broadcast(128)), [], [b_bfbc], "c1")
        T.dma("sp", lambda e: e.dma_start(out=sel[:], in_=c_sel), [], [b_sel], "c1")
        T.dma("sp", lambda e: e.dma_start(out=madd[:], in_=c_madd.rearrange("p (m t) -> p m t", m=4)),
              [], [b_madd], "c1")
        T.op("dve", lambda e: e.memset(carry[:], 0.0), [], [b_carry])

        fr_n = [0]

        def rms_scale(x_ap, b_x, xs_ap, b_xs, eng, pre_scale=1.0, extra_bias=0.0):
            s, b_s = stat[fr_n[0] % 4]
            fr_n[0] += 1
            T.op("dve", lambda e: e.memset(s[:, 0:1], 0.0), [], [b_s])
            T.op("act", lambda e: e.activation(out=junk[:], in_=x_ap, func=AF.Square, scale=pre_scale,
                                               accum_out=s[:, 0:1]), [b_x], [b_junk, b_s])
            T.op("act", lambda e: e.activation(out=s[:, 1:2], in_=s[:, 0:1], func=AF.Ln, scale=1.0 / D, bias=EPS),
                 [b_s], [b_s])
            T.op("act", lambda e: e.activation(out=s[:, 2:3], in_=s[:, 1:2], func=AF.Exp, scale=-0.5,
                                               bias=extra_bias), [b_s], [b_s])
            T.op("dve", lambda e: e.scalar_tensor_tensor(out=xs_ap, in0=x_ap, scalar=s[:, 2:3], in1=gbc[:],
                                                       op0=ALU.mult, op1=ALU.mult), [b_x, b_s, b_gbc], [b_xs])

        def transpose8(xs_t, b_xs, dst_ap, b_dst, pbi, evac_eng, n=8):
            pbt, b_pb = pb[pbi]
            for c in range(n):
                T.op("pe", lambda e, c=c: e.transpose(out=pbt[:, c * 128:(c + 1) * 128],
                                                      in_=xs_t[:, c * 128:(c + 1) * 128], identity=identb[:]),
                     [b_xs, b_identb], [b_pb], signal=(c == n - 1))
            src = pbt[:, 0:n * 128].rearrange("p (c t) -> p c t", c=n)
            if evac_eng == "act":
                T.op("act", lambda e: e.copy(out=dst_ap, in_=src), [b_pb], [b_dst])
            else:
                T.op("dve", lambda e: e.tensor_copy(out=dst_ap, in_=src), [b_pb], [b_dst])

        with ExitStack() as stCD:
            yatt = sb(stCD, "yatt", [128, NO, 512], BF16); b_yatt = Buf()

            with ExitStack() as stABC:
                Vall = sb(stABC, "Vall", [128, NT, 8, 65], BF16); b_Vall = Buf()
                QT = sb(stABC, "QT", [67, 8, SO], BF16); b_QT = Buf()
                T.op("pool", lambda e: e.memset(Vall[:, :, :, 64:65], 1.0), [], [b_Vall])

                with ExitStack() as stAB:
                    Wkvf = sb(stAB, "Wkvf", [128, 8, 1032], BF16); b_Wkvf = Buf()
                    Wq = sb(stAB, "Wq", [128, 8, 512], BF16); b_Wq = Buf()
                    wload(Wkvf[:], wkvf, b_Wkvf, "w_kvf", 8)
                    wload(Wq[:], wq, b_Wq, "w_q", 8)
                    xa = sbr(stAB, "xa", [128, D], F32, 6)
                    xs = sbr(stAB, "xs", [128, D], BF16, 4)
                    xsT = sbr(stAB, "xsT", [128, 8, 512], BF16, 2)
                    kst = sbr(stAB, "kst", [128, 4, 512], BF16, 2)
                    ftmp = sbr(stAB, "ftmp", [128, 32], F32, 2)
                    spt = sbr(stAB, "spt", [128, 32], F32, 2)
                    CS = sbr(stAB, "CS", [128, 8, 67], BF16, 2)
                    co = sbr(stAB, "co", [128, 8], F32, 2)
                    co2 = sbr(stAB, "co2", [128, 8], F32, 2)
                    for c_, b_ in CS:
                        T.op("pool", lambda e, c_=c_: e.memset(c_[:], 0.0), [], [b_])

                    tiles = [("A", t) for t in range(NT)] + [("B", t) for t in range(NO)]
                    NTT = len(tiles)

                    XR = 6
                    GT = NGA + NGO
                    fstat = {}

                    def loadG(g):
                        for l in range(4):
                            i = 4 * g + l
                            kind, t = tiles[i]
                            src = xf if kind == "A" else xo
                            xt, b_x = xa[i % XR]
                            T.dma("sp", lambda e, xt=xt, src=src, t=t: e.dma_start(out=xt[:], in_=src[t * 128:(t + 1) * 128, :]),
                                  [], [b_x], f"xa{i % XR}")

                    def frontG(g):
                        s_, b_s = stat[fr_n[0] % 4]
                        fr_n[0] += 1
                        T.op("dve", lambda e: e.memset(s_[:, 0:4], 0.0), [], [b_s])
                        for l in range(4):
                            xt, b_x = xa[(4 * g + l) % XR]
                            T.op("act", lambda e, xt=xt, l=l: e.activation(out=junk[:], in_=xt[:], func=AF.Square,
                                                                           accum_out=s_[:, l:l + 1]), [b_x], [b_junk, b_s])
                        T.op("act", lambda e: e.activation(out=s_[:, 4:8], in_=s_[:, 0:4], func=AF.Ln, scale=1.0 / D, bias=EPS),
                             [b_s], [b_s])
                        T.op("act", lambda e: e.activation(out=s_[:, 4:8], in_=s_[:, 4:8], func=AF.Exp, scale=-0.5), [b_s], [b_s])
                        fstat[g] = (s_, b_s)

                    def frontG2(g):
                        s_, b_s = fstat[g]
                        for l in range(4):
                            xt, b_x = xa[(4 * g + l) % XR]
                            xst, b_xs = xs[l]
                            T.op("dve", lambda e, xt=xt, xst=xst, l=l: e.scalar_tensor_tensor(
                                out=xst[:], in0=xt[:], scalar=s_[:, 4 + l:5 + l], in1=gbc[:], op0=ALU.mult, op1=ALU.mult),
                                [b_x, b_s, b_gbc], [b_xs])

                    def transG(g):
                        dT, b_dT = xsT[g % 2]
                        for l in range(4):
                            xst, b_xs = xs[l]
                            transpose8(xst, b_xs, dT[:, :, l * 128:(l + 1) * 128], b_dT, l % 2,
                                       "act" if l % 2 == 0 else "dve")

                    def backA(g):
                        dT, b_dT = xsT[g % 2]
                        ks, b_ks = kst[g % 2]
                        for hp in range(4):
                            pk, b_pk = pf[hp % 2]
                            for kc in range(8):
                                T.op("pe", lambda e, kc=kc, hp=hp, pk=pk: e.matmul(
                                    pk[:], lhsT=Wkvf[:, kc, hp * 128:(hp + 1) * 128], rhs=dT[:, kc, :],
                                    start=(kc == 0), stop=(kc == 7)), [b_Wkvf, b_dT], [b_pk], signal=(kc == 7))
                            T.op("dve", lambda e, hp=hp, pk=pk: e.tensor_copy(out=ks[:, hp, :], in_=pk[:]),
                                 [b_pk], [b_ks])
                        T.dma("sp", lambda e: e.dma_start(
                            out=kscr.rearrange("(hp p) s -> p hp s", p=128)[:, :, g * 512:(g + 1) * 512],
                            in_=ks[:]), [b_ks], [b_kscr], f"kst{g % 2}")

                    def backA2(g):
                        dT, b_dT = xsT[g % 2]
                        for l in range(4):
                            pv, b_pv = pf[2 + l % 2]
                            for kc in range(8):
                                T.op("pe", lambda e, kc=kc, l=l, pv=pv: e.matmul(
                                    pv[:], lhsT=dT[:, kc, l * 128:(l + 1) * 128], rhs=Wkvf[:, kc, 512:1024],
                                    start=(kc == 0), stop=(kc == 7)), [b_Wkvf, b_dT], [b_pv], signal=(kc == 7))
                            T.op("act", lambda e, l=l, pv=pv: e.copy(
                                out=Vall[:, 4 * g + l, :, 0:64], in_=pv[:].rearrange("p (h d) -> p h d", h=8)),
                                [b_pv], [b_Vall])
                        pF, b_pF = pf[4]
                        for l in range(4):
                            for kc in range(8):
                                T.op("pe", lambda e, kc=kc, l=l: e.matmul(
                                    pF[:, l * 8:(l + 1) * 8], lhsT=dT[:, kc, l * 128:(l + 1) * 128],
                                    rhs=Wkvf[:, kc, 1024:1032], start=(kc == 0), stop=(kc == 7)),
                                    [b_Wkvf, b_dT], [b_pF], signal=(kc == 7 and l == 3))
                        ft, b_ft = ftmp[g % 2]
                        sp_, b_sp = spt[g % 2]
                        T.op("dve", lambda e: e.tensor_tensor(
                            out=ft[:].rearrange("p (l h) -> p l h", l=4), in0=pF[:, 0:32].rearrange("p (l h) -> p l h", l=4),
                            in1=bfbc[:].unsqueeze(1).broadcast_to([128, 4, 8]), op=ALU.add), [b_pF, b_bfbc], [b_ft])
                        T.op("act", lambda e: e.activation(out=ft[:], in_=ft[:], func=AF.Exp, scale=-1.0), [b_ft], [b_ft])
                        T.op("act", lambda e: e.activation(out=sp_[:], in_=ft[:], func=AF.Ln, bias=1.0), [b_ft], [b_sp])

                    def backA3(g):
                        sp_, b_sp = spt[g % 2]
                        pC, b_pC = pf[5]
                        for l in range(4):
                            for l2 in range(l):
                                T.op("pe", lambda e, l=l, l2=l2: e.matmul(
                                    pC[:, l * 8:(l + 1) * 8], lhsT=onesf[:], rhs=sp_[:, l2 * 8:(l2 + 1) * 8],
                                    start=(l2 == 0), stop=False), [b_onesf, b_sp], [b_pC], signal=False)
                            T.op("pe", lambda e, l=l: e.matmul(
                                pC[:, l * 8:(l + 1) * 8], lhsT=triU[:], rhs=sp_[:, l * 8:(l + 1) * 8],
                                start=(l == 0), stop=True), [b_triU, b_sp], [b_pC], signal=False)
                        for l2 in range(4):
                            T.op("pe", lambda e, l2=l2: e.matmul(
                                pC[:, 32:40], lhsT=onesf[:], rhs=sp_[:, l2 * 8:(l2 + 1) * 8],
                                start=(l2 == 0), stop=(l2 == 3)), [b_onesf, b_sp], [b_pC], signal=(l2 == 3))
                        T.op("dve", lambda e: e.tensor_tensor(
                            out=cumsp[:, 4 * g:4 * g + 4, :], in0=pC[:, 0:32].rearrange("p (l h) -> p l h", l=4),
                            in1=carry[:].unsqueeze(1).broadcast_to([128, 4, 8]), op=ALU.add),
                            [b_pC, b_carry], [b_cumsp])
                        T.op("dve", lambda e: e.tensor_tensor(out=carry[:], in0=carry[:], in1=pC[:, 32:40], op=ALU.add),
                             [b_pC, b_carry], [b_carry])

                    def backB(go):
                        g = NGA + go
                        dT, b_dT = xsT[g % 2]
                        for h in range(8):
                            pq, b_pq = pf[h % 2]
                            for kc in range(8):
                                T.op("pe", lambda e, kc=kc, h=h, pq=pq: e.matmul(
                                    pq[0:64, :], lhsT=Wq[:, kc, h * 64:(h + 1) * 64], rhs=dT[:, kc, :],
                                    start=(kc == 0), stop=(kc == 7)), [b_Wq, b_dT], [b_pq], signal=(kc == 7))
                            if h % 2 == 0:
                                T.op("act", lambda e, h=h, pq=pq: e.copy(
                                    out=QT[0:64, h, go * 512:(go + 1) * 512], in_=pq[0:64, :]), [b_pq], [b_QT])
                            else:
                                T.op("dve", lambda e, h=h, pq=pq: e.tensor_copy(
                                    out=QT[0:64, h, go * 512:(go + 1) * 512], in_=pq[0:64, :]), [b_pq], [b_QT])
                        for l in range(4):
                            i = 4 * go + l
                            c1, b_c1 = co[i % 2]
                            c2, b_c2 = co2[i % 2]
                            cs, b_cs = CS[i % 2]
                            T.op("dve", lambda e, i=i, c1=c1: e.tensor_scalar(
                                out=c1[:], in0=cumsp[:, 4 * i, :], scalar1=sel[:, 0:1], scalar2=None, op0=ALU.mult),
                                [b_cumsp, b_sel], [b_c1])
                            for m in range(1, 4):
                                T.op("dve", lambda e, i=i, m=m, c1=c1: e.scalar_tensor_tensor(
                                    out=c1[:], in0=cumsp[:, 4 * i + m, :], scalar=sel[:, m:m + 1], in1=c1[:],
                                    op0=ALU.mult, op1=ALU.add), [b_cumsp, b_sel, b_c1], [b_c1])
                            T.op("dve", lambda e, c1=c1: e.tensor_scalar(
                                out=c1[:], in0=c1[:], scalar1=-8.0, scalar2=None, op0=ALU.mult), [b_c1], [b_c1])
                            T.op("dve", lambda e, c1=c1, cs=cs: e.tensor_copy(out=cs[:, :, 64], in_=c1[:]), [b_c1], [b_cs])
                            T.op("dve", lambda e, c1=c1, c2=c2, cs=cs: e.tensor_tensor(
                                out=c2[:], in0=c1[:], in1=cs[:, :, 64], op=ALU.subtract), [b_c1, b_cs], [b_c2])
                            T.op("dve", lambda e, c2=c2, cs=cs: e.tensor_copy(out=cs[:, :, 65], in_=c2[:]), [b_c2], [b_cs])
                            T.op("dve", lambda e, c1=c1, c2=c2, cs=cs: e.tensor_tensor(
                                out=c1[:], in0=c2[:], in1=cs[:, :, 65], op=ALU.subtract), [b_c2, b_cs], [b_c1])
                            T.op("dve", lambda e, c1=c1, cs=cs: e.tensor_copy(out=cs[:, :, 66], in_=c1[:]), [b_c1], [b_cs])
                            for half in range(2):
                                pa, b_pa = pf[2 + half]
                                for hh in range(4):
                                    h = half * 4 + hh
                                    T.op("pe", lambda e, h=h, hh=hh, pa=pa, cs=cs: e.matmul(
                                        pa[0:67, hh * 128:(hh + 1) * 128], lhsT=cs[:, h, :], rhs=identb[:],
                                        start=True, stop=True), [b_cs, b_identb], [b_pa], signal=(hh == 3))
                                T.op("act", lambda e, half=half, pa=pa, i=i: e.copy(
                                    out=QT[64:67, half * 4:half * 4 + 4, i * 128:(i + 1) * 128],
                                    in_=pa[64:67, :].rearrange("p (h t) -> p h t", h=4)), [b_pa], [b_QT])

                    loadG(0)
                    frontG(0)
                    frontG2(0)
                    if GT > 1:
                        loadG(1)
                    transG(0)
                    for g in range(GT):
                        if g + 1 < GT:
                            frontG(g + 1)
                        if g < NGA:
                            backA(g)
                        else:
                            backB(g - NGA)
                        if g + 1 < GT:
                            frontG2(g + 1)
                        if g + 2 < GT:
                            loadG(g + 2)
                        if g < NGA:
                            backA2(g)
                        if g + 1 < GT:
                            transG(g + 1)
                        if g < NGA:
                            backA3(g)
                    if dbg:
                        T.dma("sp", lambda e: e.dma_start(out=dbgo["d_cum"], in_=cumsp[:].rearrange("p t h -> p (t h)")),
                              [b_cumsp], [], "dbg")

                T.barrier()
                psum_free()
                with ExitStack() as stC:
                    pS = [(stC.enter_context(nc.psum_tensor(f"pS{i}", [128, 512], F32)), Buf(f"pS{i}")) for i in range(4)]
                    pO = [(stC.enter_context(nc.psum_tensor(f"pO{i}", [128, 512], F32)), Buf(f"pO{i}")) for i in range(4)]
                    KT = sbr(stC, "KT", [67, S], BF16, 2)
                    PT = sbr(stC, "PT", [128, 512], BF16, 4)
                    rec = sbr(stC, "rec", [128, 4], F32, 4)
                    for kt_, b_ in KT:
                        T.op("pool", lambda e, kt_=kt_: e.memset(kt_[64:67, :], 1.0), [], [b_])
                    for k_ in ("wu", "wv2", "wg", "wbs", "wba", "wout", "wup", "wdn"):
                        precast(k_)

                    def loadK(h):
                        kt_, b_ = KT[h % 2]
                        T.dma("sp", lambda e: e.dma_start(out=kt_[0:64, :], in_=kscr[h * 64:(h + 1) * 64, :]),
                              [b_kscr], [b_], f"kt{h % 2}")

                    gsets = [[a] for a in range(NGO)]
                    items = []
                    for h in range(8):
                        for si, gs in enumerate(gsets):
                            for kt in range(16 * gs[-1] + 16):
                                items.append((h, si, kt))

                    def colstart(go, kt):
                        if kt < 16 * go:
                            return 0, None
                        l = kt // 4 - 4 * go
                        return l * 128, kt % 4

                    def active(si, kt):
                        r = []
                        for slot, go in enumerate(gsets[si]):
                            if kt <= 16 * go + 15:
                                c0, m = colstart(go, kt)
                                r.append((slot, go, c0, m))
                        return r

                    def emitS(n):
                        h, si, kt = items[n]
                        kt_, b_k = KT[h % 2]
                        ps_, b_ps = pS[n % 4]
                        for slot, go, c0, m in active(si, kt):
                            T.op("pe", lambda e, slot=slot, go=go, c0=c0: e.matmul(
                                ps_[:, slot * 512 + c0:(slot + 1) * 512], lhsT=kt_[0:67, kt * 128:(kt + 1) * 128],
                                rhs=QT[0:67, h, go * 512 + c0:(go + 1) * 512], start=True, stop=True),
                                [b_k, b_QT], [b_ps])

                    def emitE(n):
                        h, si, kt = items[n]
                        ps_, b_ps = pS[n % 4]
                        pt_, b_pt = PT[n % 4]
                        act = active(si, kt)
                        for slot, go, c0, m in act:
                            if m is not None:
                                lo = slot * 512 + c0
                                T.op("dve", lambda e, lo=lo, m=m: e.tensor_tensor(
                                    out=ps_[:, lo:lo + 128], in0=ps_[:, lo:lo + 128], in1=madd[:, m, :], op=ALU.add),
                                    [b_ps, b_madd], [b_ps])
                        lo = act[0][0] * 512 + act[0][2]
                        hi = (act[-1][0] + 1) * 512
                        T.op("act", lambda e: e.activation(
                            out=pt_[:, lo:hi], in_=ps_[:, lo:hi], func=AF.Exp, scale=0.125,
                            bias=cumsp[:, kt, h:h + 1]), [b_ps, b_cumsp], [b_pt])

                    def emitPV(n):
                        h, si, kt = items[n]
                        pt_, b_pt = PT[n % 4]
                        ring = 2 * ((h * len(gsets) + si) % 2)
                        for slot, go, c0, m in active(si, kt):
                            po, b_po = pO[ring + slot]
                            l0 = c0 // 128
                            for l in range(l0, 4):
                                last = 4 * (4 * go + l) + 3
                                T.op("pe", lambda e, l=l, last=last, slot=slot, po=po: e.matmul(
                                    po[:, l * 65:(l + 1) * 65], lhsT=pt_[:, slot * 512 + l * 128:slot * 512 + (l + 1) * 128],
                                    rhs=Vall[:, kt, h, :], start=(kt == 0 and l == 0), stop=(kt == last),
                                    skip_group_check=True),
                                    [b_pt, b_Vall], [b_po], signal=(l == 3))
                            if kt == 16 * go + 15:
                                rc, b_rc = rec[ring + slot]
                                pov = po[:, 0:260].rearrange("p (l d) -> p l d", l=4)
                                T.op("dve", lambda e, rc=rc, pov=pov: e.reciprocal(out=rc[:], in_=pov[:, :, 64]), [b_po], [b_rc])
                                for l in range(4):
                                    T.op("dve", lambda e, l=l, rc=rc, pov=pov, go=go: e.tensor_scalar(
                                        out=yatt[:, 4 * go + l, h * 64:(h + 1) * 64], in0=pov[:, l, 0:64],
                                        scalar1=rc[:, l:l + 1], scalar2=None, op0=ALU.mult), [b_po, b_rc], [b_yatt])

                    loadK(0)
                    NI = len(items)
                    per_h = NI // 8
                    for n in range(NI + 3):
                        if n < NI:
                            if n % per_h == 0:
                                hh = n // per_h
                                if hh + 1 < 8:
                                    loadK(hh + 1)
                            emitS(n)
                        if 2 <= n < NI + 2:
                            emitE(n - 2)
                        if n >= 3:
                            emitPV(n - 3)
                    if dbg:
                        T.dma("sp", lambda e: e.dma_start(out=dbgo["d_yatt"], in_=yatt[:].rearrange("p t c -> p (t c)")),
                              [b_yatt], [], "dbg")
                T.barrier()
                psum_std()

            T.barrier()
            with ExitStack() as stD:
                Wu = sb(stD, "Wu", [128, 8, 512], BF16); b_Wu = Buf()
                Wv2 = sb(stD, "Wv2", [128, 8, 512], BF16); b_Wv2 = Buf()
                Wg = sb(stD, "Wg", [128, 8, 2048], BF16); b_Wg = Buf()
                Wbs = sb(stD, "Wbs", [128, 4, D], BF16); b_Wbs = Buf()
                Wba = sb(stD, "Wba", [128, 4, D], BF16); b_Wba = Buf()
                Wout = sb(stD, "Wout", [128, 8, D], BF16); b_Wout = Buf()
                wload2(Wu[:], "wu", b_Wu)
                wload2(Wv2[:], "wv2", b_Wv2)
                gpost = sb(stD, "gpost", [128, D], F32); b_gpost = Buf()
                gsg = sb(stD, "gsg", [128, 512], F32); b_gsg = Buf()
                bsg = sb(stD, "bsg", [128, 512], F32); b_bsg = Buf()
                bsp = sb(stD, "bsp", [128, 4, 128], F32); b_bsp = Buf()
                wsTm = sb(stD, "wsTm", [128, 8, 128], BF16); b_wsTm = Buf()
                T.dma("sp", lambda e: e.dma_start(out=gpost[:], in_=g_post.partition_broadcast(128)), [], [b_gpost], "c2")
                T.dma("sp", lambda e: e.dma_start(out=gsg[:], in_=g_sgu.partition_broadcast(128)), [], [b_gsg], "c2")
                T.dma("sp", lambda e: e.dma_start(out=bsg[:], in_=b_sgu.partition_broadcast(128)), [], [b_bsg], "c2")
                T.dma("sp", lambda e: e.dma_start(out=bsp[:], in_=bsp4.rearrange("p (k t) -> p k t", k=4)[:, :, 0:128]), [], [b_bsp], "c2")
                h1t = sbr(stD, "h1t", [128, D], F32, 2)
                T.dma("sp", lambda e: e.dma_start(out=h1t[0][0][:], in_=wsT), [], [h1t[0][1]], "c2")
                T.op("dve", lambda e: e.tensor_tensor(out=wsTm[:], in0=h1t[0][0][:].rearrange("p (g t) -> p g t", g=8),
                                                      in1=triU[:].unsqueeze(1).broadcast_to([128, 8, 128]), op=ALU.mult),
                     [h1t[0][1], b_triU], [b_wsTm])

                xg = sbr(stD, "xg", [128, D], F32, 2)
                xsd = sbr(stD, "xsd", [128, D], BF16, 2)
                xsTg = sbr(stD, "xsTg", [128, 8, 512], BF16, 2)
                uT = sbr(stD, "uT", [128, 4, 512], BF16, 1)
                vg = sbr(stD, "vg", [128, 512], F32, 4)
                vnb = sbr(stD, "vnb", [128, 512], BF16, 4)
                lnst = sbr(stD, "lnst", [128, 16], F32, 2)
                t1 = sbr(stD, "t1", [128, 512], F32, 1)
                ysT = sbr(stD, "ysT", [128, 4, 512], BF16, 2)
                yaT = sbr(stD, "yaT", [128, 4, 512], BF16, 2)
                tg1 = sbr(stD, "tg1", [128, 512], F32, 2)
                tg2 = sbr(stD, "tg2", [128, 512], F32, 2)
                mT = sbr(stD, "mT", [128, 8, 512], BF16, 1)
                bank_n = [0]

                def bank():
                    r = pf[bank_n[0] % 6]
                    bank_n[0] += 1
                    return r

                def rstd_batch(s, b_s, n, src0, dst0, scale, extra_bias=0.0):
                    T.op("act", lambda e: e.activation(out=s[:, dst0:dst0 + n], in_=s[:, src0:src0 + n], func=AF.Ln,
                                                       scale=scale, bias=EPS), [b_s], [b_s])
                    T.op("act", lambda e: e.activation(out=s[:, dst0:dst0 + n], in_=s[:, dst0:dst0 + n], func=AF.Exp,
                                                       scale=-0.5, bias=extra_bias), [b_s], [b_s])

                def genX(go):
                    dT, b_dT = xsTg[go % 2]
                    for pr in range(2):
                        s, b_s = stat[fr_n[0] % 4]
                        fr_n[0] += 1
                        T.op("pool", lambda e: e.memset(s[:, 0:2], 0.0), [], [b_s])
                        for q in range(2):
                            l = 2 * pr + q
                            i = 4 * go + l
                            xt, b_x = xg[q]
                            T.dma("sp", lambda e, xt=xt, i=i: e.dma_start(out=xt[:], in_=xo[i * 128:(i + 1) * 128, :]),
                                  [], [b_x], f"xg{q}")
                            T.op("act", lambda e, xt=xt, q=q: e.activation(out=junk[:], in_=xt[:], func=AF.Square,
                                                                           accum_out=s[:, q:q + 1]), [b_x], [b_junk, b_s])
                        rstd_batch(s, b_s, 2, 0, 2, 1.0 / D)
                        for q in range(2):
                            l = 2 * pr + q
                            xt, b_x = xg[q]
                            xst, b_xs = xsd[q]
                            T.op("dve", lambda e, xt=xt, xst=xst, q=q: e.scalar_tensor_tensor(
                                out=xst[:], in0=xt[:], scalar=s[:, 2 + q:3 + q], in1=gbc[:], op0=ALU.mult, op1=ALU.mult),
                                [b_x, b_s, b_gbc], [b_xs])
                            transpose8(xst, b_xs, dT[:, :, l * 128:(l + 1) * 128], b_dT, q, "act" if q == 0 else "dve")
                        yield
                    ut, b_ut = uT[0]
                    for c in range(4):
                        pu, b_pu = bank()
                        for kc in range(8):
                            T.op("pe", lambda e, kc=kc, c=c, pu=pu: e.matmul(
                                pu[:], lhsT=Wu[:, kc, c * 128:(c + 1) * 128], rhs=dT[:, kc, :],
                                start=(kc == 0), stop=(kc == 7)), [b_Wu, b_dT], [b_pu], signal=(kc == 7))
                        T.op("act", lambda e, c=c, pu=pu: e.activation(out=ut[:, c, :], in_=pu[:], func=AF.Gelu_apprx_tanh),
                             [b_pu], [b_ut])
                        yield
                    ls, b_ls = lnst[go % 2]
                    T.op("pool", lambda e: e.memset(ls[:], 0.0), [], [b_ls])
                    for l in range(4):
                        pv, b_pv = bank()
                        for kc in range(8):
                            T.op("pe", lambda e, kc=kc, l=l, pv=pv: e.matmul(
                                pv[:], lhsT=dT[:, kc, l * 128:(l + 1) * 128], rhs=Wv2[:, kc, :],
                                start=(kc == 0), stop=(kc == 7)), [b_Wv2, b_dT], [b_pv], signal=(kc == 7))
                        v_, b_v = vg[l]
                        T.op("act", lambda e, l=l, v_=v_, pv=pv: e.activation(
                            out=v_[:], in_=pv[:], func=AF.Gelu_apprx_tanh, accum_out=ls[:, l:l + 1]), [b_pv], [b_v, b_ls])
                        T.op("act", lambda e, l=l, v_=v_: e.activation(out=junk[:, 0:512], in_=v_[:], func=AF.Square,
                                                                       accum_out=ls[:, 4 + l:5 + l]), [b_v], [b_junk, b_ls])
                        yield
                    T.op("dve", lambda e: e.tensor_scalar(out=ls[:, 0:4], in0=ls[:, 0:4], scalar1=1.0 / 512, scalar2=None,
                                                          op0=ALU.mult), [b_ls], [b_ls])
                    T.op("dve", lambda e: e.tensor_tensor(out=ls[:, 8:12], in0=ls[:, 0:4], in1=ls[:, 0:4], op=ALU.mult),
                         [b_ls], [b_ls])
                    T.op("dve", lambda e: e.scalar_tensor_tensor(out=ls[:, 4:8], in0=ls[:, 4:8], scalar=1.0 / 512,
                                                                 in1=ls[:, 8:12], op0=ALU.mult, op1=ALU.subtract),
                         [b_ls], [b_ls])
                    rstd_batch(ls, b_ls, 4, 4, 12, 1.0)
                    for l in range(4):
                        v_, b_v = vg[l]
                        vb, b_vb = vnb[l]
                        T.op("dve", lambda e, l=l, v_=v_: e.tensor_scalar(
                            out=v_[:], in0=v_[:], scalar1=ls[:, l:l + 1], scalar2=ls[:, 12 + l:13 + l],
                            op0=ALU.subtract, op1=ALU.mult), [b_v, b_ls], [b_v])
                        T.op("dve", lambda e, v_=v_: e.tensor_tensor(out=v_[:], in0=v_[:], in1=gsg[:], op=ALU.mult),
                             [b_v, b_gsg], [b_v])
                        T.op("pool", lambda e, v_=v_, vb=vb: e.tensor_tensor(out=vb[:], in0=v_[:], in1=bsg[:], op=ALU.add),
                             [b_v, b_bsg], [b_vb])
                    yield
                    ys, b_ys = ysT[go % 2]
                    for k in range(4):
                        pA, b_pA = bank()
                        pB, b_pB = bank()
                        for l in range(4):
                            vb, b_vb = vnb[l]
                            T.op("pe", lambda e, l=l, k=k, vb=vb, pA=pA: e.matmul(
                                pA[:, l * 128:(l + 1) * 128], lhsT=vb[:, k * 128:(k + 1) * 128], rhs=wsTm[:, 2 * k, :],
                                start=True, stop=True), [b_vb, b_wsTm], [b_pA], signal=(l == 3))
                        for l in range(4):
                            vb, b_vb = vnb[l]
                            T.op("pe", lambda e, l=l, k=k, vb=vb, pB=pB: e.matmul(
                                pB[:, l * 128:(l + 1) * 128], lhsT=vb[:, k * 128:(k + 1) * 128], rhs=wsTm[:, 2 * k + 1, :],
                                start=True, stop=True), [b_vb, b_wsTm], [b_pB], signal=(l == 3))
                        tt, b_tt = t1[0]
                        T.op("dve", lambda e, k=k, tt=tt, pA=pA: e.tensor_tensor(
                            out=tt[0:64, :].rearrange("p (l t) -> p l t", l=4), in0=pA[0:64, :].rearrange("p (l t) -> p l t", l=4),
                            in1=bsp[0:64, k, :].unsqueeze(1).broadcast_to([64, 4, 128]), op=ALU.add), [b_pA, b_bsp], [b_tt])
                        T.op("dve", lambda e, k=k, tt=tt, pB=pB: e.tensor_tensor(
                            out=tt[64:128, :].rearrange("p (l t) -> p l t", l=4), in0=pB[64:128, :].rearrange("p (l t) -> p l t", l=4),
                            in1=bsp[64:128, k, :].unsqueeze(1).broadcast_to([64, 4, 128]), op=ALU.add), [b_pB, b_bsp], [b_tt])
                        T.op("pool", lambda e, k=k, tt=tt: e.tensor_tensor(
                            out=ys[:, k, :], in0=tt[:], in1=ut[:, k, :], op=ALU.mult), [b_tt, b_ut], [b_ys])
                        yield
                    ya, b_ya = yaT[go % 2]
                    for l in range(4):
                        i = 4 * go + l
                        transpose8(yatt[:, i, :], b_yatt, ya[:, :, l * 128:(l + 1) * 128], b_ya, l % 2,
                                   "act" if l % 2 == 0 else "dve", n=4)
                        yield

                def genY(go):
                    dT, b_dT = xsTg[go % 2]
                    ys, b_ys = ysT[go % 2]
                    ya, b_ya = yaT[go % 2]
                    mt, b_mt = mT[0]
                    for oc in range(8):
                        pG1, b_pG1 = bank()
                        pG2, b_pG2 = bank()
                        pP1, b_pP1 = bank()
                        pP2, b_pP2 = bank()
                        for kc in range(8):
                            T.op("pe", lambda e, kc=kc, oc=oc, pG1=pG1: e.matmul(
                                pG1[:], lhsT=Wg[:, kc, oc * 128:(oc + 1) * 128], rhs=dT[:, kc, :],
                                start=(kc == 0), stop=(kc == 7)), [b_Wg, b_dT], [b_pG1], signal=(kc == 7))
                        for kc in range(8):
                            T.op("pe", lambda e, kc=kc, oc=oc, pG2=pG2: e.matmul(
                                pG2[:], lhsT=Wg[:, kc, 1024 + oc * 128:1024 + (oc + 1) * 128], rhs=dT[:, kc, :],
                                start=(kc == 0), stop=(kc == 7)), [b_Wg, b_dT], [b_pG2], signal=(kc == 7))
                        for c in range(4):
                            T.op("pe", lambda e, c=c, oc=oc, pP1=pP1: e.matmul(
                                pP1[:], lhsT=Wbs[:, c, oc * 128:(oc + 1) * 128], rhs=ys[:, c, :],
                                start=(c == 0), stop=(c == 3)), [b_Wbs, b_ys], [b_pP1], signal=(c == 3))
                        for c in range(4):
                            T.op("pe", lambda e, c=c, oc=oc, pP2=pP2: e.matmul(
                                pP2[:], lhsT=Wba[:, c, oc * 128:(oc + 1) * 128], rhs=ya[:, c, :],
                                start=(c == 0), stop=(c == 3)), [b_Wba, b_ya], [b_pP2], signal=(c == 3))
                        a1, b_a1 = tg1[oc % 2]
                        a2, b_a2 = tg2[oc % 2]
                        T.op("act", lambda e, a1=a1, pG1=pG1: e.activation(out=a1[:], in_=pG1[:], func=AF.Tanh, scale=0.5),
                             [b_pG1], [b_a1])
                        T.op("act", lambda e, a2=a2, pG2=pG2: e.activation(out=a2[:], in_=pG2[:], func=AF.Tanh, scale=0.5),
                             [b_pG2], [b_a2])
                        T.op("dve", lambda e, a1=a1, pP1=pP1: e.scalar_tensor_tensor(
                            out=a1[:], in0=a1[:], scalar=1.0, in1=pP1[:], op0=ALU.add, op1=ALU.mult),
                            [b_a1, b_pP1], [b_a1])
                        T.op("dve", lambda e, a2=a2, pP2=pP2: e.scalar_tensor_tensor(
                            out=a2[:], in0=a2[:], scalar=1.0, in1=pP2[:], op0=ALU.add, op1=ALU.mult),
                            [b_a2, b_pP2], [b_a2])
                        T.op("pool", lambda e, a1=a1, a2=a2, oc=oc: e.tensor_tensor(
                            out=mt[:, oc, :], in0=a1[:], in1=a2[:], op=ALU.add), [b_a1, b_a2], [b_mt])
                        yield
                    for pr in range(2):
                        s, b_s = stat[fr_n[0] % 4]
                        fr_n[0] += 1
                        T.op("pool", lambda e, s=s: e.memset(s[:, 0:4], 0.0), [], [b_s])
                        banks = []
                        for q in range(2):
                            l = 2 * pr + q
                            for half in range(2):
                                pH, b_pH = bank()
                                banks.append((pH, b_pH))
                                for oc in range(8):
                                    T.op("pe", lambda e, oc=oc, l=l, half=half, pH=pH: e.matmul(
                                        pH[:], lhsT=mt[:, oc, l * 128:(l + 1) * 128],
                                        rhs=Wout[:, oc, half * 512:(half + 1) * 512],
                                        start=(oc == 0), stop=(oc == 7)), [b_mt, b_Wout], [b_pH], signal=(oc == 7))
                                T.op("act", lambda e, s=s, pH=pH, q=q, half=half: e.activation(
                                    out=junk[:, half * 512:(half + 1) * 512], in_=pH[:], func=AF.Square, scale=0.5,
                                    accum_out=s[:, 2 * q + half:2 * q + half + 1]), [b_pH], [b_junk, b_s])
                        T.op("dve", lambda e, s=s: e.tensor_tensor(out=s[:, 4:5], in0=s[:, 0:1], in1=s[:, 1:2], op=ALU.add),
                             [b_s], [b_s])
                        T.op("dve", lambda e, s=s: e.tensor_tensor(out=s[:, 5:6], in0=s[:, 2:3], in1=s[:, 3:4], op=ALU.add),
                             [b_s], [b_s])
                        rstd_batch(s, b_s, 2, 4, 6, 1.0 / D, extra_bias=math.log(0.5))
                        for q in range(2):
                            l = 2 * pr + q
                            i = 4 * go + l
                            ht, b_ht = h1t[q]
                            for half in range(2):
                                pH, b_pH = banks[2 * q + half]
                                T.op("dve", lambda e, s=s, ht=ht, pH=pH, q=q, half=half: e.scalar_tensor_tensor(
                                    out=ht[:, half * 512:(half + 1) * 512], in0=pH[:], scalar=s[:, 6 + q:7 + q],
                                    in1=gpost[:, half * 512:(half + 1) * 512], op0=ALU.mult, op1=ALU.mult),
                                    [b_pH, b_s, b_gpost], [b_ht])
                            T.dma("sp", lambda e, ht=ht, i=i: e.dma_start(out=h1scr[i * 128:(i + 1) * 128, :], in_=ht[:]),
                                  [b_ht], [b_h1scr], f"h1s{q}")
                            if dbg:
                                T.dma("sp", lambda e, ht=ht, i=i: e.dma_start(out=dbgo["d_h1"][i * 128:(i + 1) * 128, :], in_=ht[:]),
                                      [b_ht], [], "dbg")
                        yield

                def run_interleaved(ga, gb, nb_per_a):
                    da = db = False
                    while not (da and db):
                        if not da:
                            try:
                                next(ga)
                            except StopIteration:
                                da = True
                        for _ in range(nb_per_a):
                            if db:
                                break
                            try:
                                next(gb)
                            except StopIteration:
                                db = True
                        if da and not db:
                            for _ in gb:
                                pass
                            db = True

                gx0 = genX(0)
                next(gx0)
                next(gx0)
                wload2(Wg[:], "wg", b_Wg, nsplit=4)
                wload2(Wbs[:], "wbs", b_Wbs)
                wload2(Wba[:], "wba", b_Wba)
                wload2(Wout[:], "wout", b_Wout)
                for _ in gx0:
                    pass
                for go in range(NGO):
                    gy = genY(go)
                    gx = genX(go + 1) if go + 1 < NGO else iter(())
                    run_interleaved(gy, gx, 2)

        T.barrier()
        with ExitStack() as stE:
            Wup = sb(stE, "Wup", [128, 8, 8, 512], BF16)
            Wdn = sb(stE, "Wdn", [128, 32, D], BF16)
            b_Wup = [Buf() for _ in range(8)]
            b_Wdn = [Buf() for _ in range(8)]
            gpost2 = sb(stE, "gpost2", [128, D], F32); b_gpost2 = Buf()
            T.dma("sp", lambda e: e.dma_start(out=gpost2[:], in_=g_fpost.partition_broadcast(128)), [], [b_gpost2], "c3")
            T.dma("sp", lambda e: e.dma_start(out=gbc[:], in_=g_fpre.partition_broadcast(128)), [], [b_gbc], "c3")
            hg = sbr(stE, "hg", [128, D], F32, 4)
            xe = sbr(stE, "xe", [128, D], F32, 2)
            xs2 = sbr(stE, "xs2", [128, D], BF16, 2)
            x2T = sbr(stE, "x2T", [128, 8, 256], BF16, 2)
            rt = sbr(stE, "rt", [128, 256], F32, 3)
            hT = sbr(stE, "hT", [128, 32, 256], BF16, 1)
            ot = sbr(stE, "ot", [128, D], F32, 1)
            NSG = NO // 2
            bank_n = [0]

            def bank():
                r = pf[bank_n[0] % 6]
                bank_n[0] += 1
                return r

            def frontE(sg):
                dT, b_dT = x2T[sg % 2]
                s, b_s = stat[fr_n[0] % 4]
                fr_n[0] += 1
                T.op("pool", lambda e: e.memset(s[:, 0:2], 0.0), [], [b_s])
                for l in range(2):
                    i = 2 * sg + l
                    ht, b_h = hg[i % 4]
                    xt, b_x = xe[l]
                    T.dma("sp", lambda e, ht=ht, i=i: e.dma_start(out=ht[:], in_=h1scr[i * 128:(i + 1) * 128, :]),
                          [b_h1scr], [b_h], f"hg{i % 4}")
                    T.dma("sp", lambda e, xt=xt, i=i: e.dma_start(out=xt[:], in_=xo[i * 128:(i + 1) * 128, :]),
                          [], [b_x], f"xe{l}")
                    T.op("pool", lambda e, ht=ht, xt=xt: e.tensor_tensor(out=ht[:], in0=ht[:], in1=xt[:], op=ALU.add),
                         [b_h, b_x], [b_h])
                    T.op("act", lambda e, ht=ht, l=l: e.activation(out=junk[:], in_=ht[:], func=AF.Square,
                                                                   accum_out=s[:, l:l + 1]), [b_h], [b_junk, b_s])
                T.op("act", lambda e: e.activation(out=s[:, 2:4], in_=s[:, 0:2], func=AF.Ln, scale=1.0 / D, bias=EPS),
                     [b_s], [b_s])
                T.op("act", lambda e: e.activation(out=s[:, 2:4], in_=s[:, 2:4], func=AF.Exp, scale=-0.5), [b_s], [b_s])
                for l in range(2):
                    i = 2 * sg + l
                    ht, b_h = hg[i % 4]
                    xst, b_xs = xs2[l]
                    T.op("dve", lambda e, ht=ht, xst=xst, l=l: e.scalar_tensor_tensor(
                        out=xst[:], in0=ht[:], scalar=s[:, 2 + l:3 + l], in1=gbc[:], op0=ALU.mult, op1=ALU.mult),
                        [b_h, b_s, b_gbc], [b_xs])

            def frontET(sg):
                dT, b_dT = x2T[sg % 2]
                for l in range(2):
                    xst, b_xs = xs2[l]
                    transpose8(xst, b_xs, dT[:, :, l * 128:(l + 1) * 128], b_dT, l, "act" if l == 0 else "dve")

            def upE(sg):
                dT, b_dT = x2T[sg % 2]
                h_, b_hT = hT[0]
                for j in range(32):
                    pu, b_pu = bank()
                    for kc in range(8):
                        T.op("pe", lambda e, kc=kc, j=j, pu=pu: e.matmul(
                            pu[:, 0:256], lhsT=Wup[:, j // 4, kc, (j % 4) * 128:(j % 4 + 1) * 128], rhs=dT[:, kc, :],
                            start=(kc == 0), stop=(kc == 7)), [b_Wup[j // 4], b_dT], [b_pu], signal=(kc == 7))
                    r_, b_r = rt[j % 3]
                    T.op("act", lambda e, r_=r_, pu=pu: e.activation(out=r_[:], in_=pu[:, 0:256], func=AF.Relu),
                         [b_pu], [b_r])
                    T.op("dve" if j % 2 == 0 else "pool", lambda e, r_=r_, j=j: e.tensor_tensor(
                        out=h_[:, j, :], in0=r_[:], in1=r_[:], op=ALU.mult), [b_r], [b_hT])

            def downE(sg):
                h_, b_hT = hT[0]
                for l in range(2):
                    i = 2 * sg + l
                    pD0, b_pD0 = bank()
                    pD1, b_pD1 = bank()
                    for half, (pD, b_pD) in enumerate(((pD0, b_pD0), (pD1, b_pD1))):
                        for j in range(32):
                            T.op("pe", lambda e, j=j, l=l, half=half, pD=pD: e.matmul(
                                pD[:], lhsT=h_[:, j, l * 128:(l + 1) * 128], rhs=Wdn[:, j, half * 512:(half + 1) * 512],
                                start=(j == 0), stop=(j == 31)), [b_hT, b_Wdn[j // 4]], [b_pD], signal=(j == 31))
                    s, b_s = stat[fr_n[0] % 4]
                    fr_n[0] += 1
                    T.op("pool", lambda e, s=s: e.memset(s[:, 0:2], 0.0), [], [b_s])
                    T.op("act", lambda e, s=s, pD0=pD0: e.activation(out=junk[:, 0:512], in_=pD0[:], func=AF.Square,
                                                                     accum_out=s[:, 0:1]), [b_pD0], [b_junk, b_s])
                    T.op("act", lambda e, s=s, pD1=pD1: e.activation(out=junk[:, 512:1024], in_=pD1[:], func=AF.Square,
                                                                     accum_out=s[:, 1:2]), [b_pD1], [b_junk, b_s])
                    T.op("dve", lambda e, s=s: e.tensor_tensor(out=s[:, 2:3], in0=s[:, 0:1], in1=s[:, 1:2], op=ALU.add),
                         [b_s], [b_s])
                    T.op("act", lambda e, s=s: e.activation(out=s[:, 3:4], in_=s[:, 2:3], func=AF.Ln, scale=1.0 / D, bias=EPS),
                         [b_s], [b_s])
                    T.op("act", lambda e, s=s: e.activation(out=s[:, 4:5], in_=s[:, 3:4], func=AF.Exp, scale=-0.5), [b_s], [b_s])
                    o_, b_o = ot[0]
                    ht, b_h = hg[i % 4]
                    T.op("dve", lambda e, s=s, o_=o_, pD0=pD0: e.scalar_tensor_tensor(
                        out=o_[:, 0:512], in0=pD0[:], scalar=s[:, 4:5], in1=gpost2[:, 0:512], op0=ALU.mult, op1=ALU.mult),
                        [b_pD0, b_s, b_gpost2], [b_o])
                    T.op("dve", lambda e, s=s, o_=o_, pD1=pD1: e.scalar_tensor_tensor(
                        out=o_[:, 512:1024], in0=pD1[:], scalar=s[:, 4:5], in1=gpost2[:, 512:1024], op0=ALU.mult, op1=ALU.mult),
                        [b_pD1, b_s, b_gpost2], [b_o])
                    T.op("dve", lambda e, o_=o_, ht=ht: e.tensor_tensor(out=o_[:], in0=o_[:], in1=ht[:], op=ALU.add),
                         [b_o, b_h], [b_o])
                    T.dma("sp", lambda e, o_=o_, i=i: e.dma_start(out=out[i * 128:(i + 1) * 128, :], in_=o_[:]),
                          [b_o], [], f"ost{i % 2}")

            frontE(0)
            wupv = wscr["wup"].rearrange("p (pc kc n) -> p pc kc n", pc=8, kc=8)
            wdnv = wscr["wdn"].rearrange("p (kc n) -> p kc n", kc=32)
            for p in range(8):
                T.dma("sp" if p % 2 == 0 else "act", lambda e, p=p: e.dma_start(out=Wup[:, p, :, :], in_=wupv[:, p, :, :]),
                      [b_wscr["wup"]], [b_Wup[p]], f"w_up{p}")
            for p in range(8):
                T.dma("sp" if p % 2 == 0 else "act", lambda e, p=p: e.dma_start(
                    out=Wdn[:, p * 4:(p + 1) * 4, :], in_=wdnv[:, p * 4:(p + 1) * 4, :]),
                    [b_wscr["wdn"]], [b_Wdn[p]], f"w_dn{p}")
            frontET(0)
            for sg in range(NSG):
                if sg + 1 < NSG:
                    frontE(sg + 1)
                upE(sg)
                if sg + 1 < NSG:
                    frontET(sg + 1)
                downE(sg)
            for k in ("ost0", "ost1", "dbg"):
                if k in T.semobj:
                    nc.sync.wait_ge(T.semobj[k], T.cnt[k])
        psum_free()
        print(f"[build] ops={T.nops} waits={T.nwaits} sems={len(T.semobj)}")
    return nc


def _host_inputs(inputs, NT):
    f = lambda a: np.ascontiguousarray(np.asarray(a, dtype=np.float32))
    x = f(inputs["x"])
    w_in = f(inputs["w_in"])[0]
    wz, wq_, wk_, wv_, wf_, wg_ = np.split(w_in, [1024, 1536, 2048, 2560, 2568], axis=1)
    common = {
        "wkvf": f(np.concatenate([wk_, wv_, wf_], axis=1)),
        "wq": f(wq_),
        "wu": f(wz[:, :512]),
        "wv2": f(wz[:, 512:]),
        "wg": f(wg_),
        "wbs": f(inputs["w_branch_sgu"])[0],
        "wba": f(inputs["w_branch_attn"])[0],
        "wout": f(inputs["w_out"])[0],
        "wup": f(inputs["w_up"])[0],
        "wdn": f(inputs["w_down"])[0],
        "g_pre": f(inputs["g_mix_pre"]),
        "g_post": f(inputs["g_mix_post"]),
        "g_fpre": f(inputs["g_ffn_pre"]),
        "g_fpost": f(inputs["g_ffn_post"]),
        "g_sgu": f(inputs["g_sgu"]),
        "b_sgu": f(inputs["b_sgu"]),
        "b_fg": f(inputs["b_forget"]),
        "wsT": f(np.transpose(f(inputs["w_spatial"])[0], (2, 0, 1)).reshape(128, 8 * 128)),
        "c_id": np.eye(128, dtype=np.float32),
        "c_tri": np.triu(np.ones((128, 128), np.float32)),
        "c_one": np.ones((128, 128), np.float32),
    }
    bs = f(inputs["b_spatial"])[0]
    bsp = np.repeat(bs.reshape(4, 2, 1, 128), 64, axis=2).reshape(4, 128, 128)
    bsp4 = np.tile(bsp[:, :, None, :], (1, 1, 4, 1)).reshape(4, 128, 512)
    common["bsp4"] = f(np.transpose(bsp4, (1, 0, 2)).reshape(128, 4 * 512))
    maps = []
    s_idx = np.arange(128)[:, None]
    t_idx = np.arange(128)[None, :]
    for c in range(8):
        b, j = c // 4, c % 4
        xb = x[b]
        xo = xb.reshape(NT, 128, D)[j::4].reshape(-1, D)
        sel = np.zeros((128, 4), np.float32)
        sel[:, j] = 1.0
        madd = np.zeros((128, 4, 128), np.float32)
        for m in range(4):
            if m == j:
                madd[:, m, :] = np.where(s_idx <= t_idx, 0.0, NEG)
            elif m > j:
                madd[:, m, :] = NEG
        d = dict(common)
        d["xf"] = f(xb)
        d["xo"] = f(xo)
        d["c_sel"] = sel
        d["c_madd"] = madd.reshape(128, 512)
        maps.append(d)
    return maps


_NC_CACHE = {}


def kernel(**inputs):
    x = np.asarray(inputs["x"])
    B, S, _ = x.shape
    NT = S // 128
    if NT not in _NC_CACHE:
        _NC_CACHE[NT] = build_nc(NT)
    nc = _NC_CACHE[NT]
    maps = _host_inputs(inputs, NT)
    res = run_bass_kernel_spmd(nc, maps, core_ids=list(range(8)))
    out = np.zeros((B, NT, 128, D), np.float32)
    for c in range(8):
        b, j = c // 4, c % 4
        out[b, j::4] = np.asarray(res.results[c]["out"], dtype=np.float32).reshape(NT // 4, 128, D)
    return out.reshape(B, S, D)
```

```python
from contextlib import ExitStack
import math
import numpy as np
import concourse.bass as bass
import concourse.mybir as mybir
from concourse.bass_utils import run_bass_kernel_spmd

F32 = mybir.dt.float32
BF16 = mybir.dt.bfloat16
AF = mybir.ActivationFunctionType
ALU = mybir.AluOpType

D = 1024
NEG = -1.0e5
EPS = 1e-6


class Buf:
    def __init__(self, name=""):
        self.name = name
        self.w = None
        self.r = {}


class Tracker:
    def __init__(self, nc, stack):
        self.nc = nc
        self.stack = stack
        self.engs = {"pe": nc.tensor, "act": nc.scalar, "dve": nc.vector,
                     "pool": nc.gpsimd, "sp": nc.sync}
        self.semobj = {}
        self.cnt = {}
        for k in ["pe", "act", "dve", "pool"]:
            self.semobj[k] = stack.enter_context(nc.semaphore("s_" + k))
            self.cnt[k] = 0
        self.waited = {k: {} for k in self.engs}
        self.shared = {"c0", "c1", "c2", "c3"}
        self.nwaits = 0
        self.nops = 0

    def _wait(self, eng, ev):
        if ev is None:
            return
        key, val = ev
        if key in self.shared:
            val = self.cnt[key]
        if key == eng and eng == "pe":
            return
        if self.waited[eng].get(key, 0) >= val:
            return
        self.engs[eng].wait_ge(self.semobj[key], val)
        self.waited[eng][key] = val
        self.nwaits += 1

    def _deps(self, eng, reads, writes):
        for b in reads:
            self._wait(eng, b.w)
        for b in writes:
            self._wait(eng, b.w)
            for k, v in b.r.items():
                self._wait(eng, (k, v))

    def _record(self, ev, reads, writes):
        k, v = ev
        for b in reads:
            if b.r.get(k, 0) < v:
                b.r[k] = v
        for b in writes:
            b.w = ev
            b.r = {}

    def op(self, eng, fn, reads=(), writes=(), signal=True):
        self._deps(eng, reads, writes)
        ins = fn(self.engs[eng])
        self.nops += 1
        if signal:
            self.cnt[eng] += 1
            ins.then_inc(self.semobj[eng], 1)
            ev = (eng, self.cnt[eng])
        else:
            ev = (eng, self.cnt[eng] + 1)
        self._record(ev, reads, writes)
        return ev

    def barrier(self):
        for eng in self.engs:
            for k in list(self.semobj.keys()):
                if self.cnt[k] > 0:
                    self._wait(eng, (k, self.cnt[k]))

    def dma(self, q, fn, reads, writes, sem):
        if sem not in self.semobj:
            self.semobj[sem] = self.stack.enter_context(self.nc.semaphore("d_" + sem))
            self.cnt[sem] = 0
        self._deps(q, reads, writes)
        ins = fn(self.engs[q])
        self.nops += 1
        self.cnt[sem] += 16
        ins.then_inc(self.semobj[sem], 16)
        ev = (sem, self.cnt[sem])
        self._record(ev, reads, writes)
        return ev


def build_nc(NT, dbg=False):
    NO = NT // 4
    NGA = NT // 4
    NGO = NO // 4
    S = NT * 128
    SO = NO * 128
    nc = bass.Bass("TRN2", target_bir_lowering=False)

    def din(name, shape):
        return nc.dram_tensor(name, shape, F32, kind="ExternalInput").ap()

    xf = din("xf", [S, D])
    xo = din("xo", [SO, D])
    wkvf = din("wkvf", [D, 1032])
    wq = din("wq", [D, 512])
    wu = din("wu", [D, 512])
    wv2 = din("wv2", [D, 512])
    wg = din("wg", [D, 2048])
    wbs = din("wbs", [512, D])
    wba = din("wba", [512, D])
    wout = din("wout", [D, D])
    wup = din("wup", [D, 4096])
    wdn = din("wdn", [4096, D])
    g_pre = din("g_pre", [1, D])
    g_post = din("g_post", [1, D])
    g_fpre = din("g_fpre", [1, D])
    g_fpost = din("g_fpost", [1, D])
    g_sgu = din("g_sgu", [1, 512])
    b_sgu = din("b_sgu", [1, 512])
    b_fg = din("b_fg", [1, 8])
    wsT = din("wsT", [128, 8 * 128])
    bsp4 = din("bsp4", [128, 4 * 512])
    c_id = din("c_id", [128, 128])
    c_tri = din("c_tri", [128, 128])
    c_one = din("c_one", [128, 128])
    c_sel = din("c_sel", [128, 4])
    c_madd = din("c_madd", [128, 4 * 128])
    kscr = nc.dram_tensor("kscr", [512, S], BF16).ap()
    h1scr = nc.dram_tensor("h1scr", [SO, D], F32).ap()
    out = nc.dram_tensor("out", [SO, D], F32, kind="ExternalOutput").ap()
    dbgo = {}
    if dbg:
        dbgo["d_cum"] = nc.dram_tensor("d_cum", [128, NT * 8], F32, kind="ExternalOutput").ap()
        dbgo["d_yatt"] = nc.dram_tensor("d_yatt", [128, NO * 512], BF16, kind="ExternalOutput").ap()
        dbgo["d_h1"] = nc.dram_tensor("d_h1", [SO, D], F32, kind="ExternalOutput").ap()

    with ExitStack() as st0:
        T = Tracker(nc, st0)

        def sb(stk, name, shape, dt):
            return stk.enter_context(nc.sbuf_tensor(name, shape, dt))

        def sbr(stk, name, shape, dt, n):
            return [(sb(stk, f"{name}{i}", shape, dt), Buf(f"{name}{i}")) for i in range(n)]

        def wload(dst, src, buf, sem, kc):
            T.dma("pool", lambda e: e.dma_start(out=dst, in_=src.rearrange("(kc p) n -> p kc n", p=128)),
                  [], [buf], sem)

        wsrc = {"wu": (wu, [D, 512]), "wv2": (wv2, [D, 512]), "wg": (wg, [D, 2048]), "wbs": (wbs, [512, D]),
                "wba": (wba, [512, D]), "wout": (wout, [D, D]), "wup": (wup, [D, 4096]), "wdn": (wdn, [4096, D])}
        wscr = {k: nc.dram_tensor("scr_" + k, [128, shp[0] // 128 * shp[1]], BF16).ap() for k, (_, shp) in wsrc.items()}
        b_wscr = {k: Buf("scr_" + k) for k in wsrc}

        def precast(k):
            src, shp = wsrc[k]
            rows, n = shp
            KC = rows // 128
            if k == "wup":
                dstv = wscr[k].rearrange("p (pc kc n) -> p pc kc n", pc=8, kc=8)
                for pc in range(8):
                    T.dma("pool", lambda e, pc=pc: e.dma_start(
                        out=dstv[:, pc, :, :], in_=src[:, pc * 512:(pc + 1) * 512].rearrange("(kc p) n -> p kc n", p=128)),
                        [], [b_wscr[k]], "pc_" + k)
                return
            dstv = wscr[k].rearrange("p (kc n) -> p kc n", kc=KC)
            step = 4 if KC >= 4 else KC
            for c0 in range(0, KC, step):
                T.dma("pool", lambda e, c0=c0: e.dma_start(
                    out=dstv[:, c0:c0 + step, :],
                    in_=src[c0 * 128:(c0 + step) * 128, :].rearrange("(kc p) n -> p kc n", p=128)),
                    [], [b_wscr[k]], "pc_" + k)

        wl_n = [0]

        def wload2(dst, k, buf, nsplit=2):
            KC = dst.shape[1]
            srcv = wscr[k].rearrange("p (kc n) -> p kc n", kc=KC)
            step = max(1, KC // nsplit)
            for c0 in range(0, KC, step):
                q = "sp" if wl_n[0] % 2 == 0 else "act"
                wl_n[0] += 1
                T.dma(q, lambda e, c0=c0: e.dma_start(out=dst[:, c0:c0 + step, :], in_=srcv[:, c0:c0 + step, :]),
                      [b_wscr[k]], [buf], f"w_{k}")

        b_kscr = Buf("kscr")
        b_h1scr = Buf("h1scr")
        pf = []
        pb = []
        ps_stack = [None]
        ps_gen = [0]

        def psum_std():
            stk = ExitStack()
            ps_stack[0] = stk
            k = ps_gen[0]
            ps_gen[0] += 1
            pf[:] = [(stk.enter_context(nc.psum_tensor(f"pf{k}_{i}", [128, 512], F32)), Buf(f"pf{i}")) for i in range(6)]
            pb[:] = [(stk.enter_context(nc.psum_tensor(f"pb{k}_{i}", [128, 1024], BF16)), Buf(f"pb{i}")) for i in range(2)]

        def psum_free():
            ps_stack[0].close()
            ps_stack[0] = None

        psum_std()

        identb = sb(st0, "identb", [128, 128], BF16); b_identb = Buf()
        triU = sb(st0, "triU", [128, 128], F32); b_triU = Buf()
        onesf = sb(st0, "onesf", [128, 128], F32); b_onesf = Buf()
        gbc = sb(st0, "gbc", [128, D], F32); b_gbc = Buf()
        bfbc = sb(st0, "bfbc", [128, 8], F32); b_bfbc = Buf()
        sel = sb(st0, "sel", [128, 4], F32); b_sel = Buf()
        madd = sb(st0, "madd", [128, 4, 128], F32); b_madd = Buf()
        cumsp = sb(st0, "cumsp", [128, NT, 8], F32); b_cumsp = Buf()
        carry = sb(st0, "carry", [128, 8], F32); b_carry = Buf()
        junk = sb(st0, "junk", [128, D], BF16); b_junk = Buf()
        stat = sbr(st0, "stat", [128, 8], F32, 4)

        T.dma("pool", lambda e: e.dma_start(out=identb[:], in_=c_id), [], [b_identb], "c0")
        T.dma("sp", lambda e: e.dma_start(out=triU[:], in_=c_tri), [], [b_triU], "c1")
        T.dma("sp", lambda e: e.dma_start(out=onesf[:], in_=c_one), [], [b_onesf], "c1")
        T.dma("sp", lambda e: e.dma_start(out=gbc[:], in_=g_pre.partition_broadcast(128)), [], [b_gbc], "c1")
        T.dma("sp", lambda e: e.dma_start(out=bfbc[:], in_=b_fg.partition_broadcast(128)), [], [b_bfbc], "c1")
        T.dma("sp", lambda e: e.dma_start(out=sel[:], in_=c_sel), [], [b_sel], "c1")
        T.dma("sp", lambda e: e.dma_start(out=madd[:], in_=c_madd.rearrange("p (m t) -> p m t", m=4)),
              [], [b_madd], "c1")
        T.op("dve", lambda e: e.memset(carry[:], 0.0), [], [b_carry])

        fr_n = [0]

        def rms_scale(x_ap, b_x, xs_ap, b_xs, eng, pre_scale=1.0, extra_bias=0.0):
            s, b_s = stat[fr_n[0] % 4]
            fr_n[0] += 1
            T.op("dve", lambda e: e.memset(s[:, 0:1], 0.0), [], [b_s])
            T.op("act", lambda e: e.activation(out=junk[:], in_=x_ap, func=AF.Square, scale=pre_scale,
                                               accum_out=s[:, 0:1]), [b_x], [b_junk, b_s])
            T.op("act", lambda e: e.activation(out=s[:, 1:2], in_=s[:, 0:1], func=AF.Ln, scale=1.0 / D, bias=EPS),
                 [b_s], [b_s])
            T.op("act", lambda e: e.activation(out=s[:, 2:3], in_=s[:, 1:2], func=AF.Exp, scale=-0.5,
                                               bias=extra_bias), [b_s], [b_s])
            T.op("dve", lambda e: e.scalar_tensor_tensor(out=xs_ap, in0=x_ap, scalar=s[:, 2:3], in1=gbc[:],
                                                       op0=ALU.mult, op1=ALU.mult), [b_x, b_s, b_gbc], [b_xs])

        def transpose8(xs_t, b_xs, dst_ap, b_dst, pbi, evac_eng, n=8):
            pbt, b_pb = pb[pbi]
            for c in range(n):
                T.op("pe", lambda e, c=c: e.transpose(out=pbt[:, c * 128:(c + 1) * 128],
                                                      in_=xs_t[:, c * 128:(c + 1) * 128], identity=identb[:]),
                     [b_xs, b_identb], [b_pb], signal=(c == n - 1))
            src = pbt[:, 0:n * 128].rearrange("p (c t) -> p c t", c=n)
            if evac_eng == "act":
                T.op("act", lambda e: e.copy(out=dst_ap, in_=src), [b_pb], [b_dst])
            else:
                T.op("dve", lambda e: e.tensor_copy(out=dst_ap, in_=src), [b_pb], [b_dst])

        with ExitStack() as stCD:
            yatt = sb(stCD, "yatt", [128, NO, 512], BF16); b_yatt = Buf()

            with ExitStack() as stABC:
                Vall = sb(stABC, "Vall", [128, NT, 8, 65], BF16); b_Vall = Buf()
                QT = sb(stABC, "QT", [67, 8, SO], BF16); b_QT = Buf()
                T.op("pool", lambda e: e.memset(Vall[:, :, :, 64:65], 1.0), [], [b_Vall])

                with ExitStack() as stAB:
                    Wkvf = sb(stAB, "Wkvf", [128, 8, 1032], BF16); b_Wkvf = Buf()
                    Wq = sb(stAB, "Wq", [128, 8, 512], BF16); b_Wq = Buf()
                    wload(Wkvf[:], wkvf, b_Wkvf, "w_kvf", 8)
                    wload(Wq[:], wq, b_Wq, "w_q", 8)
                    xa = sbr(stAB, "xa", [128, D], F32, 6)
                    xs = sbr(stAB, "xs", [128, D], BF16, 4)
                    xsT = sbr(stAB, "xsT", [128, 8, 512], BF16, 2)
                    kst = sbr(stAB, "kst", [128, 4, 512], BF16, 2)
                    ftmp = sbr(stAB, "ftmp", [128, 32], F32, 2)
                    spt = sbr(stAB, "spt", [128, 32], F32, 2)
                    CS = sbr(stAB, "CS", [128, 8, 67], BF16, 2)
                    co = sbr(stAB, "co", [128, 8], F32, 2)
                    co2 = sbr(stAB, "co2", [128, 8], F32, 2)
                    for c_, b_ in CS:
                        T.op("pool", lambda e, c_=c_: e.memset(c_[:], 0.0), [], [b_])

                    tiles = [("A", t) for t in range(NT)] + [("B", t) for t in range(NO)]
                    NTT = len(tiles)

                    XR = 6
                    GT = NGA + NGO
                    fstat = {}

                    def loadG(g):
                        for l in range(4):
                            i = 4 * g + l
                            kind, t = tiles[i]
                            src = xf if kind == "A" else xo
                            xt, b_x = xa[i % XR]
                            T.dma("sp", lambda e, xt=xt, src=src, t=t: e.dma_start(out=xt[:], in_=src[t * 128:(t + 1) * 128, :]),
                                  [], [b_x], f"xa{i % XR}")

                    def frontG(g):
                        s_, b_s = stat[fr_n[0] % 4]
                        fr_n[0] += 1
                        T.op("dve", lambda e: e.memset(s_[:, 0:4], 0.0), [], [b_s])
                        for l in range(4):
                            xt, b_x = xa[(4 * g + l) % XR]
                            T.op("act", lambda e, xt=xt, l=l: e.activation(out=junk[:], in_=xt[:], func=AF.Square,
                                                                           accum_out=s_[:, l:l + 1]), [b_x], [b_junk, b_s])
                        T.op("act", lambda e: e.activation(out=s_[:, 4:8], in_=s_[:, 0:4], func=AF.Ln, scale=1.0 / D, bias=EPS),
                             [b_s], [b_s])
                        T.op("act", lambda e: e.activation(out=s_[:, 4:8], in_=s_[:, 4:8], func=AF.Exp, scale=-0.5), [b_s], [b_s])
                        fstat[g] = (s_, b_s)

                    def frontG2(g):
                        s_, b_s = fstat[g]
                        for l in range(4):
                            xt, b_x = xa[(4 * g + l) % XR]
                            xst, b_xs = xs[l]
                            T.op("dve", lambda e, xt=xt, xst=xst, l=l: e.scalar_tensor_tensor(
                                out=xst[:], in0=xt[:], scalar=s_[:, 4 + l:5 + l], in1=gbc[:], op0=ALU.mult, op1=ALU.mult),
                                [b_x, b_s, b_gbc], [b_xs])

                    def transG(g):
                        dT, b_dT = xsT[g % 2]
                        for l in range(4):
                            xst, b_xs = xs[l]
                            transpose8(xst, b_xs, dT[:, :, l * 128:(l + 1) * 128], b_dT, l % 2,
                                       "act" if l % 2 == 0 else "dve")

                    def backA(g):
                        dT, b_dT = xsT[g % 2]
                        ks, b_ks = kst[g % 2]
                        for hp in range(4):
                            pk, b_pk = pf[hp % 2]
                            for kc in range(8):
                                T.op("pe", lambda e, kc=kc, hp=hp, pk=pk: e.matmul(
                                    pk[:], lhsT=Wkvf[:, kc, hp * 128:(hp + 1) * 128], rhs=dT[:, kc, :],
                                    start=(kc == 0), stop=(kc == 7)), [b_Wkvf, b_dT], [b_pk], signal=(kc == 7))
                            T.op("dve", lambda e, hp=hp, pk=pk: e.tensor_copy(out=ks[:, hp, :], in_=pk[:]),
                                 [b_pk], [b_ks])
                        T.dma("sp", lambda e: e.dma_start(
                            out=kscr.rearrange("(hp p) s -> p hp s", p=128)[:, :, g * 512:(g + 1) * 512],
                            in_=ks[:]), [b_ks], [b_kscr], f"kst{g % 2}")

                    def backA2(g):
                        dT, b_dT = xsT[g % 2]
                        for l in range(4):
                            pv, b_pv = pf[2 + l % 2]
                            for kc in range(8):
                                T.op("pe", lambda e, kc=kc, l=l, pv=pv: e.matmul(
                                    pv[:], lhsT=dT[:, kc, l * 128:(l + 1) * 128], rhs=Wkvf[:, kc, 512:1024],
                                    start=(kc == 0), stop=(kc == 7)), [b_Wkvf, b_dT], [b_pv], signal=(kc == 7))
                            T.op("act", lambda e, l=l, pv=pv: e.copy(
                                out=Vall[:, 4 * g + l, :, 0:64], in_=pv[:].rearrange("p (h d) -> p h d", h=8)),
                                [b_pv], [b_Vall])
                        pF, b_pF = pf[4]
                        for l in range(4):
                            for kc in range(8):
                                T.op("pe", lambda e, kc=kc, l=l: e.matmul(
                                    pF[:, l * 8:(l + 1) * 8], lhsT=dT[:, kc, l * 128:(l + 1) * 128],
                                    rhs=Wkvf[:, kc, 1024:1032], start=(kc == 0), stop=(kc == 7)),
                                    [b_Wkvf, b_dT], [b_pF], signal=(kc == 7 and l == 3))
                        ft, b_ft = ftmp[g % 2]
                        sp_, b_sp = spt[g % 2]
                        T.op("dve", lambda e: e.tensor_tensor(
                            out=ft[:].rearrange("p (l h) -> p l h", l=4), in0=pF[:, 0:32].rearrange("p (l h) -> p l h", l=4),
                            in1=bfbc[:].unsqueeze(1).broadcast_to([128, 4, 8]), op=ALU.add), [b_pF, b_bfbc], [b_ft])
                        T.op("act", lambda e: e.activation(out=ft[:], in_=ft[:], func=AF.Exp, scale=-1.0), [b_ft], [b_ft])
                        T.op("act", lambda e: e.activation(out=sp_[:], in_=ft[:], func=AF.Ln, bias=1.0), [b_ft], [b_sp])

                    def backA3(g):
                        sp_, b_sp = spt[g % 2]
                        pC, b_pC = pf[5]
                        for l in range(4):
                            for l2 in range(l):
                                T.op("pe", lambda e, l=l, l2=l2: e.matmul(
                                    pC[:, l * 8:(l + 1) * 8], lhsT=onesf[:], rhs=sp_[:, l2 * 8:(l2 + 1) * 8],
                                    start=(l2 == 0), stop=False), [b_onesf, b_sp], [b_pC], signal=False)
                            T.op("pe", lambda e, l=l: e.matmul(
                                pC[:, l * 8:(l + 1) * 8], lhsT=triU[:], rhs=sp_[:, l * 8:(l + 1) * 8],
                                start=(l == 0), stop=True), [b_triU, b_sp], [b_pC], signal=False)
                        for l2 in range(4):
                            T.op("pe", lambda e, l2=l2: e.matmul(
                                pC[:, 32:40], lhsT=onesf[:], rhs=sp_[:, l2 * 8:(l2 + 1) * 8],
                                start=(l2 == 0), stop=(l2 == 3)), [b_onesf, b_sp], [b_pC], signal=(l2 == 3))
                        T.op("dve", lambda e: e.tensor_tensor(
                            out=cumsp[:, 4 * g:4 * g + 4, :], in0=pC[:, 0:32].rearrange("p (l h) -> p l h", l=4),
                            in1=carry[:].unsqueeze(1).broadcast_to([128, 4, 8]), op=ALU.add),
                            [b_pC, b_carry], [b_cumsp])
                        T.op("dve", lambda e: e.tensor_tensor(out=carry[:], in0=carry[:], in1=pC[:, 32:40], op=ALU.add),
                             [b_pC, b_carry], [b_carry])

                    def backB(go):
                        g = NGA + go
                        dT, b_dT = xsT[g % 2]
                        for h in range(8):
                            pq, b_pq = pf[h % 2]
                            for kc in range(8):
                                T.op("pe", lambda e, kc=kc, h=h, pq=pq: e.matmul(
                                    pq[0:64, :], lhsT=Wq[:, kc, h * 64:(h + 1) * 64], rhs=dT[:, kc, :],
                                    start=(kc == 0), stop=(kc == 7)), [b_Wq, b_dT], [b_pq], signal=(kc == 7))
                            if h % 2 == 0:
                                T.op("act", lambda e, h=h, pq=pq: e.copy(
                                    out=QT[0:64, h, go * 512:(go + 1) * 512], in_=pq[0:64, :]), [b_pq], [b_QT])
                            else:
                                T.op("dve", lambda e, h=h, pq=pq: e.tensor_copy(
                                    out=QT[0:64, h, go * 512:(go + 1) * 512], in_=pq[0:64, :]), [b_pq], [b_QT])
                        for l in range(4):
                            i = 4 * go + l
                            c1, b_c1 = co[i % 2]
                            c2, b_c2 = co2[i % 2]
                            cs, b_cs = CS[i % 2]
                            T.op("dve", lambda e, i=i, c1=c1: e.tensor_scalar(
                                out=c1[:], in0=cumsp[:, 4 * i, :], scalar1=sel[:, 0:1], scalar2=None, op0=ALU.mult),
                                [b_cumsp, b_sel], [b_c1])
                            for m in range(1, 4):
                                T.op("dve", lambda e, i=i, m=m, c1=c1: e.scalar_tensor_tensor(
                                    out=c1[:], in0=cumsp[:, 4 * i + m, :], scalar=sel[:, m:m + 1], in1=c1[:],
                                    op0=ALU.mult, op1=ALU.add), [b_cumsp, b_sel, b_c1], [b_c1])
                            T.op("dve", lambda e, c1=c1: e.tensor_scalar(
                                out=c1[:], in0=c1[:], scalar1=-8.0, scalar2=None, op0=ALU.mult), [b_c1], [b_c1])
                            T.op("dve", lambda e, c1=c1, cs=cs: e.tensor_copy(out=cs[:, :, 64], in_=c1[:]), [b_c1], [b_cs])
                            T.op("dve", lambda e, c1=c1, c2=c2, cs=cs: e.tensor_tensor(
                                out=c2[:], in0=c1[:], in1=cs[:, :, 64], op=ALU.subtract), [b_c1, b_cs], [b_c2])
                            T.op("dve", lambda e, c2=c2, cs=cs: e.tensor_copy(out=cs[:, :, 65], in_=c2[:]), [b_c2], [b_cs])
                            T.op("dve", lambda e, c1=c1, c2=c2, cs=cs: e.tensor_tensor(
                                out=c1[:], in0=c2[:], in1=cs[:, :, 65], op=ALU.subtract), [b_c2, b_cs], [b_c1])
                            T.op("dve", lambda e, c1=c1, cs=cs: e.tensor_copy(out=cs[:, :, 66], in_=c1[:]), [b_c1], [b_cs])
                            for half in range(2):
                                pa, b_pa = pf[2 + half]
                                for hh in range(4):
                                    h = half * 4 + hh
                                    T.op("pe", lambda e, h=h, hh=hh, pa=pa, cs=cs: e.matmul(
                                        pa[0:67, hh * 128:(hh + 1) * 128], lhsT=cs[:, h, :], rhs=identb[:],
                                        start=True, stop=True), [b_cs, b_identb], [b_pa], signal=(hh == 3))
                                T.op("act", lambda e, half=half, pa=pa, i=i: e.copy(
                                    out=QT[64:67, half * 4:half * 4 + 4, i * 128:(i + 1) * 128],
                                    in_=pa[64:67, :].rearrange("p (h t) -> p h t", h=4)), [b_pa], [b_QT])

                    loadG(0)
                    frontG(0)
                    frontG2(0)
                    if GT > 1:
                        loadG(1)
                    transG(0)
                    for g in range(GT):
                        if g + 1 < GT:
                            frontG(g + 1)
                        if g < NGA:
                            backA(g)
                        else:
                            backB(g - NGA)
                        if g + 1 < GT:
                            frontG2(g + 1)
                        if g + 2 < GT:
                            loadG(g + 2)
                        if g < NGA:
                            backA2(g)
                        if g + 1 < GT:
                            transG(g + 1)
                        if g < NGA:
                            backA3(g)
                    if dbg:
                        T.dma("sp", lambda e: e.dma_start(out=dbgo["d_cum"], in_=cumsp[:].rearrange("p t h -> p (t h)")),
                              [b_cumsp], [], "dbg")

                T.barrier()
                psum_free()
                with ExitStack() as stC:
                    pS = [(stC.enter_context(nc.psum_tensor(f"pS{i}", [128, 512], F32)), Buf(f"pS{i}")) for i in range(4)]
                    pO = [(stC.enter_context(nc.psum_tensor(f"pO{i}", [128, 512], F32)), Buf(f"pO{i}")) for i in range(4)]
                    KT = sbr(stC, "KT", [67, S], BF16, 2)
                    PT = sbr(stC, "PT", [128, 512], BF16, 4)
                    rec = sbr(stC, "rec", [128, 4], F32, 4)
                    for kt_, b_ in KT:
                        T.op("pool", lambda e, kt_=kt_: e.memset(kt_[64:67, :], 1.0), [], [b_])
                    for k_ in ("wu", "wv2", "wg", "wbs", "wba", "wout", "wup", "wdn"):
                        precast(k_)

                    def loadK(h):
                        kt_, b_ = KT[h % 2]
                        T.dma("sp", lambda e: e.dma_start(out=kt_[0:64, :], in_=kscr[h * 64:(h + 1) * 64, :]),
                              [b_kscr], [b_], f"kt{h % 2}")

                    gsets = [[a] for a in range(NGO)]
                    items = []
                    for h in range(8):
                        for si, gs in enumerate(gsets):
                            for kt in range(16 * gs[-1] + 16):
                                items.append((h, si, kt))

                    def colstart(go, kt):
                        if kt < 16 * go:
                            return 0, None
                        l = kt // 4 - 4 * go
                        return l * 128, kt % 4

                    def active(si, kt):
                        r = []
                        for slot, go in enumerate(gsets[si]):
                            if kt <= 16 * go + 15:
                                c0, m = colstart(go, kt)
                                r.append((slot, go, c0, m))
                        return r

                    def emitS(n):
                        h, si, kt = items[n]
                        kt_, b_k = KT[h % 2]
                        ps_, b_ps = pS[n % 4]
                        for slot, go, c0, m in active(si, kt):
                            T.op("pe", lambda e, slot=slot, go=go, c0=c0: e.matmul(
                                ps_[:, slot * 512 + c0:(slot + 1) * 512], lhsT=kt_[0:67, kt * 128:(kt + 1) * 128],
                                rhs=QT[0:67, h, go * 512 + c0:(go + 1) * 512], start=True, stop=True),
                                [b_k, b_QT], [b_ps])

                    def emitE(n):
                        h, si, kt = items[n]
                        ps_, b_ps = pS[n % 4]
                        pt_, b_pt = PT[n % 4]
                        act = active(si, kt)
                        for slot, go, c0, m in act:
                            if m is not None:
                                lo = slot * 512 + c0
                                T.op("dve", lambda e, lo=lo, m=m: e.tensor_tensor(
                                    out=ps_[:, lo:lo + 128], in0=ps_[:, lo:lo + 128], in1=madd[:, m, :], op=ALU.add),
                                    [b_ps, b_madd], [b_ps])
                        lo = act[0][0] * 512 + act[0][2]
                        hi = (act[-1][0] + 1) * 512
                        T.op("act", lambda e: e.activation(
                            out=pt_[:, lo:hi], in_=ps_[:, lo:hi], func=AF.Exp, scale=0.125,
                            bias=cumsp[:, kt, h:h + 1]), [b_ps, b_cumsp], [b_pt])

                    def emitPV(n):
                        h, si, kt = items[n]
                        pt_, b_pt = PT[n % 4]
                        ring = 2 * ((h * len(gsets) + si) % 2)
                        for slot, go, c0, m in active(si, kt):
                            po, b_po = pO[ring + slot]
                            l0 = c0 // 128
                            for l in range(l0, 4):
                                last = 4 * (4 * go + l) + 3
                                T.op("pe", lambda e, l=l, last=last, slot=slot, po=po: e.matmul(
                                    po[:, l * 65:(l + 1) * 65], lhsT=pt_[:, slot * 512 + l * 128:slot * 512 + (l + 1) * 128],
                                    rhs=Vall[:, kt, h, :], start=(kt == 0 and l == 0), stop=(kt == last),
                                    skip_group_check=True),
                                    [b_pt, b_Vall], [b_po], signal=(l == 3))
                            if kt == 16 * go + 15:
                                rc, b_rc = rec[ring + slot]
                                pov = po[:, 0:260].rearrange("p (l d) -> p l d", l=4)
                                T.op("dve", lambda e, rc=rc, pov=pov: e.reciprocal(out=rc[:], in_=pov[:, :, 64]), [b_po], [b_rc])
                                for l in range(4):
                                    T.op("dve", lambda e, l=l, rc=rc, pov=pov, go=go: e.tensor_scalar(
                                        out=yatt[:, 4 * go + l, h * 64:(h + 1) * 64], in0=pov[:, l, 0:64],
                                        scalar1=rc[:, l:l + 1], scalar2=None, op0=ALU.mult), [b_po, b_rc], [b_yatt])

                    loadK(0)
                    NI = len(items)
                    per_h = NI // 8
                    for n in range(NI + 3):
                        if n < NI:
                            if n % per_h == 0:
                                hh = n // per_h
                                if hh + 1 < 8:
                                    loadK(hh + 1)
                            emitS(n)
                        if 2 <= n < NI + 2:
                            emitE(n - 2)
                        if n >= 3:
                            emitPV(n - 3)
                    if dbg:
                        T.dma("sp", lambda e: e.dma_start(out=dbgo["d_yatt"], in_=yatt[:].rearrange("p t c -> p (t c)")),
                              [b_yatt], [], "dbg")
                T.barrier()
                psum_std()

            T.barrier()
            with ExitStack() as stD:
                Wu = sb(stD, "Wu", [128, 8, 512], BF16); b_Wu = Buf()
                Wv2 = sb(stD, "Wv2", [128, 8, 512], BF16); b_Wv2 = Buf()
                Wg = sb(stD, "Wg", [128, 8, 2048], BF16); b_Wg = Buf()
                Wbs = sb(stD, "Wbs", [128, 4, D], BF16); b_Wbs = Buf()
                Wba = sb(stD, "Wba", [128, 4, D], BF16); b_Wba = Buf()
                Wout = sb(stD, "Wout", [128, 8, D], BF16); b_Wout = Buf()
                wload2(Wu[:], "wu", b_Wu)
                wload2(Wv2[:], "wv2", b_Wv2)
                gpost = sb(stD, "gpost", [128, D], F32); b_gpost = Buf()
                gsg = sb(stD, "gsg", [128, 512], F32); b_gsg = Buf()
                bsg = sb(stD, "bsg", [128, 512], F32); b_bsg = Buf()
                bsp = sb(stD, "bsp", [128, 4, 128], F32); b_bsp = Buf()
                wsTm = sb(stD, "wsTm", [128, 8, 128], BF16); b_wsTm = Buf()
                T.dma("sp", lambda e: e.dma_start(out=gpost[:], in_=g_post.partition_broadcast(128)), [], [b_gpost], "c2")
                T.dma("sp", lambda e: e.dma_start(out=gsg[:], in_=g_sgu.partition_broadcast(128)), [], [b_gsg], "c2")
                T.dma("sp", lambda e: e.dma_start(out=bsg[:], in_=b_sgu.partition_broadcast(128)), [], [b_bsg], "c2")
                T.dma("sp", lambda e: e.dma_start(out=bsp[:], in_=bsp4.rearrange("p (k t) -> p k t", k=4)[:, :, 0:128]), [], [b_bsp], "c2")
                h1t = sbr(stD, "h1t", [128, D], F32, 2)
                T.dma("sp", lambda e: e.dma_start(out=h1t[0][0][:], in_=wsT), [], [h1t[0][1]], "c2")
                T.op("dve", lambda e: e.tensor_tensor(out=wsTm[:], in0=h1t[0][0][:].rearrange("p (g t) -> p g t", g=8),
                                                      in1=triU[:].unsqueeze(1).broadcast_to([128, 8, 128]), op=ALU.mult),
                     [h1t[0][1], b_triU], [b_wsTm])

                xg = sbr(stD, "xg", [128, D], F32, 2)
                xsd = sbr(stD, "xsd", [128, D], BF16, 2)
                xsTg = sbr(stD, "xsTg", [128, 8, 512], BF16, 2)
                uT = sbr(stD, "uT", [128, 4, 512], BF16, 1)
                vg = sbr(stD, "vg", [128, 512], F32, 4)
                vnb = sbr(stD, "vnb", [128, 512], BF16, 4)
                lnst = sbr(stD, "lnst", [128, 16], F32, 2)
                t1 = sbr(stD, "t1", [128, 512], F32, 1)
                ysT = sbr(stD, "ysT", [128, 4, 512], BF16, 2)
                yaT = sbr(stD, "yaT", [128, 4, 512], BF16, 2)
                tg1 = sbr(stD, "tg1", [128, 512], F32, 2)
                tg2 = sbr(stD, "tg2", [128, 512], F32, 2)
                mT = sbr(stD, "mT", [128, 8, 512], BF16, 1)
                bank_n = [0]

                def bank():
                    r = pf[bank_n[0] % 6]
                    bank_n[0] += 1
                    return r

                def rstd_batch(s, b_s, n, src0, dst0, scale, extra_bias=0.0):
                    T.op("act", lambda e: e.activation(out=s[:, dst0:dst0 + n], in_=s[:, src0:src0 + n], func=AF.Ln,
                                                       scale=scale, bias=EPS), [b_s], [b_s])
                    T.op("act", lambda e: e.activation(out=s[:, dst0:dst0 + n], in_=s[:, dst0:dst0 + n], func=AF.Exp,
                                                       scale=-0.5, bias=extra_bias), [b_s], [b_s])

                def genX(go):
                    dT, b_dT = xsTg[go % 2]
                    for pr in range(2):
                        s, b_s = stat[fr_n[0] % 4]
                        fr_n[0] += 1
                        T.op("pool", lambda e: e.memset(s[:, 0:2], 0.0), [], [b_s])
                        for q in range(2):
                            l = 2 * pr + q
                            i = 4 * go + l
                            xt, b_x = xg[q]
                            T.dma("sp", lambda e, xt=xt, i=i: e.dma_start(out=xt[:], in_=xo[i * 128:(i + 1) * 128, :]),
                                  [], [b_x], f"xg{q}")
                            T.op("act", lambda e, xt=xt, q=q: e.activation(out=junk[:], in_=xt[:], func=AF.Square,
                                                                           accum_out=s[:, q:q + 1]), [b_x], [b_junk, b_s])
                        rstd_batch(s, b_s, 2, 0, 2, 1.0 / D)
                        for q in range(2):
                            l = 2 * pr + q
                            xt, b_x = xg[q]
                            xst, b_xs = xsd[q]
                            T.op("dve", lambda e, xt=xt, xst=xst, q=q: e.scalar_tensor_tensor(
                                out=xst[:], in0=xt[:], scalar=s[:, 2 + q:3 + q], in1=gbc[:], op0=ALU.mult, op1=ALU.mult),
                                [b_x, b_s, b_gbc], [b_xs])
                            transpose8(xst, b_xs, dT[:, :, l * 128:(l + 1) * 128], b_dT, q, "act" if q == 0 else "dve")
                        yield
                    ut, b_ut = uT[0]
                    for c in range(4):
                        pu, b_pu = bank()
                        for kc in range(8):
                            T.op("pe", lambda e, kc=kc, c=c, pu=pu: e.matmul(
                                pu[:], lhsT=Wu[:, kc, c * 128:(c + 1) * 128], rhs=dT[:, kc, :],
                                start=(kc == 0), stop=(kc == 7)), [b_Wu, b_dT], [b_pu], signal=(kc == 7))
                        T.op("act", lambda e, c=c, pu=pu: e.activation(out=ut[:, c, :], in_=pu[:], func=AF.Gelu_apprx_tanh),
                             [b_pu], [b_ut])
                        yield
                    ls, b_ls = lnst[go % 2]
                    T.op("pool", lambda e: e.memset(ls[:], 0.0), [], [b_ls])
                    for l in range(4):
                        pv, b_pv = bank()
                        for kc in range(8):
                            T.op("pe", lambda e, kc=kc, l=l, pv=pv: e.matmul(
                                pv[:], lhsT=dT[:, kc, l * 128:(l + 1) * 128], rhs=Wv2[:, kc, :],
                                start=(kc == 0), stop=(kc == 7)), [b_Wv2, b_dT], [b_pv], signal=(kc == 7))
                        v_, b_v = vg[l]
                        T.op("act", lambda e, l=l, v_=v_, pv=pv: e.activation(
                            out=v_[:], in_=pv[:], func=AF.Gelu_apprx_tanh, accum_out=ls[:, l:l + 1]), [b_pv], [b_v, b_ls])
                        T.op("act", lambda e, l=l, v_=v_: e.activation(out=junk[:, 0:512], in_=v_[:], func=AF.Square,
                                                                       accum_out=ls[:, 4 + l:5 + l]), [b_v], [b_junk, b_ls])
                        yield
                    T.op("dve", lambda e: e.tensor_scalar(out=ls[:, 0:4], in0=ls[:, 0:4], scalar1=1.0 / 512, scalar2=None,
                                                          op0=ALU.mult), [b_ls], [b_ls])
                    T.op("dve", lambda e: e.tensor_tensor(out=ls[:, 8:12], in0=ls[:, 0:4], in1=ls[:, 0:4], op=ALU.mult),
                         [b_ls], [b_ls])
                    T.op("dve", lambda e: e.scalar_tensor_tensor(out=ls[:, 4:8], in0=ls[:, 4:8], scalar=1.0 / 512,
                                                                 in1=ls[:, 8:12], op0=ALU.mult, op1=ALU.subtract),
                         [b_ls], [b_ls])
                    rstd_batch(ls, b_ls, 4, 4, 12, 1.0)
                    for l in range(4):
                        v_, b_v = vg[l]
                        vb, b_vb = vnb[l]
                        T.op("dve", lambda e, l=l, v_=v_: e.tensor_scalar(
                            out=v_[:], in0=v_[:], scalar1=ls[:, l:l + 1], scalar2=ls[:, 12 + l:13 + l],
                            op0=ALU.subtract, op1=ALU.mult), [b_v, b_ls], [b_v])
                        T.op("dve", lambda e, v_=v_: e.tensor_tensor(out=v_[:], in0=v_[:], in1=gsg[:], op=ALU.mult),
                             [b_v, b_gsg], [b_v])
                        T.op("pool", lambda e, v_=v_, vb=vb: e.tensor_tensor(out=vb[:], in0=v_[:], in1=bsg[:], op=ALU.add),
                             [b_v, b_bsg], [b_vb])
                    yield
                    ys, b_ys = ysT[go % 2]
                    for k in range(4):
                        pA, b_pA = bank()
                        pB, b_pB = bank()
                        for l in range(4):
                            vb, b_vb = vnb[l]
                            T.op("pe", lambda e, l=l, k=k, vb=vb, pA=pA: e.matmul(
                                pA[:, l * 128:(l + 1) * 128], lhsT=vb[:, k * 128:(k + 1) * 128], rhs=wsTm[:, 2 * k, :],
                                start=True, stop=True), [b_vb, b_wsTm], [b_pA], signal=(l == 3))
                        for l in range(4):
                            vb, b_vb = vnb[l]
                            T.op("pe", lambda e, l=l, k=k, vb=vb, pB=pB: e.matmul(
                                pB[:, l * 128:(l + 1) * 128], lhsT=vb[:, k * 128:(k + 1) * 128], rhs=wsTm[:, 2 * k + 1, :],
                                start=True, stop=True), [b_vb, b_wsTm], [b_pB], signal=(l == 3))
                        tt, b_tt = t1[0]
                        T.op("dve", lambda e, k=k, tt=tt, pA=pA: e.tensor_tensor(
                            out=tt[0:64, :].rearrange("p (l t) -> p l t", l=4), in0=pA[0:64, :].rearrange("p (l t) -> p l t", l=4),
                            in1=bsp[0:64, k, :].unsqueeze(1).broadcast_to([64, 4, 128]), op=ALU.add), [b_pA, b_bsp], [b_tt])
                        T.op("dve", lambda e, k=k, tt=tt, pB=pB: e.tensor_tensor(
                            out=tt[64:128, :].rearrange("p (l t) -> p l t", l=4), in0=pB[64:128, :].rearrange("p (l t) -> p l t", l=4),
                            in1=bsp[64:128, k, :].unsqueeze(1).broadcast_to([64, 4, 128]), op=ALU.add), [b_pB, b_bsp], [b_tt])
                        T.op("pool", lambda e, k=k, tt=tt: e.tensor_tensor(
                            out=ys[:, k, :], in0=tt[:], in1=ut[:, k, :], op=ALU.mult), [b_tt, b_ut], [b_ys])
                        yield
                    ya, b_ya = yaT[go % 2]
                    for l in range(4):
                        i = 4 * go + l
                        transpose8(yatt[:, i, :], b_yatt, ya[:, :, l * 128:(l + 1) * 128], b_ya, l % 2,
                                   "act" if l % 2 == 0 else "dve", n=4)
                        yield

                def genY(go):
                    dT, b_dT = xsTg[go % 2]
                    ys, b_ys = ysT[go % 2]
                    ya, b_ya = yaT[go % 2]
                    mt, b_mt = mT[0]
                    for oc in range(8):
                        pG1, b_pG1 = bank()
                        pG2, b_pG2 = bank()
                        pP1, b_pP1 = bank()
                        pP2, b_pP2 = bank()
                        for kc in range(8):
                            T.op("pe", lambda e, kc=kc, oc=oc, pG1=pG1: e.matmul(
                                pG1[:], lhsT=Wg[:, kc, oc * 128:(oc + 1) * 128], rhs=dT[:, kc, :],
                                start=(kc == 0), stop=(kc == 7)), [b_Wg, b_dT], [b_pG1], signal=(kc == 7))
                        for kc in range(8):
                            T.op("pe", lambda e, kc=kc, oc=oc, pG2=pG2: e.matmul(
                                pG2[:], lhsT=Wg[:, kc, 1024 + oc * 128:1024 + (oc + 1) * 128], rhs=dT[:, kc, :],
                                start=(kc == 0), stop=(kc == 7)), [b_Wg, b_dT], [b_pG2], signal=(kc == 7))
                        for c in range(4):
                            T.op("pe", lambda e, c=c, oc=oc, pP1=pP1: e.matmul(
                                pP1[:], lhsT=Wbs[:, c, oc * 128:(oc + 1) * 128], rhs=ys[:, c, :],
                                start=(c == 0), stop=(c == 3)), [b_Wbs, b_ys], [b_pP1], signal=(c == 3))
                        for c in range(4):
                            T.op("pe", lambda e, c=c, oc=oc, pP2=pP2: e.matmul(
                                pP2[:], lhsT=Wba[:, c, oc * 128:(oc + 1) * 128], rhs=ya[:, c, :],
                                start=(c == 0), stop=(c == 3)), [b_Wba, b_ya], [b_pP2], signal=(c == 3))
                        a1, b_a1 = tg1[oc % 2]
                        a2, b_a2 = tg2[oc % 2]
                        T.op("act", lambda e, a1=a1, pG1=pG1: e.activation(out=a1[:], in_=pG1[:], func=AF.Tanh, scale=0.5),
                             [b_pG1], [b_a1])
                        T.op("act", lambda e, a2=a2, pG2=pG2: e.activation(out=a2[:], in_=pG2[:], func=AF.Tanh, scale=0.5),
                             [b_pG2], [b_a2])
                        T.op("dve", lambda e, a1=a1, pP1=pP1: e.scalar_tensor_tensor(
                            out=a1[:], in0=a1[:], scalar=1.0, in1=pP1[:], op0=ALU.add, op1=ALU.mult),
                            [b_a1, b_pP1], [b_a1])
                        T.op("dve", lambda e, a2=a2, pP2=pP2: e.scalar_tensor_tensor(
                            out=a2[:], in0=a2[:], scalar=1.0, in1=pP2[:], op0=ALU.add, op1=ALU.mult),
                            [b_a2, b_pP2], [b_a2])
                        T.op("pool", lambda e, a1=a1, a2=a2, oc=oc: e.tensor_tensor(
                            out=mt[:, oc, :], in0=a1[:], in1=a2[:], op=ALU.add), [b_a1, b_a2], [b_mt])
                        yield
                    for pr in range(2):
                        s, b_s = stat[fr_n[0] % 4]
                        fr_n[0] += 1
                        T.op("pool", lambda e, s=s: e.memset(s[:, 0:4], 0.0), [], [b_s])
                        banks = []
                        for q in range(2):
                            l = 2 * pr + q
                            for half in range(2):
                                pH, b_pH = bank()
                                banks.append((pH, b_pH))
                                for oc in range(8):
                                    T.op("pe", lambda e, oc=oc, l=l, half=half, pH=pH: e.matmul(
                                        pH[:], lhsT=mt[:, oc, l * 128:(l + 1) * 128],
                                        rhs=Wout[:, oc, half * 512:(half + 1) * 512],
                                        start=(oc == 0), stop=(oc == 7)), [b_mt, b_Wout], [b_pH], signal=(oc == 7))
                                T.op("act", lambda e, s=s, pH=pH, q=q, half=half: e.activation(
                                    out=junk[:, half * 512:(half + 1) * 512], in_=pH[:], func=AF.Square, scale=0.5,
                                    accum_out=s[:, 2 * q + half:2 * q + half + 1]), [b_pH], [b_junk, b_s])
                        T.op("dve", lambda e, s=s: e.tensor_tensor(out=s[:, 4:5], in0=s[:, 0:1], in1=s[:, 1:2], op=ALU.add),
                             [b_s], [b_s])
                        T.op("dve", lambda e, s=s: e.tensor_tensor(out=s[:, 5:6], in0=s[:, 2:3], in1=s[:, 3:4], op=ALU.add),
                             [b_s], [b_s])
                        rstd_batch(s, b_s, 2, 4, 6, 1.0 / D, extra_bias=math.log(0.5))
                        for q in range(2):
                            l = 2 * pr + q
                            i = 4 * go + l
                            ht, b_ht = h1t[q]
                            for half in range(2):
                                pH, b_pH = banks[2 * q + half]
                                T.op("dve", lambda e, s=s, ht=ht, pH=pH, q=q, half=half: e.scalar_tensor_tensor(
                                    out=ht[:, half * 512:(half + 1) * 512], in0=pH[:], scalar=s[:, 6 + q:7 + q],
                                    in1=gpost[:, half * 512:(half + 1) * 512], op0=ALU.mult, op1=ALU.mult),
                                    [b_pH, b_s, b_gpost], [b_ht])
                            T.dma("sp", lambda e, ht=ht, i=i: e.dma_start(out=h1scr[i * 128:(i + 1) * 128, :], in_=ht[:]),
                                  [b_ht], [b_h1scr], f"h1s{q}")
                            if dbg:
                                T.dma("sp", lambda e, ht=ht, i=i: e.dma_start(out=dbgo["d_h1"][i * 128:(i + 1) * 128, :], in_=ht[:]),
                                      [b_ht], [], "dbg")
                        yield

                def run_interleaved(ga, gb, nb_per_a):
                    da = db = False
                    while not (da and db):
                        if not da:
                            try:
                                next(ga)
                            except StopIteration:
                                da = True
                        for _ in range(nb_per_a):
                            if db:
                                break
                            try:
                                next(gb)
                            except StopIteration:
                                db = True
                        if da and not db:
                            for _ in gb:
                                pass
                            db = True

                gx0 = genX(0)
                next(gx0)
                next(gx0)
                wload2(Wg[:], "wg", b_Wg, nsplit=4)
                wload2(Wbs[:], "wbs", b_Wbs)
                wload2(Wba[:], "wba", b_Wba)
                wload2(Wout[:], "wout", b_Wout)
                for _ in gx0:
                    pass
                for go in range(NGO):
                    gy = genY(go)
                    gx = genX(go + 1) if go + 1 < NGO else iter(())
                    run_interleaved(gy, gx, 2)

        T.barrier()
        with ExitStack() as stE:
            Wup = sb(stE, "Wup", [128, 8, 8, 512], BF16)
            Wdn = sb(stE, "Wdn", [128, 32, D], BF16)
            b_Wup = [Buf() for _ in range(8)]
            b_Wdn = [Buf() for _ in range(8)]
            gpost2 = sb(stE, "gpost2", [128, D], F32); b_gpost2 = Buf()
            T.dma("sp", lambda e: e.dma_start(out=gpost2[:], in_=g_fpost.partition_broadcast(128)), [], [b_gpost2], "c3")
            T.dma("sp", lambda e: e.dma_start(out=gbc[:], in_=g_fpre.partition_broadcast(128)), [], [b_gbc], "c3")
            hg = sbr(stE, "hg", [128, D], F32, 4)
            xe = sbr(stE, "xe", [128, D], F32, 2)
            xs2 = sbr(stE, "xs2", [128, D], BF16, 2)
            x2T = sbr(stE, "x2T", [128, 8, 256], BF16, 2)
            rt = sbr(stE, "rt", [128, 256], F32, 3)
            hT = sbr(stE, "hT", [128, 32, 256], BF16, 1)
            ot = sbr(stE, "ot", [128, D], F32, 1)
            NSG = NO // 2
            bank_n = [0]

            def bank():
                r = pf[bank_n[0] % 6]
                bank_n[0] += 1
                return r

            def frontE(sg):
                dT, b_dT = x2T[sg % 2]
                s, b_s = stat[fr_n[0] % 4]
                fr_n[0] += 1
                T.op("pool", lambda e: e.memset(s[:, 0:2], 0.0), [], [b_s])
                for l in range(2):
                    i = 2 * sg + l
                    ht, b_h = hg[i % 4]
                    xt, b_x = xe[l]
                    T.dma("sp", lambda e, ht=ht, i=i: e.dma_start(out=ht[:], in_=h1scr[i * 128:(i + 1) * 128, :]),
                          [b_h1scr], [b_h], f"hg{i % 4}")
                    T.dma("sp", lambda e, xt=xt, i=i: e.dma_start(out=xt[:], in_=xo[i * 128:(i + 1) * 128, :]),
                          [], [b_x], f"xe{l}")
                    T.op("dve", lambda e, ht=ht, xt=xt: e.tensor_tensor(out=ht[:], in0=ht[:], in1=xt[:], op=ALU.add),
                         [b_h, b_x], [b_h])
                    T.op("act", lambda e, ht=ht, l=l: e.activation(out=junk[:], in_=ht[:], func=AF.Square,
                                                                   accum_out=s[:, l:l + 1]), [b_h], [b_junk, b_s])
                T.op("act", lambda e: e.activation(out=s[:, 2:4], in_=s[:, 0:2], func=AF.Ln, scale=1.0 / D, bias=EPS),
                     [b_s], [b_s])
                T.op("act", lambda e: e.activation(out=s[:, 2:4], in_=s[:, 2:4], func=AF.Exp, scale=-0.5), [b_s], [b_s])
                for l in range(2):
                    i = 2 * sg + l
                    ht, b_h = hg[i % 4]
                    xst, b_xs = xs2[l]
                    T.op("dve", lambda e, ht=ht, xst=xst, l=l: e.scalar_tensor_tensor(
                        out=xst[:], in0=ht[:], scalar=s[:, 2 + l:3 + l], in1=gbc[:], op0=ALU.mult, op1=ALU.mult),
                        [b_h, b_s, b_gbc], [b_xs])

            def frontET(sg):
                dT, b_dT = x2T[sg % 2]
                for l in range(2):
                    xst, b_xs = xs2[l]
                    transpose8(xst, b_xs, dT[:, :, l * 128:(l + 1) * 128], b_dT, l, "act" if l == 0 else "dve")

            def upE(sg, mid=None):
                dT, b_dT = x2T[sg % 2]
                h_, b_hT = hT[0]
                for j in range(32):
                    if j == 12 and mid is not None:
                        mid()
                    pu, b_pu = bank()
                    for kc in range(8):
                        T.op("pe", lambda e, kc=kc, j=j, pu=pu: e.matmul(
                            pu[:, 0:256], lhsT=Wup[:, j // 4, kc, (j % 4) * 128:(j % 4 + 1) * 128], rhs=dT[:, kc, :],
                            start=(kc == 0), stop=(kc == 7)), [b_Wup[j // 4], b_dT], [b_pu], signal=(kc == 7))
                    r_, b_r = rt[j % 3]
                    T.op("act", lambda e, r_=r_, pu=pu: e.activation(out=r_[:], in_=pu[:, 0:256], func=AF.Relu),
                         [b_pu], [b_r])
                    T.op("dve" if j % 2 == 0 else "pool", lambda e, r_=r_, j=j: e.tensor_tensor(
                        out=h_[:, j, :], in0=r_[:], in1=r_[:], op=ALU.mult), [b_r], [b_hT])

            def downE(sg):
                h_, b_hT = hT[0]
                for l in range(2):
                    i = 2 * sg + l
                    pD0, b_pD0 = bank()
                    pD1, b_pD1 = bank()
                    for half, (pD, b_pD) in enumerate(((pD0, b_pD0), (pD1, b_pD1))):
                        for j in range(32):
                            T.op("pe", lambda e, j=j, l=l, half=half, pD=pD: e.matmul(
                                pD[:], lhsT=h_[:, j, l * 128:(l + 1) * 128], rhs=Wdn[:, j, half * 512:(half + 1) * 512],
                                start=(j == 0), stop=(j == 31)), [b_hT, b_Wdn[j // 4]], [b_pD], signal=(j == 31))
                    s, b_s = stat[fr_n[0] % 4]
                    fr_n[0] += 1
                    T.op("pool", lambda e, s=s: e.memset(s[:, 0:2], 0.0), [], [b_s])
                    T.op("act", lambda e, s=s, pD0=pD0: e.activation(out=junk[:, 0:512], in_=pD0[:], func=AF.Square,
                                                                     accum_out=s[:, 0:1]), [b_pD0], [b_junk, b_s])
                    T.op("act", lambda e, s=s, pD1=pD1: e.activation(out=junk[:, 512:1024], in_=pD1[:], func=AF.Square,
                                                                     accum_out=s[:, 1:2]), [b_pD1], [b_junk, b_s])
                    T.op("dve", lambda e, s=s: e.tensor_tensor(out=s[:, 2:3], in0=s[:, 0:1], in1=s[:, 1:2], op=ALU.add),
                         [b_s], [b_s])
                    T.op("act", lambda e, s=s: e.activation(out=s[:, 3:4], in_=s[:, 2:3], func=AF.Ln, scale=1.0 / D, bias=EPS),
                         [b_s], [b_s])
                    T.op("act", lambda e, s=s: e.activation(out=s[:, 4:5], in_=s[:, 3:4], func=AF.Exp, scale=-0.5), [b_s], [b_s])
                    o_, b_o = ot[0]
                    ht, b_h = hg[i % 4]
                    T.op("dve", lambda e, s=s, o_=o_, pD0=pD0: e.scalar_tensor_tensor(
                        out=o_[:, 0:512], in0=pD0[:], scalar=s[:, 4:5], in1=gpost2[:, 0:512], op0=ALU.mult, op1=ALU.mult),
                        [b_pD0, b_s, b_gpost2], [b_o])
                    T.op("dve", lambda e, s=s, o_=o_, pD1=pD1: e.scalar_tensor_tensor(
                        out=o_[:, 512:1024], in0=pD1[:], scalar=s[:, 4:5], in1=gpost2[:, 512:1024], op0=ALU.mult, op1=ALU.mult),
                        [b_pD1, b_s, b_gpost2], [b_o])
                    T.op("dve", lambda e, o_=o_, ht=ht: e.tensor_tensor(out=o_[:], in0=o_[:], in1=ht[:], op=ALU.add),
                         [b_o, b_h], [b_o])
                    T.dma("sp", lambda e, o_=o_, i=i: e.dma_start(out=out[i * 128:(i + 1) * 128, :], in_=o_[:]),
                          [b_o], [], f"ost{i % 2}")

            wupv = wscr["wup"].rearrange("p (pc kc n) -> p pc kc n", pc=8, kc=8)
            wdnv = wscr["wdn"].rearrange("p (kc n) -> p kc n", kc=32)
            for p in range(8):
                T.dma("act", lambda e, p=p: e.dma_start(
                    out=Wdn[:, p * 4:(p + 1) * 4, :], in_=wdnv[:, p * 4:(p + 1) * 4, :]),
                    [b_wscr["wdn"]], [b_Wdn[p]], f"w_dn{p}")
            frontE(0)
            for p in range(8):
                T.dma("sp", lambda e, p=p: e.dma_start(out=Wup[:, p, :, :], in_=wupv[:, p, :, :]),
                      [b_wscr["wup"]], [b_Wup[p]], f"w_up{p}")
            frontET(0)
            for sg in range(NSG):
                upE(sg, (lambda sg=sg: frontE(sg + 1)) if sg + 1 < NSG else None)
                if sg + 1 < NSG:
                    frontET(sg + 1)
                downE(sg)
            for k in ("ost0", "ost1", "dbg"):
                if k in T.semobj:
                    nc.sync.wait_ge(T.semobj[k], T.cnt[k])
        psum_free()
        print(f"[build] ops={T.nops} waits={T.nwaits} sems={len(T.semobj)}")
    return nc


def _host_inputs(inputs, NT):
    f = lambda a: np.ascontiguousarray(np.asarray(a, dtype=np.float32))
    x = f(inputs["x"])
    w_in = f(inputs["w_in"])[0]
    wz, wq_, wk_, wv_, wf_, wg_ = np.split(w_in, [1024, 1536, 2048, 2560, 2568], axis=1)
    common = {
        "wkvf": f(np.concatenate([wk_, wv_, wf_], axis=1)),
        "wq": f(wq_),
        "wu": f(wz[:, :512]),
        "wv2": f(wz[:, 512:]),
        "wg": f(wg_),
        "wbs": f(inputs["w_branch_sgu"])[0],
        "wba": f(inputs["w_branch_attn"])[0],
        "wout": f(inputs["w_out"])[0],
        "wup": f(inputs["w_up"])[0],
        "wdn": f(inputs["w_down"])[0],
        "g_pre": f(inputs["g_mix_pre"]),
        "g_post": f(inputs["g_mix_post"]),
        "g_fpre": f(inputs["g_ffn_pre"]),
        "g_fpost": f(inputs["g_ffn_post"]),
        "g_sgu": f(inputs["g_sgu"]),
        "b_sgu": f(inputs["b_sgu"]),
        "b_fg": f(inputs["b_forget"]),
        "wsT": f(np.transpose(f(inputs["w_spatial"])[0], (2, 0, 1)).reshape(128, 8 * 128)),
        "c_id": np.eye(128, dtype=np.float32),
        "c_tri": np.triu(np.ones((128, 128), np.float32)),
        "c_one": np.ones((128, 128), np.float32),
    }
    bs = f(inputs["b_spatial"])[0]
    bsp = np.repeat(bs.reshape(4, 2, 1, 128), 64, axis=2).reshape(4, 128, 128)
    bsp4 = np.tile(bsp[:, :, None, :], (1, 1, 4, 1)).reshape(4, 128, 512)
    common["bsp4"] = f(np.transpose(bsp4, (1, 0, 2)).reshape(128, 4 * 512))
    maps = []
    s_idx = np.arange(128)[:, None]
    t_idx = np.arange(128)[None, :]
    for c in range(8):
        b, j = c // 4, c % 4
        xb = x[b]
        xo = xb.reshape(NT, 128, D)[j::4].reshape(-1, D)
        sel = np.zeros((128, 4), np.float32)
        sel[:, j] = 1.0
        madd = np.zeros((128, 4, 128), np.float32)
        for m in range(4):
            if m == j:
                madd[:, m, :] = np.where(s_idx <= t_idx, 0.0, NEG)
            elif m > j:
                madd[:, m, :] = NEG
        d = dict(common)
        d["xf"] = f(xb)
        d["xo"] = f(xo)
        d["c_sel"] = sel
        d["c_madd"] = madd.reshape(128, 512)
        maps.append(d)
    return maps


_NC_CACHE = {}


def kernel(**inputs):
    x = np.asarray(inputs["x"])
    B, S, _ = x.shape
    NT = S // 128
    if NT not in _NC_CACHE:
        _NC_CACHE[NT] = build_nc(NT)
    nc = _NC_CACHE[NT]
    maps = _host_inputs(inputs, NT)
    res = run_bass_kernel_spmd(nc, maps, core_ids=list(range(8)))
    out = np.zeros((B, NT, 128, D), np.float32)
    for c in range(8):
        b, j = c // 4, c % 4
        out[b, j::4] = np.asarray(res.results[c]["out"], dtype=np.float32).reshape(NT // 4, 128, D)
    return out.reshape(B, S, D)
```

```python
from contextlib import ExitStack
import math
import numpy as np
import concourse.bass as bass
import concourse.mybir as mybir
from concourse.bass_utils import run_bass_kernel_spmd

F32 = mybir.dt.float32
BF16 = mybir.dt.bfloat16
AF = mybir.ActivationFunctionType
ALU = mybir.AluOpType

D = 1024
NEG = -1.0e5
EPS = 1e-6


class Buf:
    def __init__(self, name=""):
        self.name = name
        self.w = None
        self.r = {}


class Tracker:
    def __init__(self, nc, stack):
        self.nc = nc
        self.stack = stack
        self.engs = {"pe": nc.tensor, "act": nc.scalar, "dve": nc.vector,
                     "pool": nc.gpsimd, "sp": nc.sync}
        self.semobj = {}
        self.cnt = {}
        for k in ["pe", "act", "dve", "pool"]:
            self.semobj[k] = stack.enter_context(nc.semaphore("s_" + k))
            self.cnt[k] = 0
        self.waited = {k: {} for k in self.engs}
        self.shared = {"c0", "c1", "c2", "c3"}
        self.nwaits = 0
        self.nops = 0

    def _wait(self, eng, ev):
        if ev is None:
            return
        key, val = ev
        if key in self.shared:
            val = self.cnt[key]
        if key == eng and eng == "pe":
            return
        if self.waited[eng].get(key, 0) >= val:
            return
        self.engs[eng].wait_ge(self.semobj[key], val)
        self.waited[eng][key] = val
        self.nwaits += 1

    def _deps(self, eng, reads, writes):
        for b in reads:
            self._wait(eng, b.w)
        for b in writes:
            self._wait(eng, b.w)
            for k, v in b.r.items():
                self._wait(eng, (k, v))

    def _record(self, ev, reads, writes):
        k, v = ev
        for b in reads:
            if b.r.get(k, 0) < v:
                b.r[k] = v
        for b in writes:
            b.w = ev
            b.r = {}

    def op(self, eng, fn, reads=(), writes=(), signal=True):
        self._deps(eng, reads, writes)
        ins = fn(self.engs[eng])
        self.nops += 1
        if signal:
            self.cnt[eng] += 1
            ins.then_inc(self.semobj[eng], 1)
            ev = (eng, self.cnt[eng])
        else:
            ev = (eng, self.cnt[eng] + 1)
        self._record(ev, reads, writes)
        return ev

    def barrier(self):
        for eng in self.engs:
            for k in list(self.semobj.keys()):
                if self.cnt[k] > 0:
                    self._wait(eng, (k, self.cnt[k]))

    def dma(self, q, fn, reads, writes, sem):
        if sem not in self.semobj:
            self.semobj[sem] = self.stack.enter_context(self.nc.semaphore("d_" + sem))
            self.cnt[sem] = 0
        self._deps(q, reads, writes)
        ins = fn(self.engs[q])
        self.nops += 1
        self.cnt[sem] += 16
        ins.then_inc(self.semobj[sem], 16)
        ev = (sem, self.cnt[sem])
        self._record(ev, reads, writes)
        return ev


def build_nc(NT, dbg=False):
    NO = NT // 4
    NGA = NT // 4
    NGO = NO // 4
    S = NT * 128
    SO = NO * 128
    nc = bass.Bass("TRN2", target_bir_lowering=False)

    def din(name, shape):
        return nc.dram_tensor(name, shape, F32, kind="ExternalInput").ap()

    xf = din("xf", [S, D])
    xo = din("xo", [SO, D])
    wkvf = din("wkvf", [D, 1032])
    wq = din("wq", [D, 512])
    wu = din("wu", [D, 512])
    wv2 = din("wv2", [D, 512])
    wg = din("wg", [D, 2048])
    wbs = din("wbs", [512, D])
    wba = din("wba", [512, D])
    wout = din("wout", [D, D])
    wup = din("wup", [D, 4096])
    wdn = din("wdn", [4096, D])
    g_pre = din("g_pre", [1, D])
    g_post = din("g_post", [1, D])
    g_fpre = din("g_fpre", [1, D])
    g_fpost = din("g_fpost", [1, D])
    g_sgu = din("g_sgu", [1, 512])
    b_sgu = din("b_sgu", [1, 512])
    b_fg = din("b_fg", [1, 8])
    wsT = din("wsT", [128, 8 * 128])
    bsp4 = din("bsp4", [128, 4 * 512])
    c_id = din("c_id", [128, 128])
    c_tri = din("c_tri", [128, 128])
    c_one = din("c_one", [128, 128])
    c_sel = din("c_sel", [128, 4])
    c_madd = din("c_madd", [128, 4 * 128])
    kscr = nc.dram_tensor("kscr", [512, S], BF16).ap()
    h1scr = nc.dram_tensor("h1scr", [SO, D], F32).ap()
    out = nc.dram_tensor("out", [SO, D], F32, kind="ExternalOutput").ap()
    dbgo = {}
    if dbg:
        dbgo["d_cum"] = nc.dram_tensor("d_cum", [128, NT * 8], F32, kind="ExternalOutput").ap()
        dbgo["d_yatt"] = nc.dram_tensor("d_yatt", [128, NO * 512], BF16, kind="ExternalOutput").ap()
        dbgo["d_h1"] = nc.dram_tensor("d_h1", [SO, D], F32, kind="ExternalOutput").ap()

    with ExitStack() as st0:
        T = Tracker(nc, st0)

        def sb(stk, name, shape, dt):
            return stk.enter_context(nc.sbuf_tensor(name, shape, dt))

        def sbr(stk, name, shape, dt, n):
            return [(sb(stk, f"{name}{i}", shape, dt), Buf(f"{name}{i}")) for i in range(n)]

        def wload(dst, src, buf, sem, kc):
            T.dma("pool", lambda e: e.dma_start(out=dst, in_=src.rearrange("(kc p) n -> p kc n", p=128)),
                  [], [buf], sem)

        wsrc = {"wu": (wu, [D, 512]), "wv2": (wv2, [D, 512]), "wg": (wg, [D, 2048]), "wbs": (wbs, [512, D]),
                "wba": (wba, [512, D]), "wout": (wout, [D, D]), "wup": (wup, [D, 4096]), "wdn": (wdn, [4096, D])}
        wscr = {k: nc.dram_tensor("scr_" + k, [128, shp[0] // 128 * shp[1]], BF16).ap() for k, (_, shp) in wsrc.items()}
        b_wscr = {k: Buf("scr_" + k) for k in wsrc}

        def precast(k):
            src, shp = wsrc[k]
            rows, n = shp
            KC = rows // 128
            if k == "wup":
                dstv = wscr[k].rearrange("p (pc kc n) -> p pc kc n", pc=8, kc=8)
                for pc in range(8):
                    T.dma("pool", lambda e, pc=pc: e.dma_start(
                        out=dstv[:, pc, :, :], in_=src[:, pc * 512:(pc + 1) * 512].rearrange("(kc p) n -> p kc n", p=128)),
                        [], [b_wscr[k]], "pc_" + k)
                return
            dstv = wscr[k].rearrange("p (kc n) -> p kc n", kc=KC)
            step = 4 if KC >= 4 else KC
            for c0 in range(0, KC, step):
                T.dma("pool", lambda e, c0=c0: e.dma_start(
                    out=dstv[:, c0:c0 + step, :],
                    in_=src[c0 * 128:(c0 + step) * 128, :].rearrange("(kc p) n -> p kc n", p=128)),
                    [], [b_wscr[k]], "pc_" + k)

        wl_n = [0]

        def wload2(dst, k, buf, nsplit=2):
            KC = dst.shape[1]
            srcv = wscr[k].rearrange("p (kc n) -> p kc n", kc=KC)
            step = max(1, KC // nsplit)
            for c0 in range(0, KC, step):
                q = "sp" if wl_n[0] % 2 == 0 else "act"
                wl_n[0] += 1
                T.dma(q, lambda e, c0=c0: e.dma_start(out=dst[:, c0:c0 + step, :], in_=srcv[:, c0:c0 + step, :]),
                      [b_wscr[k]], [buf], f"w_{k}")

        b_kscr = Buf("kscr")
        b_h1scr = Buf("h1scr")
        pf = []
        pb = []
        ps_stack = [None]
        ps_gen = [0]

        def psum_std():
            stk = ExitStack()
            ps_stack[0] = stk
            k = ps_gen[0]
            ps_gen[0] += 1
            pf[:] = [(stk.enter_context(nc.psum_tensor(f"pf{k}_{i}", [128, 512], F32)), Buf(f"pf{i}")) for i in range(6)]
            pb[:] = [(stk.enter_context(nc.psum_tensor(f"pb{k}_{i}", [128, 1024], BF16)), Buf(f"pb{i}")) for i in range(2)]

        def psum_free():
            ps_stack[0].close()
            ps_stack[0] = None

        psum_std()

        identb = sb(st0, "identb", [128, 128], BF16); b_identb = Buf()
        triU = sb(st0, "triU", [128, 128], F32); b_triU = Buf()
        onesf = sb(st0, "onesf", [128, 128], F32); b_onesf = Buf()
        gbc = sb(st0, "gbc", [128, D], F32); b_gbc = Buf()
        bfbc = sb(st0, "bfbc", [128, 8], F32); b_bfbc = Buf()
        sel = sb(st0, "sel", [128, 4], F32); b_sel = Buf()
        madd = sb(st0, "madd", [128, 4, 128], F32); b_madd = Buf()
        cumsp = sb(st0, "cumsp", [128, NT, 8], F32); b_cumsp = Buf()
        carry = sb(st0, "carry", [128, 8], F32); b_carry = Buf()
        junk = sb(st0, "junk", [128, D], BF16); b_junk = Buf()
        stat = sbr(st0, "stat", [128, 8], F32, 4)

        T.dma("pool", lambda e: e.dma_start(out=identb[:], in_=c_id), [], [b_identb], "c0")
        T.dma("sp", lambda e: e.dma_start(out=triU[:], in_=c_tri), [], [b_triU], "c1")
        T.dma("sp", lambda e: e.dma_start(out=onesf[:], in_=c_one), [], [b_onesf], "c1")
        T.dma("sp", lambda e: e.dma_start(out=gbc[:], in_=g_pre.partition_broadcast(128)), [], [b_gbc], "c1")
        T.dma("sp", lambda e: e.dma_start(out=bfbc[:], in_=b_fg.partition_broadcast(128)), [], [b_bfbc], "c1")
        T.dma("sp", lambda e: e.dma_start(out=sel[:], in_=c_sel), [], [b_sel], "c1")
        T.dma("sp", lambda e: e.dma_start(out=madd[:], in_=c_madd.rearrange("p (m t) -> p m t", m=4)),
              [], [b_madd], "c1")
        T.op("dve", lambda e: e.memset(carry[:], 0.0), [], [b_carry])

        fr_n = [0]

        def rms_scale(x_ap, b_x, xs_ap, b_xs, eng, pre_scale=1.0, extra_bias=0.0):
            s, b_s = stat[fr_n[0] % 4]
            fr_n[0] += 1
            T.op("dve", lambda e: e.memset(s[:, 0:1], 0.0), [], [b_s])
            T.op("act", lambda e: e.activation(out=junk[:], in_=x_ap, func=AF.Square, scale=pre_scale,
                                               accum_out=s[:, 0:1]), [b_x], [b_junk, b_s])
            T.op("act", lambda e: e.activation(out=s[:, 1:2], in_=s[:, 0:1], func=AF.Ln, scale=1.0 / D, bias=EPS),
                 [b_s], [b_s])
            T.op("act", lambda e: e.activation(out=s[:, 2:3], in_=s[:, 1:2], func=AF.Exp, scale=-0.5,
                                               bias=extra_bias), [b_s], [b_s])
            T.op("dve", lambda e: e.scalar_tensor_tensor(out=xs_ap, in0=x_ap, scalar=s[:, 2:3], in1=gbc[:],
                                                       op0=ALU.mult, op1=ALU.mult), [b_x, b_s, b_gbc], [b_xs])

        def transpose8(xs_t, b_xs, dst_ap, b_dst, pbi, evac_eng, n=8):
            pbt, b_pb = pb[pbi]
            for c in range(n):
                T.op("pe", lambda e, c=c: e.transpose(out=pbt[:, c * 128:(c + 1) * 128],
                                                      in_=xs_t[:, c * 128:(c + 1) * 128], identity=identb[:]),
                     [b_xs, b_identb], [b_pb], signal=(c == n - 1))
            src = pbt[:, 0:n * 128].rearrange("p (c t) -> p c t", c=n)
            if evac_eng == "act":
                T.op("act", lambda e: e.copy(out=dst_ap, in_=src), [b_pb], [b_dst])
            else:
                T.op("dve", lambda e: e.tensor_copy(out=dst_ap, in_=src), [b_pb], [b_dst])

        with ExitStack() as stCD:
            yatt = sb(stCD, "yatt", [128, NO, 512], BF16); b_yatt = Buf()

            with ExitStack() as stABC:
                Vall = sb(stABC, "Vall", [128, NT, 8, 65], BF16); b_Vall = Buf()
                QT = sb(stABC, "QT", [67, 8, SO], BF16); b_QT = Buf()
                T.op("pool", lambda e: e.memset(Vall[:, :, :, 64:65], 1.0), [], [b_Vall])

                with ExitStack() as stAB:
                    Wkvf = sb(stAB, "Wkvf", [128, 8, 1032], BF16); b_Wkvf = Buf()
                    Wq = sb(stAB, "Wq", [128, 8, 512], BF16); b_Wq = Buf()
                    wload(Wkvf[:], wkvf, b_Wkvf, "w_kvf", 8)
                    wload(Wq[:], wq, b_Wq, "w_q", 8)
                    xa = sbr(stAB, "xa", [128, D], F32, 6)
                    xs = sbr(stAB, "xs", [128, D], BF16, 4)
                    xsT = sbr(stAB, "xsT", [128, 8, 512], BF16, 2)
                    kst = sbr(stAB, "kst", [128, 4, 512], BF16, 2)
                    ftmp = sbr(stAB, "ftmp", [128, 32], F32, 2)
                    spt = sbr(stAB, "spt", [128, 32], F32, 2)
                    CS = sbr(stAB, "CS", [128, 8, 67], BF16, 2)
                    co = sbr(stAB, "co", [128, 8], F32, 2)
                    co2 = sbr(stAB, "co2", [128, 8], F32, 2)
                    for c_, b_ in CS:
                        T.op("pool", lambda e, c_=c_: e.memset(c_[:], 0.0), [], [b_])

                    tiles = [("A", t) for t in range(NT)] + [("B", t) for t in range(NO)]
                    NTT = len(tiles)

                    XR = 6
                    GT = NGA + NGO
                    fstat = {}

                    def loadG(g):
                        for l in range(4):
                            i = 4 * g + l
                            kind, t = tiles[i]
                            src = xf if kind == "A" else xo
                            xt, b_x = xa[i % XR]
                            T.dma("sp", lambda e, xt=xt, src=src, t=t: e.dma_start(out=xt[:], in_=src[t * 128:(t + 1) * 128, :]),
                                  [], [b_x], f"xa{i % XR}")

                    def frontG(g):
                        s_, b_s = stat[fr_n[0] % 4]
                        fr_n[0] += 1
                        T.op("dve", lambda e: e.memset(s_[:, 0:4], 0.0), [], [b_s])
                        for l in range(4):
                            xt, b_x = xa[(4 * g + l) % XR]
                            T.op("act", lambda e, xt=xt, l=l: e.activation(out=junk[:], in_=xt[:], func=AF.Square,
                                                                           accum_out=s_[:, l:l + 1]), [b_x], [b_junk, b_s])
                        T.op("act", lambda e: e.activation(out=s_[:, 4:8], in_=s_[:, 0:4], func=AF.Ln, scale=1.0 / D, bias=EPS),
                             [b_s], [b_s])
                        T.op("act", lambda e: e.activation(out=s_[:, 4:8], in_=s_[:, 4:8], func=AF.Exp, scale=-0.5), [b_s], [b_s])
                        fstat[g] = (s_, b_s)

                    def frontG2(g):
                        s_, b_s = fstat[g]
                        for l in range(4):
                            xt, b_x = xa[(4 * g + l) % XR]
                            xst, b_xs = xs[l]
                            T.op("dve", lambda e, xt=xt, xst=xst, l=l: e.scalar_tensor_tensor(
                                out=xst[:], in0=xt[:], scalar=s_[:, 4 + l:5 + l], in1=gbc[:], op0=ALU.mult, op1=ALU.mult),
                                [b_x, b_s, b_gbc], [b_xs])

                    def transG(g):
                        dT, b_dT = xsT[g % 2]
                        for l in range(4):
                            xst, b_xs = xs[l]
                            transpose8(xst, b_xs, dT[:, :, l * 128:(l + 1) * 128], b_dT, l % 2,
                                       "act" if l % 2 == 0 else "dve")

                    def backA(g):
                        dT, b_dT = xsT[g % 2]
                        ks, b_ks = kst[g % 2]
                        for hp in range(4):
                            pk, b_pk = pf[hp % 2]
                            for kc in range(8):
                                T.op("pe", lambda e, kc=kc, hp=hp, pk=pk: e.matmul(
                                    pk[:], lhsT=Wkvf[:, kc, hp * 128:(hp + 1) * 128], rhs=dT[:, kc, :],
                                    start=(kc == 0), stop=(kc == 7)), [b_Wkvf, b_dT], [b_pk], signal=(kc == 7))
                            T.op("dve", lambda e, hp=hp, pk=pk: e.tensor_copy(out=ks[:, hp, :], in_=pk[:]),
                                 [b_pk], [b_ks])
                        T.dma("sp", lambda e: e.dma_start(
                            out=kscr.rearrange("(hp p) s -> p hp s", p=128)[:, :, g * 512:(g + 1) * 512],
                            in_=ks[:]), [b_ks], [b_kscr], f"kst{g % 2}")

                    def backA2(g):
                        dT, b_dT = xsT[g % 2]
                        for l in range(4):
                            pv, b_pv = pf[2 + l % 2]
                            for kc in range(8):
                                T.op("pe", lambda e, kc=kc, l=l, pv=pv: e.matmul(
                                    pv[:], lhsT=dT[:, kc, l * 128:(l + 1) * 128], rhs=Wkvf[:, kc, 512:1024],
                                    start=(kc == 0), stop=(kc == 7)), [b_Wkvf, b_dT], [b_pv], signal=(kc == 7))
                            T.op("act", lambda e, l=l, pv=pv: e.copy(
                                out=Vall[:, 4 * g + l, :, 0:64], in_=pv[:].rearrange("p (h d) -> p h d", h=8)),
                                [b_pv], [b_Vall])
                        pF, b_pF = pf[4]
                        for l in range(4):
                            for kc in range(8):
                                T.op("pe", lambda e, kc=kc, l=l: e.matmul(
                                    pF[:, l * 8:(l + 1) * 8], lhsT=dT[:, kc, l * 128:(l + 1) * 128],
                                    rhs=Wkvf[:, kc, 1024:1032], start=(kc == 0), stop=(kc == 7)),
                                    [b_Wkvf, b_dT], [b_pF], signal=(kc == 7 and l == 3))
                        ft, b_ft = ftmp[g % 2]
                        sp_, b_sp = spt[g % 2]
                        T.op("dve", lambda e: e.tensor_tensor(
                            out=ft[:].rearrange("p (l h) -> p l h", l=4), in0=pF[:, 0:32].rearrange("p (l h) -> p l h", l=4),
                            in1=bfbc[:].unsqueeze(1).broadcast_to([128, 4, 8]), op=ALU.add), [b_pF, b_bfbc], [b_ft])
                        T.op("act", lambda e: e.activation(out=ft[:], in_=ft[:], func=AF.Exp, scale=-1.0), [b_ft], [b_ft])
                        T.op("act", lambda e: e.activation(out=sp_[:], in_=ft[:], func=AF.Ln, bias=1.0), [b_ft], [b_sp])

                    def backA3(g):
                        sp_, b_sp = spt[g % 2]
                        pC, b_pC = pf[5]
                        for l in range(4):
                            for l2 in range(l):
                                T.op("pe", lambda e, l=l, l2=l2: e.matmul(
                                    pC[:, l * 8:(l + 1) * 8], lhsT=onesf[:], rhs=sp_[:, l2 * 8:(l2 + 1) * 8],
                                    start=(l2 == 0), stop=False), [b_onesf, b_sp], [b_pC], signal=False)
                            T.op("pe", lambda e, l=l: e.matmul(
                                pC[:, l * 8:(l + 1) * 8], lhsT=triU[:], rhs=sp_[:, l * 8:(l + 1) * 8],
                                start=(l == 0), stop=True), [b_triU, b_sp], [b_pC], signal=False)
                        for l2 in range(4):
                            T.op("pe", lambda e, l2=l2: e.matmul(
                                pC[:, 32:40], lhsT=onesf[:], rhs=sp_[:, l2 * 8:(l2 + 1) * 8],
                                start=(l2 == 0), stop=(l2 == 3)), [b_onesf, b_sp], [b_pC], signal=(l2 == 3))
                        T.op("dve", lambda e: e.tensor_tensor(
                            out=cumsp[:, 4 * g:4 * g + 4, :], in0=pC[:, 0:32].rearrange("p (l h) -> p l h", l=4),
                            in1=carry[:].unsqueeze(1).broadcast_to([128, 4, 8]), op=ALU.add),
                            [b_pC, b_carry], [b_cumsp])
                        T.op("dve", lambda e: e.tensor_tensor(out=carry[:], in0=carry[:], in1=pC[:, 32:40], op=ALU.add),
                             [b_pC, b_carry], [b_carry])

                    def backB(go):
                        g = NGA + go
                        dT, b_dT = xsT[g % 2]
                        for h in range(8):
                            pq, b_pq = pf[h % 2]
                            for kc in range(8):
                                T.op("pe", lambda e, kc=kc, h=h, pq=pq: e.matmul(
                                    pq[0:64, :], lhsT=Wq[:, kc, h * 64:(h + 1) * 64], rhs=dT[:, kc, :],
                                    start=(kc == 0), stop=(kc == 7)), [b_Wq, b_dT], [b_pq], signal=(kc == 7))
                            if h % 2 == 0:
                                T.op("act", lambda e, h=h, pq=pq: e.copy(
                                    out=QT[0:64, h, go * 512:(go + 1) * 512], in_=pq[0:64, :]), [b_pq], [b_QT])
                            else:
                                T.op("dve", lambda e, h=h, pq=pq: e.tensor_copy(
                                    out=QT[0:64, h, go * 512:(go + 1) * 512], in_=pq[0:64, :]), [b_pq], [b_QT])
                        for l in range(4):
                            i = 4 * go + l
                            c1, b_c1 = co[i % 2]
                            c2, b_c2 = co2[i % 2]
                            cs, b_cs = CS[i % 2]
                            T.op("dve", lambda e, i=i, c1=c1: e.tensor_scalar(
                                out=c1[:], in0=cumsp[:, 4 * i, :], scalar1=sel[:, 0:1], scalar2=None, op0=ALU.mult),
                                [b_cumsp, b_sel], [b_c1])
                            for m in range(1, 4):
                                T.op("dve", lambda e, i=i, m=m, c1=c1: e.scalar_tensor_tensor(
                                    out=c1[:], in0=cumsp[:, 4 * i + m, :], scalar=sel[:, m:m + 1], in1=c1[:],
                                    op0=ALU.mult, op1=ALU.add), [b_cumsp, b_sel, b_c1], [b_c1])
                            T.op("dve", lambda e, c1=c1: e.tensor_scalar(
                                out=c1[:], in0=c1[:], scalar1=-8.0, scalar2=None, op0=ALU.mult), [b_c1], [b_c1])
                            T.op("dve", lambda e, c1=c1, cs=cs: e.tensor_copy(out=cs[:, :, 64], in_=c1[:]), [b_c1], [b_cs])
                            T.op("dve", lambda e, c1=c1, c2=c2, cs=cs: e.tensor_tensor(
                                out=c2[:], in0=c1[:], in1=cs[:, :, 64], op=ALU.subtract), [b_c1, b_cs], [b_c2])
                            T.op("dve", lambda e, c2=c2, cs=cs: e.tensor_copy(out=cs[:, :, 65], in_=c2[:]), [b_c2], [b_cs])
                            T.op("dve", lambda e, c1=c1, c2=c2, cs=cs: e.tensor_tensor(
                                out=c1[:], in0=c2[:], in1=cs[:, :, 65], op=ALU.subtract), [b_c2, b_cs], [b_c1])
                            T.op("dve", lambda e, c1=c1, cs=cs: e.tensor_copy(out=cs[:, :, 66], in_=c1[:]), [b_c1], [b_cs])
                            for half in range(2):
                                pa, b_pa = pf[2 + half]
                                for hh in range(4):
                                    h = half * 4 + hh
                                    T.op("pe", lambda e, h=h, hh=hh, pa=pa, cs=cs: e.matmul(
                                        pa[0:67, hh * 128:(hh + 1) * 128], lhsT=cs[:, h, :], rhs=identb[:],
                                        start=True, stop=True), [b_cs, b_identb], [b_pa], signal=(hh == 3))
                                T.op("act", lambda e, half=half, pa=pa, i=i: e.copy(
                                    out=QT[64:67, half * 4:half * 4 + 4, i * 128:(i + 1) * 128],
                                    in_=pa[64:67, :].rearrange("p (h t) -> p h t", h=4)), [b_pa], [b_QT])

                    loadG(0)
                    frontG(0)
                    frontG2(0)
                    if GT > 1:
                        loadG(1)
                    transG(0)
                    for g in range(GT):
                        if g + 1 < GT:
                            frontG(g + 1)
                        if g < NGA:
                            backA(g)
                        else:
                            backB(g - NGA)
                        if g + 1 < GT:
                            frontG2(g + 1)
                        if g + 2 < GT:
                            loadG(g + 2)
                        if g < NGA:
                            backA2(g)
                        if g + 1 < GT:
                            transG(g + 1)
                        if g < NGA:
                            backA3(g)
                    if dbg:
                        T.dma("sp", lambda e: e.dma_start(out=dbgo["d_cum"], in_=cumsp[:].rearrange("p t h -> p (t h)")),
                              [b_cumsp], [], "dbg")

                T.barrier()
                psum_free()
                with ExitStack() as stC:
                    pS = [(stC.enter_context(nc.psum_tensor(f"pS{i}", [128, 512], F32)), Buf(f"pS{i}")) for i in range(4)]
                    pO = [(stC.enter_context(nc.psum_tensor(f"pO{i}", [128, 512], F32)), Buf(f"pO{i}")) for i in range(4)]
                    KT = sbr(stC, "KT", [67, S], BF16, 2)
                    PT = sbr(stC, "PT", [128, 512], BF16, 4)
                    rec = sbr(stC, "rec", [128, 4], F32, 4)
                    for kt_, b_ in KT:
                        T.op("pool", lambda e, kt_=kt_: e.memset(kt_[64:67, :], 1.0), [], [b_])
                    for k_ in ("wu", "wv2", "wg", "wbs", "wba", "wout", "wup", "wdn"):
                        precast(k_)

                    def loadK(h):
                        kt_, b_ = KT[h % 2]
                        T.dma("sp", lambda e: e.dma_start(out=kt_[0:64, :], in_=kscr[h * 64:(h + 1) * 64, :]),
                              [b_kscr], [b_], f"kt{h % 2}")

                    gsets = [[a] for a in range(NGO)]
                    items = []
                    for h in range(8):
                        for si, gs in enumerate(gsets):
                            for kt in range(16 * gs[-1] + 16):
                                items.append((h, si, kt))

                    def colstart(go, kt):
                        if kt < 16 * go:
                            return 0, None
                        l = kt // 4 - 4 * go
                        return l * 128, kt % 4

                    def active(si, kt):
                        r = []
                        for slot, go in enumerate(gsets[si]):
                            if kt <= 16 * go + 15:
                                c0, m = colstart(go, kt)
                                r.append((slot, go, c0, m))
                        return r

                    def emitS(n):
                        h, si, kt = items[n]
                        kt_, b_k = KT[h % 2]
                        ps_, b_ps = pS[n % 4]
                        for slot, go, c0, m in active(si, kt):
                            T.op("pe", lambda e, slot=slot, go=go, c0=c0: e.matmul(
                                ps_[:, slot * 512 + c0:(slot + 1) * 512], lhsT=kt_[0:67, kt * 128:(kt + 1) * 128],
                                rhs=QT[0:67, h, go * 512 + c0:(go + 1) * 512], start=True, stop=True),
                                [b_k, b_QT], [b_ps])

                    def emitE(n):
                        h, si, kt = items[n]
                        ps_, b_ps = pS[n % 4]
                        pt_, b_pt = PT[n % 4]
                        act = active(si, kt)
                        for slot, go, c0, m in act:
                            if m is not None:
                                lo = slot * 512 + c0
                                T.op("dve", lambda e, lo=lo, m=m: e.tensor_tensor(
                                    out=ps_[:, lo:lo + 128], in0=ps_[:, lo:lo + 128], in1=madd[:, m, :], op=ALU.add),
                                    [b_ps, b_madd], [b_ps])
                        lo = act[0][0] * 512 + act[0][2]
                        hi = (act[-1][0] + 1) * 512
                        T.op("act", lambda e: e.activation(
                            out=pt_[:, lo:hi], in_=ps_[:, lo:hi], func=AF.Exp, scale=0.125,
                            bias=cumsp[:, kt, h:h + 1]), [b_ps, b_cumsp], [b_pt])

                    def emitPV(n):
                        h, si, kt = items[n]
                        pt_, b_pt = PT[n % 4]
                        ring = 2 * ((h * len(gsets) + si) % 2)
                        for slot, go, c0, m in active(si, kt):
                            po, b_po = pO[ring + slot]
                            l0 = c0 // 128
                            for l in range(l0, 4):
                                last = 4 * (4 * go + l) + 3
                                T.op("pe", lambda e, l=l, last=last, slot=slot, po=po: e.matmul(
                                    po[:, l * 65:(l + 1) * 65], lhsT=pt_[:, slot * 512 + l * 128:slot * 512 + (l + 1) * 128],
                                    rhs=Vall[:, kt, h, :], start=(kt == 0 and l == 0), stop=(kt == last),
                                    skip_group_check=True),
                                    [b_pt, b_Vall], [b_po], signal=(l == 3))
                            if kt == 16 * go + 15:
                                rc, b_rc = rec[ring + slot]
                                pov = po[:, 0:260].rearrange("p (l d) -> p l d", l=4)
                                T.op("dve", lambda e, rc=rc, pov=pov: e.reciprocal(out=rc[:], in_=pov[:, :, 64]), [b_po], [b_rc])
                                for l in range(4):
                                    T.op("dve", lambda e, l=l, rc=rc, pov=pov, go=go: e.tensor_scalar(
                                        out=yatt[:, 4 * go + l, h * 64:(h + 1) * 64], in0=pov[:, l, 0:64],
                                        scalar1=rc[:, l:l + 1], scalar2=None, op0=ALU.mult), [b_po, b_rc], [b_yatt])

                    loadK(0)
                    NI = len(items)
                    per_h = NI // 8
                    for n in range(NI + 3):
                        if n < NI:
                            if n % per_h == 0:
                                hh = n // per_h
                                if hh + 1 < 8:
                                    loadK(hh + 1)
                            emitS(n)
                        if 2 <= n < NI + 2:
                            emitE(n - 2)
                        if n >= 3:
                            emitPV(n - 3)
                    if dbg:
                        T.dma("sp", lambda e: e.dma_start(out=dbgo["d_yatt"], in_=yatt[:].rearrange("p t c -> p (t c)")),
                              [b_yatt], [], "dbg")
                T.barrier()
                psum_std()

            T.barrier()
            with ExitStack() as stD:
                Wu = sb(stD, "Wu", [128, 8, 512], BF16); b_Wu = Buf()
                Wv2 = sb(stD, "Wv2", [128, 8, 512], BF16); b_Wv2 = Buf()
                Wg = sb(stD, "Wg", [128, 8, 2048], BF16); b_Wg = Buf()
                Wbs = sb(stD, "Wbs", [128, 4, D], BF16); b_Wbs = Buf()
                Wba = sb(stD, "Wba", [128, 4, D], BF16); b_Wba = Buf()
                Wout = sb(stD, "Wout", [128, 8, D], BF16); b_Wout = Buf()
                wload2(Wu[:], "wu", b_Wu)
                wload2(Wv2[:], "wv2", b_Wv2)
                gpost = sb(stD, "gpost", [128, D], F32); b_gpost = Buf()
                gsg = sb(stD, "gsg", [128, 512], F32); b_gsg = Buf()
                bsg = sb(stD, "bsg", [128, 512], F32); b_bsg = Buf()
                bsp = sb(stD, "bsp", [128, 4, 128], F32); b_bsp = Buf()
                wsTm = sb(stD, "wsTm", [128, 8, 128], BF16); b_wsTm = Buf()
                T.dma("sp", lambda e: e.dma_start(out=gpost[:], in_=g_post.partition_broadcast(128)), [], [b_gpost], "c2")
                T.dma("sp", lambda e: e.dma_start(out=gsg[:], in_=g_sgu.partition_broadcast(128)), [], [b_gsg], "c2")
                T.dma("sp", lambda e: e.dma_start(out=bsg[:], in_=b_sgu.partition_broadcast(128)), [], [b_bsg], "c2")
                T.dma("sp", lambda e: e.dma_start(out=bsp[:], in_=bsp4.rearrange("p (k t) -> p k t", k=4)[:, :, 0:128]), [], [b_bsp], "c2")
                h1t = sbr(stD, "h1t", [128, D], F32, 2)
                T.dma("sp", lambda e: e.dma_start(out=h1t[0][0][:], in_=wsT), [], [h1t[0][1]], "c2")
                T.op("dve", lambda e: e.tensor_tensor(out=wsTm[:], in0=h1t[0][0][:].rearrange("p (g t) -> p g t", g=8),
                                                      in1=triU[:].unsqueeze(1).broadcast_to([128, 8, 128]), op=ALU.mult),
                     [h1t[0][1], b_triU], [b_wsTm])

                xg = sbr(stD, "xg", [128, D], F32, 2)
                xsd = sbr(stD, "xsd", [128, D], BF16, 2)
                xsTg = sbr(stD, "xsTg", [128, 8, 512], BF16, 2)
                uT = sbr(stD, "uT", [128, 4, 512], BF16, 1)
                vg = sbr(stD, "vg", [128, 512], F32, 4)
                vnb = sbr(stD, "vnb", [128, 512], BF16, 4)
                lnst = sbr(stD, "lnst", [128, 16], F32, 2)
                t1 = sbr(stD, "t1", [128, 512], F32, 1)
                ysT = sbr(stD, "ysT", [128, 4, 512], BF16, 2)
                yaT = sbr(stD, "yaT", [128, 4, 512], BF16, 2)
                tg1 = sbr(stD, "tg1", [128, 512], F32, 2)
                tg2 = sbr(stD, "tg2", [128, 512], F32, 2)
                mT = sbr(stD, "mT", [128, 8, 512], BF16, 1)
                bank_n = [0]

                def bank():
                    r = pf[bank_n[0] % 6]
                    bank_n[0] += 1
                    return r

                def rstd_batch(s, b_s, n, src0, dst0, scale, extra_bias=0.0):
                    T.op("act", lambda e: e.activation(out=s[:, dst0:dst0 + n], in_=s[:, src0:src0 + n], func=AF.Ln,
                                                       scale=scale, bias=EPS), [b_s], [b_s])
                    T.op("act", lambda e: e.activation(out=s[:, dst0:dst0 + n], in_=s[:, dst0:dst0 + n], func=AF.Exp,
                                                       scale=-0.5, bias=extra_bias), [b_s], [b_s])

                def genX(go):
                    dT, b_dT = xsTg[go % 2]
                    for pr in range(2):
                        s, b_s = stat[fr_n[0] % 4]
                        fr_n[0] += 1
                        T.op("pool", lambda e: e.memset(s[:, 0:2], 0.0), [], [b_s])
                        for q in range(2):
                            l = 2 * pr + q
                            i = 4 * go + l
                            xt, b_x = xg[q]
                            T.dma("sp", lambda e, xt=xt, i=i: e.dma_start(out=xt[:], in_=xo[i * 128:(i + 1) * 128, :]),
                                  [], [b_x], f"xg{q}")
                            T.op("act", lambda e, xt=xt, q=q: e.activation(out=junk[:], in_=xt[:], func=AF.Square,
                                                                           accum_out=s[:, q:q + 1]), [b_x], [b_junk, b_s])
                        rstd_batch(s, b_s, 2, 0, 2, 1.0 / D)
                        for q in range(2):
                            l = 2 * pr + q
                            xt, b_x = xg[q]
                            xst, b_xs = xsd[q]
                            T.op("dve", lambda e, xt=xt, xst=xst, q=q: e.scalar_tensor_tensor(
                                out=xst[:], in0=xt[:], scalar=s[:, 2 + q:3 + q], in1=gbc[:], op0=ALU.mult, op1=ALU.mult),
                                [b_x, b_s, b_gbc], [b_xs])
                            transpose8(xst, b_xs, dT[:, :, l * 128:(l + 1) * 128], b_dT, q, "act" if q == 0 else "dve")
                        yield
                    ut, b_ut = uT[0]
                    for c in range(4):
                        pu, b_pu = bank()
                        for kc in range(8):
                            T.op("pe", lambda e, kc=kc, c=c, pu=pu: e.matmul(
                                pu[:], lhsT=Wu[:, kc, c * 128:(c + 1) * 128], rhs=dT[:, kc, :],
                                start=(kc == 0), stop=(kc == 7)), [b_Wu, b_dT], [b_pu], signal=(kc == 7))
                        T.op("act", lambda e, c=c, pu=pu: e.activation(out=ut[:, c, :], in_=pu[:], func=AF.Gelu_apprx_tanh),
                             [b_pu], [b_ut])
                        yield
                    ls, b_ls = lnst[go % 2]
                    T.op("pool", lambda e: e.memset(ls[:], 0.0), [], [b_ls])
                    for l in range(4):
                        pv, b_pv = bank()
                        for kc in range(8):
                            T.op("pe", lambda e, kc=kc, l=l, pv=pv: e.matmul(
                                pv[:], lhsT=dT[:, kc, l * 128:(l + 1) * 128], rhs=Wv2[:, kc, :],
                                start=(kc == 0), stop=(kc == 7)), [b_Wv2, b_dT], [b_pv], signal=(kc == 7))
                        v_, b_v = vg[l]
                        T.op("act", lambda e, l=l, v_=v_, pv=pv: e.activation(
                            out=v_[:], in_=pv[:], func=AF.Gelu_apprx_tanh, accum_out=ls[:, l:l + 1]), [b_pv], [b_v, b_ls])
                        T.op("act", lambda e, l=l, v_=v_: e.activation(out=junk[:, 0:512], in_=v_[:], func=AF.Square,
                                                                       accum_out=ls[:, 4 + l:5 + l]), [b_v], [b_junk, b_ls])
                        yield
                    T.op("dve", lambda e: e.tensor_scalar(out=ls[:, 0:4], in0=ls[:, 0:4], scalar1=1.0 / 512, scalar2=None,
                                                          op0=ALU.mult), [b_ls], [b_ls])
                    T.op("dve", lambda e: e.tensor_tensor(out=ls[:, 8:12], in0=ls[:, 0:4], in1=ls[:, 0:4], op=ALU.mult),
                         [b_ls], [b_ls])
                    T.op("dve", lambda e: e.scalar_tensor_tensor(out=ls[:, 4:8], in0=ls[:, 4:8], scalar=1.0 / 512,
                                                                 in1=ls[:, 8:12], op0=ALU.mult, op1=ALU.subtract),
                         [b_ls], [b_ls])
                    rstd_batch(ls, b_ls, 4, 4, 12, 1.0)
                    for l in range(4):
                        v_, b_v = vg[l]
                        vb, b_vb = vnb[l]
                        T.op("dve", lambda e, l=l, v_=v_: e.tensor_scalar(
                            out=v_[:], in0=v_[:], scalar1=ls[:, l:l + 1], scalar2=ls[:, 12 + l:13 + l],
                            op0=ALU.subtract, op1=ALU.mult), [b_v, b_ls], [b_v])
                        T.op("dve", lambda e, v_=v_: e.tensor_tensor(out=v_[:], in0=v_[:], in1=gsg[:], op=ALU.mult),
                             [b_v, b_gsg], [b_v])
                        T.op("pool", lambda e, v_=v_, vb=vb: e.tensor_tensor(out=vb[:], in0=v_[:], in1=bsg[:], op=ALU.add),
                             [b_v, b_bsg], [b_vb])
                    yield
                    ys, b_ys = ysT[go % 2]
                    for k in range(4):
                        pA, b_pA = bank()
                        pB, b_pB = bank()
                        for l in range(4):
                            vb, b_vb = vnb[l]
                            T.op("pe", lambda e, l=l, k=k, vb=vb, pA=pA: e.matmul(
                                pA[:, l * 128:(l + 1) * 128], lhsT=vb[:, k * 128:(k + 1) * 128], rhs=wsTm[:, 2 * k, :],
                                start=True, stop=True), [b_vb, b_wsTm], [b_pA], signal=(l == 3))
                        for l in range(4):
                            vb, b_vb = vnb[l]
                            T.op("pe", lambda e, l=l, k=k, vb=vb, pB=pB: e.matmul(
                                pB[:, l * 128:(l + 1) * 128], lhsT=vb[:, k * 128:(k + 1) * 128], rhs=wsTm[:, 2 * k + 1, :],
                                start=True, stop=True), [b_vb, b_wsTm], [b_pB], signal=(l == 3))
                        tt, b_tt = t1[0]
                        T.op("dve", lambda e, k=k, tt=tt, pA=pA: e.tensor_tensor(
                            out=tt[0:64, :].rearrange("p (l t) -> p l t", l=4), in0=pA[0:64, :].rearrange("p (l t) -> p l t", l=4),
                            in1=bsp[0:64, k, :].unsqueeze(1).broadcast_to([64, 4, 128]), op=ALU.add), [b_pA, b_bsp], [b_tt])
                        T.op("dve", lambda e, k=k, tt=tt, pB=pB: e.tensor_tensor(
                            out=tt[64:128, :].rearrange("p (l t) -> p l t", l=4), in0=pB[64:128, :].rearrange("p (l t) -> p l t", l=4),
                            in1=bsp[64:128, k, :].unsqueeze(1).broadcast_to([64, 4, 128]), op=ALU.add), [b_pB, b_bsp], [b_tt])
                        T.op("pool", lambda e, k=k, tt=tt: e.tensor_tensor(
                            out=ys[:, k, :], in0=tt[:], in1=ut[:, k, :], op=ALU.mult), [b_tt, b_ut], [b_ys])
                        yield
                    ya, b_ya = yaT[go % 2]
                    for l in range(4):
                        i = 4 * go + l
                        transpose8(yatt[:, i, :], b_yatt, ya[:, :, l * 128:(l + 1) * 128], b_ya, l % 2,
                                   "act" if l % 2 == 0 else "dve", n=4)
                        yield

                def genY(go):
                    dT, b_dT = xsTg[go % 2]
                    ys, b_ys = ysT[go % 2]
                    ya, b_ya = yaT[go % 2]
                    mt, b_mt = mT[0]
                    for oc in range(8):
                        pG1, b_pG1 = bank()
                        pG2, b_pG2 = bank()
                        pP1, b_pP1 = bank()
                        pP2, b_pP2 = bank()
                        for kc in range(8):
                            T.op("pe", lambda e, kc=kc, oc=oc, pG1=pG1: e.matmul(
                                pG1[:], lhsT=Wg[:, kc, oc * 128:(oc + 1) * 128], rhs=dT[:, kc, :],
                                start=(kc == 0), stop=(kc == 7)), [b_Wg, b_dT], [b_pG1], signal=(kc == 7))
                        for kc in range(8):
                            T.op("pe", lambda e, kc=kc, oc=oc, pG2=pG2: e.matmul(
                                pG2[:], lhsT=Wg[:, kc, 1024 + oc * 128:1024 + (oc + 1) * 128], rhs=dT[:, kc, :],
                                start=(kc == 0), stop=(kc == 7)), [b_Wg, b_dT], [b_pG2], signal=(kc == 7))
                        for c in range(4):
                            T.op("pe", lambda e, c=c, oc=oc, pP1=pP1: e.matmul(
                                pP1[:], lhsT=Wbs[:, c, oc * 128:(oc + 1) * 128], rhs=ys[:, c, :],
                                start=(c == 0), stop=(c == 3)), [b_Wbs, b_ys], [b_pP1], signal=(c == 3))
                        for c in range(4):
                            T.op("pe", lambda e, c=c, oc=oc, pP2=pP2: e.matmul(
                                pP2[:], lhsT=Wba[:, c, oc * 128:(oc + 1) * 128], rhs=ya[:, c, :],
                                start=(c == 0), stop=(c == 3)), [b_Wba, b_ya], [b_pP2], signal=(c == 3))
                        a1, b_a1 = tg1[oc % 2]
                        a2, b_a2 = tg2[oc % 2]
                        T.op("act", lambda e, a1=a1, pG1=pG1: e.activation(out=a1[:], in_=pG1[:], func=AF.Tanh, scale=0.5),
                             [b_pG1], [b_a1])
                        T.op("act", lambda e, a2=a2, pG2=pG2: e.activation(out=a2[:], in_=pG2[:], func=AF.Tanh, scale=0.5),
                             [b_pG2], [b_a2])
                        T.op("dve", lambda e, a1=a1, pP1=pP1: e.scalar_tensor_tensor(
                            out=a1[:], in0=a1[:], scalar=1.0, in1=pP1[:], op0=ALU.add, op1=ALU.mult),
                            [b_a1, b_pP1], [b_a1])
                        T.op("dve", lambda e, a2=a2, pP2=pP2: e.scalar_tensor_tensor(
                            out=a2[:], in0=a2[:], scalar=1.0, in1=pP2[:], op0=ALU.add, op1=ALU.mult),
                            [b_a2, b_pP2], [b_a2])
                        T.op("pool", lambda e, a1=a1, a2=a2, oc=oc: e.tensor_tensor(
                            out=mt[:, oc, :], in0=a1[:], in1=a2[:], op=ALU.add), [b_a1, b_a2], [b_mt])
                        yield
                    for l in range(4):
                        i = 4 * go + l
                        s, b_s = stat[fr_n[0] % 4]
                        fr_n[0] += 1
                        T.op("pool", lambda e, s=s: e.memset(s[:, 0:2], 0.0), [], [b_s])
                        banks = []
                        for half in range(2):
                            pH, b_pH = bank()
                            banks.append((pH, b_pH))
                            for oc in range(8):
                                T.op("pe", lambda e, oc=oc, l=l, half=half, pH=pH: e.matmul(
                                    pH[:], lhsT=mt[:, oc, l * 128:(l + 1) * 128],
                                    rhs=Wout[:, oc, half * 512:(half + 1) * 512],
                                    start=(oc == 0), stop=(oc == 7)), [b_mt, b_Wout], [b_pH], signal=(oc == 7))
                            T.op("act", lambda e, s=s, pH=pH, half=half: e.activation(
                                out=junk[:, half * 512:(half + 1) * 512], in_=pH[:], func=AF.Square, scale=0.5,
                                accum_out=s[:, half:half + 1]), [b_pH], [b_junk, b_s])
                        T.op("dve", lambda e, s=s: e.tensor_tensor(out=s[:, 2:3], in0=s[:, 0:1], in1=s[:, 1:2], op=ALU.add),
                             [b_s], [b_s])
                        rstd_batch(s, b_s, 1, 2, 3, 1.0 / D, extra_bias=math.log(0.5))
                        ht, b_ht = h1t[l % 2]
                        for half in range(2):
                            pH, b_pH = banks[half]
                            T.op("dve", lambda e, s=s, ht=ht, pH=pH, half=half: e.scalar_tensor_tensor(
                                out=ht[:, half * 512:(half + 1) * 512], in0=pH[:], scalar=s[:, 3:4],
                                in1=gpost[:, half * 512:(half + 1) * 512], op0=ALU.mult, op1=ALU.mult),
                                [b_pH, b_s, b_gpost], [b_ht])
                        T.dma("sp", lambda e, ht=ht, i=i: e.dma_start(out=h1scr[i * 128:(i + 1) * 128, :], in_=ht[:]),
                              [b_ht], [b_h1scr], f"h1s{l % 2}")
                        if dbg:
                            T.dma("sp", lambda e, ht=ht, i=i: e.dma_start(out=dbgo["d_h1"][i * 128:(i + 1) * 128, :], in_=ht[:]),
                                  [b_ht], [], "dbg")
                        yield

                def run_interleaved(ga, gb, nb_per_a):
                    da = db = False
                    while not (da and db):
                        if not da:
                            try:
                                next(ga)
                            except StopIteration:
                                da = True
                        for _ in range(nb_per_a):
                            if db:
                                break
                            try:
                                next(gb)
                            except StopIteration:
                                db = True
                        if da and not db:
                            for _ in gb:
                                pass
                            db = True

                gx0 = genX(0)
                next(gx0)
                next(gx0)
                wload2(Wg[:], "wg", b_Wg, nsplit=4)
                wload2(Wbs[:], "wbs", b_Wbs)
                wload2(Wba[:], "wba", b_Wba)
                wload2(Wout[:], "wout", b_Wout)
                for _ in gx0:
                    pass
                for go in range(NGO):
                    gy = genY(go)
                    gx = genX(go + 1) if go + 1 < NGO else iter(())
                    run_interleaved(gy, gx, 2)

        T.barrier()
        with ExitStack() as stE:
            Wup = sb(stE, "Wup", [128, 8, 8, 512], BF16)
            Wdn = sb(stE, "Wdn", [128, 32, D], BF16)
            b_Wup = [Buf() for _ in range(8)]
            b_Wdn = [Buf() for _ in range(8)]
            gpost2 = sb(stE, "gpost2", [128, D], F32); b_gpost2 = Buf()
            T.dma("sp", lambda e: e.dma_start(out=gpost2[:], in_=g_fpost.partition_broadcast(128)), [], [b_gpost2], "c3")
            T.dma("sp", lambda e: e.dma_start(out=gbc[:], in_=g_fpre.partition_broadcast(128)), [], [b_gbc], "c3")
            hg = sbr(stE, "hg", [128, D], F32, 4)
            xe = sbr(stE, "xe", [128, D], F32, 2)
            xs2 = sbr(stE, "xs2", [128, D], BF16, 2)
            x2T = sbr(stE, "x2T", [128, 8, 256], BF16, 2)
            rt = sbr(stE, "rt", [128, 256], F32, 3)
            hT = sbr(stE, "hT", [128, 32, 256], BF16, 1)
            ot = sbr(stE, "ot", [128, D], F32, 1)
            NSG = NO // 2
            bank_n = [0]

            def bank():
                r = pf[bank_n[0] % 6]
                bank_n[0] += 1
                return r

            def frontE(sg):
                dT, b_dT = x2T[sg % 2]
                s, b_s = stat[fr_n[0] % 4]
                fr_n[0] += 1
                T.op("pool", lambda e: e.memset(s[:, 0:2], 0.0), [], [b_s])
                for l in range(2):
                    i = 2 * sg + l
                    ht, b_h = hg[i % 4]
                    xt, b_x = xe[l]
                    T.dma("sp", lambda e, ht=ht, i=i: e.dma_start(out=ht[:], in_=h1scr[i * 128:(i + 1) * 128, :]),
                          [b_h1scr], [b_h], f"hg{i % 4}")
                    T.dma("sp", lambda e, xt=xt, i=i: e.dma_start(out=xt[:], in_=xo[i * 128:(i + 1) * 128, :]),
                          [], [b_x], f"xe{l}")
                    T.op("dve", lambda e, ht=ht, xt=xt: e.tensor_tensor(out=ht[:], in0=ht[:], in1=xt[:], op=ALU.add),
                         [b_h, b_x], [b_h])
                    T.op("act", lambda e, ht=ht, l=l: e.activation(out=junk[:], in_=ht[:], func=AF.Square,
                                                                   accum_out=s[:, l:l + 1]), [b_h], [b_junk, b_s])
                T.op("act", lambda e: e.activation(out=s[:, 2:4], in_=s[:, 0:2], func=AF.Ln, scale=1.0 / D, bias=EPS),
                     [b_s], [b_s])
                T.op("act", lambda e: e.activation(out=s[:, 2:4], in_=s[:, 2:4], func=AF.Exp, scale=-0.5), [b_s], [b_s])
                for l in range(2):
                    i = 2 * sg + l
                    ht, b_h = hg[i % 4]
                    xst, b_xs = xs2[l]
                    T.op("dve", lambda e, ht=ht, xst=xst, l=l: e.scalar_tensor_tensor(
                        out=xst[:], in0=ht[:], scalar=s[:, 2 + l:3 + l], in1=gbc[:], op0=ALU.mult, op1=ALU.mult),
                        [b_h, b_s, b_gbc], [b_xs])

            def frontET(sg):
                dT, b_dT = x2T[sg % 2]
                for l in range(2):
                    xst, b_xs = xs2[l]
                    transpose8(xst, b_xs, dT[:, :, l * 128:(l + 1) * 128], b_dT, l, "act" if l == 0 else "dve")

            def upE(sg, mid=None):
                dT, b_dT = x2T[sg % 2]
                h_, b_hT = hT[0]
                for j in range(32):
                    if j == 12 and mid is not None:
                        mid()
                    pu, b_pu = bank()
                    for kc in range(8):
                        T.op("pe", lambda e, kc=kc, j=j, pu=pu: e.matmul(
                            pu[:, 0:256], lhsT=Wup[:, j // 4, kc, (j % 4) * 128:(j % 4 + 1) * 128], rhs=dT[:, kc, :],
                            start=(kc == 0), stop=(kc == 7)), [b_Wup[j // 4], b_dT], [b_pu], signal=(kc == 7))
                    r_, b_r = rt[j % 3]
                    T.op("act", lambda e, r_=r_, pu=pu: e.activation(out=r_[:], in_=pu[:, 0:256], func=AF.Relu),
                         [b_pu], [b_r])
                    T.op("dve" if j % 2 == 0 else "pool", lambda e, r_=r_, j=j: e.tensor_tensor(
                        out=h_[:, j, :], in0=r_[:], in1=r_[:], op=ALU.mult), [b_r], [b_hT])

            def downE(sg):
                h_, b_hT = hT[0]
                for l in range(2):
                    i = 2 * sg + l
                    pD0, b_pD0 = bank()
                    pD1, b_pD1 = bank()
                    for half, (pD, b_pD) in enumerate(((pD0, b_pD0), (pD1, b_pD1))):
                        for j in range(32):
                            T.op("pe", lambda e, j=j, l=l, half=half, pD=pD: e.matmul(
                                pD[:], lhsT=h_[:, j, l * 128:(l + 1) * 128], rhs=Wdn[:, j, half * 512:(half + 1) * 512],
                                start=(j == 0), stop=(j == 31)), [b_hT, b_Wdn[j // 4]], [b_pD], signal=(j == 31))
                    s, b_s = stat[fr_n[0] % 4]
                    fr_n[0] += 1
                    T.op("pool", lambda e, s=s: e.memset(s[:, 0:2], 0.0), [], [b_s])
                    T.op("act", lambda e, s=s, pD0=pD0: e.activation(out=junk[:, 0:512], in_=pD0[:], func=AF.Square,
                                                                     accum_out=s[:, 0:1]), [b_pD0], [b_junk, b_s])
                    T.op("act", lambda e, s=s, pD1=pD1: e.activation(out=junk[:, 512:1024], in_=pD1[:], func=AF.Square,
                                                                     accum_out=s[:, 1:2]), [b_pD1], [b_junk, b_s])
                    T.op("dve", lambda e, s=s: e.tensor_tensor(out=s[:, 2:3], in0=s[:, 0:1], in1=s[:, 1:2], op=ALU.add),
                         [b_s], [b_s])
                    T.op("act", lambda e, s=s: e.activation(out=s[:, 3:4], in_=s[:, 2:3], func=AF.Ln, scale=1.0 / D, bias=EPS),
                         [b_s], [b_s])
                    T.op("act", lambda e, s=s: e.activation(out=s[:, 4:5], in_=s[:, 3:4], func=AF.Exp, scale=-0.5), [b_s], [b_s])
                    o_, b_o = ot[0]
                    ht, b_h = hg[i % 4]
                    T.op("dve", lambda e, s=s, o_=o_, pD0=pD0: e.scalar_tensor_tensor(
                        out=o_[:, 0:512], in0=pD0[:], scalar=s[:, 4:5], in1=gpost2[:, 0:512], op0=ALU.mult, op1=ALU.mult),
                        [b_pD0, b_s, b_gpost2], [b_o])
                    T.op("dve", lambda e, s=s, o_=o_, pD1=pD1: e.scalar_tensor_tensor(
                        out=o_[:, 512:1024], in0=pD1[:], scalar=s[:, 4:5], in1=gpost2[:, 512:1024], op0=ALU.mult, op1=ALU.mult),
                        [b_pD1, b_s, b_gpost2], [b_o])
                    T.op("dve", lambda e, o_=o_, ht=ht: e.tensor_tensor(out=o_[:], in0=o_[:], in1=ht[:], op=ALU.add),
                         [b_o, b_h], [b_o])
                    T.dma("sp", lambda e, o_=o_, i=i: e.dma_start(out=out[i * 128:(i + 1) * 128, :], in_=o_[:]),
                          [b_o], [], f"ost{i % 2}")

            wupv = wscr["wup"].rearrange("p (pc kc n) -> p pc kc n", pc=8, kc=8)
            wdnv = wscr["wdn"].rearrange("p (kc n) -> p kc n", kc=32)
            for p in range(8):
                T.dma("act", lambda e, p=p: e.dma_start(
                    out=Wdn[:, p * 4:(p + 1) * 4, :], in_=wdnv[:, p * 4:(p + 1) * 4, :]),
                    [b_wscr["wdn"]], [b_Wdn[p]], f"w_dn{p}")
            frontE(0)
            for p in range(8):
                T.dma("sp", lambda e, p=p: e.dma_start(out=Wup[:, p, :, :], in_=wupv[:, p, :, :]),
                      [b_wscr["wup"]], [b_Wup[p]], f"w_up{p}")
            frontET(0)
            for sg in range(NSG):
                upE(sg, (lambda sg=sg: frontE(sg + 1)) if sg + 1 < NSG else None)
                if sg + 1 < NSG:
                    frontET(sg + 1)
                downE(sg)
            for k in ("ost0", "ost1", "dbg"):
                if k in T.semobj:
                    nc.sync.wait_ge(T.semobj[k], T.cnt[k])
        psum_free()
        print(f"[build] ops={T.nops} waits={T.nwaits} sems={len(T.semobj)}")
    return nc


def _host_inputs(inputs, NT):
    f = lambda a: np.ascontiguousarray(np.asarray(a, dtype=np.float32))
    x = f(inputs["x"])
    w_in = f(inputs["w_in"])[0]
    wz, wq_, wk_, wv_, wf_, wg_ = np.split(w_in, [1024, 1536, 2048, 2560, 2568], axis=1)
    common = {
        "wkvf": f(np.concatenate([wk_, wv_, wf_], axis=1)),
        "wq": f(wq_),
        "wu": f(wz[:, :512]),
        "wv2": f(wz[:, 512:]),
        "wg": f(wg_),
        "wbs": f(inputs["w_branch_sgu"])[0],
        "wba": f(inputs["w_branch_attn"])[0],
        "wout": f(inputs["w_out"])[0],
        "wup": f(inputs["w_up"])[0],
        "wdn": f(inputs["w_down"])[0],
        "g_pre": f(inputs["g_mix_pre"]),
        "g_post": f(inputs["g_mix_post"]),
        "g_fpre": f(inputs["g_ffn_pre"]),
        "g_fpost": f(inputs["g_ffn_post"]),
        "g_sgu": f(inputs["g_sgu"]),
        "b_sgu": f(inputs["b_sgu"]),
        "b_fg": f(inputs["b_forget"]),
        "wsT": f(np.transpose(f(inputs["w_spatial"])[0], (2, 0, 1)).reshape(128, 8 * 128)),
        "c_id": np.eye(128, dtype=np.float32),
        "c_tri": np.triu(np.ones((128, 128), np.float32)),
        "c_one": np.ones((128, 128), np.float32),
    }
    bs = f(inputs["b_spatial"])[0]
    bsp = np.repeat(bs.reshape(4, 2, 1, 128), 64, axis=2).reshape(4, 128, 128)
    bsp4 = np.tile(bsp[:, :, None, :], (1, 1, 4, 1)).reshape(4, 128, 512)
    common["bsp4"] = f(np.transpose(bsp4, (1, 0, 2)).reshape(128, 4 * 512))
    maps = []
    s_idx = np.arange(128)[:, None]
    t_idx = np.arange(128)[None, :]
    for c in range(8):
        b, j = c // 4, c % 4
        xb = x[b]
        xo = xb.reshape(NT, 128, D)[j::4].reshape(-1, D)
        sel = np.zeros((128, 4), np.float32)
        sel[:, j] = 1.0
        madd = np.zeros((128, 4, 128), np.float32)
        for m in range(4):
            if m == j:
                madd[:, m, :] = np.where(s_idx <= t_idx, 0.0, NEG)
            elif m > j:
                madd[:, m, :] = NEG
        d = dict(common)
        d["xf"] = f(xb)
        d["xo"] = f(xo)
        d["c_sel"] = sel
        d["c_madd"] = madd.reshape(128, 512)
        maps.append(d)
    return maps


_NC_CACHE = {}


def kernel(**inputs):
    x = np.asarray(inputs["x"])
    B, S, _ = x.shape
    NT = S // 128
    if NT not in _NC_CACHE:
        _NC_CACHE[NT] = build_nc(NT)
    nc = _NC_CACHE[NT]
    maps = _host_inputs(inputs, NT)
    res = run_bass_kernel_spmd(nc, maps, core_ids=list(range(8)))
    out = np.zeros((B, NT, 128, D), np.float32)
    for c in range(8):
        b, j = c // 4, c % 4
        out[b, j::4] = np.asarray(res.results[c]["out"], dtype=np.float32).reshape(NT // 4, 128, D)
    return out.reshape(B, S, D)
```

```python
from contextlib import ExitStack
import math
import numpy as np
import concourse.bass as bass
import concourse.mybir as mybir
from concourse.bass_utils import run_bass_kernel_spmd

F32 = mybir.dt.float32
BF16 = mybir.dt.bfloat16
AF = mybir.ActivationFunctionType
ALU = mybir.AluOpType

D = 1024
NEG = -1.0e5
EPS = 1e-6


class Buf:
    def __init__(self, name=""):
        self.name = name
        self.w = None
        self.r = {}


class Tracker:
    def __init__(self, nc, stack):
        self.nc = nc
        self.stack = stack
        self.engs = {"pe": nc.tensor, "act": nc.scalar, "dve": nc.vector,
                     "pool": nc.gpsimd, "sp": nc.sync}
        self.semobj = {}
        self.cnt = {}
        for k in ["pe", "act", "dve", "pool"]:
            self.semobj[k] = stack.enter_context(nc.semaphore("s_" + k))
            self.cnt[k] = 0
        self.waited = {k: {} for k in self.engs}
        self.shared = {"c0", "c1", "c2", "c3"}
        self.nwaits = 0
        self.nops = 0

    def _wait(self, eng, ev):
        if ev is None:
            return
        key, val = ev
        if key in self.shared:
            val = self.cnt[key]
        if key == eng and eng == "pe":
            return
        if self.waited[eng].get(key, 0) >= val:
            return
        self.engs[eng].wait_ge(self.semobj[key], val)
        self.waited[eng][key] = val
        self.nwaits += 1

    def _deps(self, eng, reads, writes):
        for b in reads:
            self._wait(eng, b.w)
        for b in writes:
            self._wait(eng, b.w)
            for k, v in b.r.items():
                self._wait(eng, (k, v))

    def _record(self, ev, reads, writes):
        k, v = ev
        for b in reads:
            if b.r.get(k, 0) < v:
                b.r[k] = v
        for b in writes:
            b.w = ev
            b.r = {}

    def op(self, eng, fn, reads=(), writes=(), signal=True):
        self._deps(eng, reads, writes)
        ins = fn(self.engs[eng])
        self.nops += 1
        if signal:
            self.cnt[eng] += 1
            ins.then_inc(self.semobj[eng], 1)
            ev = (eng, self.cnt[eng])
        else:
            ev = (eng, self.cnt[eng] + 1)
        self._record(ev, reads, writes)
        return ev

    def barrier(self):
        for eng in self.engs:
            for k in list(self.semobj.keys()):
                if self.cnt[k] > 0:
                    self._wait(eng, (k, self.cnt[k]))

    def dma(self, q, fn, reads, writes, sem):
        if sem not in self.semobj:
            self.semobj[sem] = self.stack.enter_context(self.nc.semaphore("d_" + sem))
            self.cnt[sem] = 0
        self._deps(q, reads, writes)
        ins = fn(self.engs[q])
        self.nops += 1
        self.cnt[sem] += 16
        ins.then_inc(self.semobj[sem], 16)
        ev = (sem, self.cnt[sem])
        self._record(ev, reads, writes)
        return ev


def build_nc(NT, dbg=False):
    NO = NT // 4
    NGA = NT // 4
    NGO = NO // 4
    S = NT * 128
    SO = NO * 128
    nc = bass.Bass("TRN2", target_bir_lowering=False)

    def din(name, shape):
        return nc.dram_tensor(name, shape, F32, kind="ExternalInput").ap()

    xf = din("xf", [S, D])
    xo = din("xo", [SO, D])
    wkvf = din("wkvf", [D, 1032])
    wq = din("wq", [D, 512])
    wu = din("wu", [D, 512])
    wv2 = din("wv2", [D, 512])
    wg = din("wg", [D, 2048])
    wbs = din("wbs", [512, D])
    wba = din("wba", [512, D])
    wout = din("wout", [D, D])
    wup = din("wup", [D, 4096])
    wdn = din("wdn", [4096, D])
    g_pre = din("g_pre", [1, D])
    g_post = din("g_post", [1, D])
    g_fpre = din("g_fpre", [1, D])
    g_fpost = din("g_fpost", [1, D])
    g_sgu = din("g_sgu", [1, 512])
    b_sgu = din("b_sgu", [1, 512])
    b_fg = din("b_fg", [1, 8])
    wsT = din("wsT", [128, 8 * 128])
    bsp4 = din("bsp4", [128, 4 * 512])
    c_id = din("c_id", [128, 128])
    c_tri = din("c_tri", [128, 128])
    c_one = din("c_one", [128, 128])
    c_sel = din("c_sel", [128, 4])
    c_madd = din("c_madd", [128, 4 * 128])
    kscr = nc.dram_tensor("kscr", [512, S], BF16).ap()
    h1scr = nc.dram_tensor("h1scr", [SO, D], F32).ap()
    out = nc.dram_tensor("out", [SO, D], F32, kind="ExternalOutput").ap()
    dbgo = {}
    if dbg:
        dbgo["d_cum"] = nc.dram_tensor("d_cum", [128, NT * 8], F32, kind="ExternalOutput").ap()
        dbgo["d_yatt"] = nc.dram_tensor("d_yatt", [128, NO * 512], BF16, kind="ExternalOutput").ap()
        dbgo["d_h1"] = nc.dram_tensor("d_h1", [SO, D], F32, kind="ExternalOutput").ap()

    with ExitStack() as st0:
        T = Tracker(nc, st0)

        def sb(stk, name, shape, dt):
            return stk.enter_context(nc.sbuf_tensor(name, shape, dt))

        def sbr(stk, name, shape, dt, n):
            return [(sb(stk, f"{name}{i}", shape, dt), Buf(f"{name}{i}")) for i in range(n)]

        def wload(dst, src, buf, sem, kc):
            T.dma("pool", lambda e: e.dma_start(out=dst, in_=src.rearrange("(kc p) n -> p kc n", p=128)),
                  [], [buf], sem)

        wsrc = {"wu": (wu, [D, 512]), "wv2": (wv2, [D, 512]), "wg": (wg, [D, 2048]), "wbs": (wbs, [512, D]),
                "wba": (wba, [512, D]), "wout": (wout, [D, D]), "wup": (wup, [D, 4096]), "wdn": (wdn, [4096, D])}
        wscr = {k: nc.dram_tensor("scr_" + k, [128, shp[0] // 128 * shp[1]], BF16).ap() for k, (_, shp) in wsrc.items()}
        b_wscr = {k: Buf("scr_" + k) for k in wsrc}

        def precast(k):
            src, shp = wsrc[k]
            rows, n = shp
            KC = rows // 128
            if k == "wup":
                dstv = wscr[k].rearrange("p (pc kc n) -> p pc kc n", pc=8, kc=8)
                for pc in range(8):
                    T.dma("pool", lambda e, pc=pc: e.dma_start(
                        out=dstv[:, pc, :, :], in_=src[:, pc * 512:(pc + 1) * 512].rearrange("(kc p) n -> p kc n", p=128)),
                        [], [b_wscr[k]], "pc_" + k)
                return
            dstv = wscr[k].rearrange("p (kc n) -> p kc n", kc=KC)
            step = 4 if KC >= 4 else KC
            for c0 in range(0, KC, step):
                T.dma("pool", lambda e, c0=c0: e.dma_start(
                    out=dstv[:, c0:c0 + step, :],
                    in_=src[c0 * 128:(c0 + step) * 128, :].rearrange("(kc p) n -> p kc n", p=128)),
                    [], [b_wscr[k]], "pc_" + k)

        wl_n = [0]

        def wload2(dst, k, buf, nsplit=2):
            KC = dst.shape[1]
            srcv = wscr[k].rearrange("p (kc n) -> p kc n", kc=KC)
            step = max(1, KC // nsplit)
            for c0 in range(0, KC, step):
                q = "sp" if wl_n[0] % 2 == 0 else "act"
                wl_n[0] += 1
                T.dma(q, lambda e, c0=c0: e.dma_start(out=dst[:, c0:c0 + step, :], in_=srcv[:, c0:c0 + step, :]),
                      [b_wscr[k]], [buf], f"w_{k}")

        b_kscr = Buf("kscr")
        b_h1scr = Buf("h1scr")
        pf = []
        pb = []
        ps_stack = [None]
        ps_gen = [0]

        def psum_std():
            stk = ExitStack()
            ps_stack[0] = stk
            k = ps_gen[0]
            ps_gen[0] += 1
            pf[:] = [(stk.enter_context(nc.psum_tensor(f"pf{k}_{i}", [128, 512], F32)), Buf(f"pf{i}")) for i in range(6)]
            pb[:] = [(stk.enter_context(nc.psum_tensor(f"pb{k}_{i}", [128, 1024], BF16)), Buf(f"pb{i}")) for i in range(2)]

        def psum_free():
            ps_stack[0].close()
            ps_stack[0] = None

        psum_std()

        identb = sb(st0, "identb", [128, 128], BF16); b_identb = Buf()
        triU = sb(st0, "triU", [128, 128], F32); b_triU = Buf()
        onesf = sb(st0, "onesf", [128, 128], F32); b_onesf = Buf()
        gbc = sb(st0, "gbc", [128, D], F32); b_gbc = Buf()
        bfbc = sb(st0, "bfbc", [128, 8], F32); b_bfbc = Buf()
        sel = sb(st0, "sel", [128, 4], F32); b_sel = Buf()
        madd = sb(st0, "madd", [128, 4, 128], F32); b_madd = Buf()
        cumsp = sb(st0, "cumsp", [128, NT, 8], F32); b_cumsp = Buf()
        carry = sb(st0, "carry", [128, 8], F32); b_carry = Buf()
        junk = sb(st0, "junk", [128, D], BF16); b_junk = Buf()
        stat = sbr(st0, "stat", [128, 8], F32, 4)

        T.dma("pool", lambda e: e.dma_start(out=identb[:], in_=c_id), [], [b_identb], "c0")
        T.dma("sp", lambda e: e.dma_start(out=triU[:], in_=c_tri), [], [b_triU], "c1")
        T.dma("sp", lambda e: e.dma_start(out=onesf[:], in_=c_one), [], [b_onesf], "c1")
        T.dma("sp", lambda e: e.dma_start(out=gbc[:], in_=g_pre.partition_broadcast(128)), [], [b_gbc], "c1")
        T.dma("sp", lambda e: e.dma_start(out=bfbc[:], in_=b_fg.partition_broadcast(128)), [], [b_bfbc], "c1")
        T.dma("sp", lambda e: e.dma_start(out=sel[:], in_=c_sel), [], [b_sel], "c1")
        T.dma("sp", lambda e: e.dma_start(out=madd[:], in_=c_madd.rearrange("p (m t) -> p m t", m=4)),
              [], [b_madd], "c1")
        T.op("dve", lambda e: e.memset(carry[:], 0.0), [], [b_carry])

        fr_n = [0]

        def rms_scale(x_ap, b_x, xs_ap, b_xs, eng, pre_scale=1.0, extra_bias=0.0):
            s, b_s = stat[fr_n[0] % 4]
            fr_n[0] += 1
            T.op("dve", lambda e: e.memset(s[:, 0:1], 0.0), [], [b_s])
            T.op("act", lambda e: e.activation(out=junk[:], in_=x_ap, func=AF.Square, scale=pre_scale,
                                               accum_out=s[:, 0:1]), [b_x], [b_junk, b_s])
            T.op("act", lambda e: e.activation(out=s[:, 1:2], in_=s[:, 0:1], func=AF.Ln, scale=1.0 / D, bias=EPS),
                 [b_s], [b_s])
            T.op("act", lambda e: e.activation(out=s[:, 2:3], in_=s[:, 1:2], func=AF.Exp, scale=-0.5,
                                               bias=extra_bias), [b_s], [b_s])
            T.op("dve", lambda e: e.scalar_tensor_tensor(out=xs_ap, in0=x_ap, scalar=s[:, 2:3], in1=gbc[:],
                                                       op0=ALU.mult, op1=ALU.mult), [b_x, b_s, b_gbc], [b_xs])

        def transpose8(xs_t, b_xs, dst_ap, b_dst, pbi, evac_eng, n=8):
            pbt, b_pb = pb[pbi]
            for c in range(n):
                T.op("pe", lambda e, c=c: e.transpose(out=pbt[:, c * 128:(c + 1) * 128],
                                                      in_=xs_t[:, c * 128:(c + 1) * 128], identity=identb[:]),
                     [b_xs, b_identb], [b_pb], signal=(c == n - 1))
            src = pbt[:, 0:n * 128].rearrange("p (c t) -> p c t", c=n)
            if evac_eng == "act":
                T.op("act", lambda e: e.copy(out=dst_ap, in_=src), [b_pb], [b_dst])
            else:
                T.op("dve", lambda e: e.tensor_copy(out=dst_ap, in_=src), [b_pb], [b_dst])

        with ExitStack() as stCD:
            yatt = sb(stCD, "yatt", [128, NO, 512], BF16); b_yatt = Buf()

            with ExitStack() as stABC:
                Vall = sb(stABC, "Vall", [128, NT, 8, 65], BF16); b_Vall = Buf()
                QT = sb(stABC, "QT", [67, 8, SO], BF16); b_QT = Buf()
                T.op("pool", lambda e: e.memset(Vall[:, :, :, 64:65], 1.0), [], [b_Vall])

                with ExitStack() as stAB:
                    Wkvf = sb(stAB, "Wkvf", [128, 8, 1032], BF16); b_Wkvf = Buf()
                    Wq = sb(stAB, "Wq", [128, 8, 512], BF16); b_Wq = Buf()
                    wload(Wkvf[:], wkvf, b_Wkvf, "w_kvf", 8)
                    wload(Wq[:], wq, b_Wq, "w_q", 8)
                    xa = sbr(stAB, "xa", [128, D], F32, 6)
                    xs = sbr(stAB, "xs", [128, D], BF16, 4)
                    xsT = sbr(stAB, "xsT", [128, 8, 512], BF16, 2)
                    kst = sbr(stAB, "kst", [128, 4, 512], BF16, 2)
                    ftmp = sbr(stAB, "ftmp", [128, 32], F32, 2)
                    spt = sbr(stAB, "spt", [128, 32], F32, 2)
                    CS = sbr(stAB, "CS", [128, 8, 67], BF16, 2)
                    co = sbr(stAB, "co", [128, 8], F32, 2)
                    co2 = sbr(stAB, "co2", [128, 8], F32, 2)
                    for c_, b_ in CS:
                        T.op("pool", lambda e, c_=c_: e.memset(c_[:], 0.0), [], [b_])

                    tiles = [("A", t) for t in range(NT)] + [("B", t) for t in range(NO)]
                    NTT = len(tiles)

                    XR = 6
                    GT = NGA + NGO
                    fstat = {}

                    def loadG(g):
                        for l in range(4):
                            i = 4 * g + l
                            kind, t = tiles[i]
                            src = xf if kind == "A" else xo
                            xt, b_x = xa[i % XR]
                            T.dma("sp", lambda e, xt=xt, src=src, t=t: e.dma_start(out=xt[:], in_=src[t * 128:(t + 1) * 128, :]),
                                  [], [b_x], f"xa{i % XR}")

                    def frontG(g):
                        s_, b_s = stat[fr_n[0] % 4]
                        fr_n[0] += 1
                        T.op("dve", lambda e: e.memset(s_[:, 0:4], 0.0), [], [b_s])
                        for l in range(4):
                            xt, b_x = xa[(4 * g + l) % XR]
                            T.op("act", lambda e, xt=xt, l=l: e.activation(out=junk[:], in_=xt[:], func=AF.Square,
                                                                           accum_out=s_[:, l:l + 1]), [b_x], [b_junk, b_s])
                        T.op("act", lambda e: e.activation(out=s_[:, 4:8], in_=s_[:, 0:4], func=AF.Ln, scale=1.0 / D, bias=EPS),
                             [b_s], [b_s])
                        T.op("act", lambda e: e.activation(out=s_[:, 4:8], in_=s_[:, 4:8], func=AF.Exp, scale=-0.5), [b_s], [b_s])
                        fstat[g] = (s_, b_s)

                    def frontG2(g):
                        s_, b_s = fstat[g]
                        for l in range(4):
                            xt, b_x = xa[(4 * g + l) % XR]
                            xst, b_xs = xs[l]
                            T.op("dve", lambda e, xt=xt, xst=xst, l=l: e.scalar_tensor_tensor(
                                out=xst[:], in0=xt[:], scalar=s_[:, 4 + l:5 + l], in1=gbc[:], op0=ALU.mult, op1=ALU.mult),
                                [b_x, b_s, b_gbc], [b_xs])

                    def transG(g):
                        dT, b_dT = xsT[g % 2]
                        for l in range(4):
                            xst, b_xs = xs[l]
                            transpose8(xst, b_xs, dT[:, :, l * 128:(l + 1) * 128], b_dT, l % 2,
                                       "act" if l % 2 == 0 else "dve")

                    def backA(g):
                        dT, b_dT = xsT[g % 2]
                        ks, b_ks = kst[g % 2]
                        for hp in range(4):
                            pk, b_pk = pf[hp % 2]
                            for kc in range(8):
                                T.op("pe", lambda e, kc=kc, hp=hp, pk=pk: e.matmul(
                                    pk[:], lhsT=Wkvf[:, kc, hp * 128:(hp + 1) * 128], rhs=dT[:, kc, :],
                                    start=(kc == 0), stop=(kc == 7)), [b_Wkvf, b_dT], [b_pk], signal=(kc == 7))
                            T.op("dve", lambda e, hp=hp, pk=pk: e.tensor_copy(out=ks[:, hp, :], in_=pk[:]),
                                 [b_pk], [b_ks])
                        T.dma("sp", lambda e: e.dma_start(
                            out=kscr.rearrange("(hp p) s -> p hp s", p=128)[:, :, g * 512:(g + 1) * 512],
                            in_=ks[:]), [b_ks], [b_kscr], f"kst{g % 2}")

                    def backA2(g):
                        dT, b_dT = xsT[g % 2]
                        for l in range(4):
                            pv, b_pv = pf[2 + l % 2]
                            for kc in range(8):
                                T.op("pe", lambda e, kc=kc, l=l, pv=pv: e.matmul(
                                    pv[:], lhsT=dT[:, kc, l * 128:(l + 1) * 128], rhs=Wkvf[:, kc, 512:1024],
                                    start=(kc == 0), stop=(kc == 7)), [b_Wkvf, b_dT], [b_pv], signal=(kc == 7))
                            T.op("act", lambda e, l=l, pv=pv: e.copy(
                                out=Vall[:, 4 * g + l, :, 0:64], in_=pv[:].rearrange("p (h d) -> p h d", h=8)),
                                [b_pv], [b_Vall])
                        pF, b_pF = pf[4]
                        for l in range(4):
                            for kc in range(8):
                                T.op("pe", lambda e, kc=kc, l=l: e.matmul(
                                    pF[:, l * 8:(l + 1) * 8], lhsT=dT[:, kc, l * 128:(l + 1) * 128],
                                    rhs=Wkvf[:, kc, 1024:1032], start=(kc == 0), stop=(kc == 7)),
                                    [b_Wkvf, b_dT], [b_pF], signal=(kc == 7 and l == 3))
                        ft, b_ft = ftmp[g % 2]
                        sp_, b_sp = spt[g % 2]
                        T.op("dve", lambda e: e.tensor_tensor(
                            out=ft[:].rearrange("p (l h) -> p l h", l=4), in0=pF[:, 0:32].rearrange("p (l h) -> p l h", l=4),
                            in1=bfbc[:].unsqueeze(1).broadcast_to([128, 4, 8]), op=ALU.add), [b_pF, b_bfbc], [b_ft])
                        T.op("act", lambda e: e.activation(out=ft[:], in_=ft[:], func=AF.Exp, scale=-1.0), [b_ft], [b_ft])
                        T.op("act", lambda e: e.activation(out=sp_[:], in_=ft[:], func=AF.Ln, bias=1.0), [b_ft], [b_sp])

                    def backA3(g):
                        sp_, b_sp = spt[g % 2]
                        pC, b_pC = pf[5]
                        for l in range(4):
                            for l2 in range(l):
                                T.op("pe", lambda e, l=l, l2=l2: e.matmul(
                                    pC[:, l * 8:(l + 1) * 8], lhsT=onesf[:], rhs=sp_[:, l2 * 8:(l2 + 1) * 8],
                                    start=(l2 == 0), stop=False), [b_onesf, b_sp], [b_pC], signal=False)
                            T.op("pe", lambda e, l=l: e.matmul(
                                pC[:, l * 8:(l + 1) * 8], lhsT=triU[:], rhs=sp_[:, l * 8:(l + 1) * 8],
                                start=(l == 0), stop=True), [b_triU, b_sp], [b_pC], signal=False)
                        for l2 in range(4):
                            T.op("pe", lambda e, l2=l2: e.matmul(
                                pC[:, 32:40], lhsT=onesf[:], rhs=sp_[:, l2 * 8:(l2 + 1) * 8],
                                start=(l2 == 0), stop=(l2 == 3)), [b_onesf, b_sp], [b_pC], signal=(l2 == 3))
                        T.op("dve", lambda e: e.tensor_tensor(
                            out=cumsp[:, 4 * g:4 * g + 4, :], in0=pC[:, 0:32].rearrange("p (l h) -> p l h", l=4),
                            in1=carry[:].unsqueeze(1).broadcast_to([128, 4, 8]), op=ALU.add),
                            [b_pC, b_carry], [b_cumsp])
                        T.op("dve", lambda e: e.tensor_tensor(out=carry[:], in0=carry[:], in1=pC[:, 32:40], op=ALU.add),
                             [b_pC, b_carry], [b_carry])

                    def backB(go):
                        g = NGA + go
                        dT, b_dT = xsT[g % 2]
                        for h in range(8):
                            pq, b_pq = pf[h % 2]
                            for kc in range(8):
                                T.op("pe", lambda e, kc=kc, h=h, pq=pq: e.matmul(
                                    pq[0:64, :], lhsT=Wq[:, kc, h * 64:(h + 1) * 64], rhs=dT[:, kc, :],
                                    start=(kc == 0), stop=(kc == 7)), [b_Wq, b_dT], [b_pq], signal=(kc == 7))
                            if h % 2 == 0:
                                T.op("act", lambda e, h=h, pq=pq: e.copy(
                                    out=QT[0:64, h, go * 512:(go + 1) * 512], in_=pq[0:64, :]), [b_pq], [b_QT])
                            else:
                                T.op("dve", lambda e, h=h, pq=pq: e.tensor_copy(
                                    out=QT[0:64, h, go * 512:(go + 1) * 512], in_=pq[0:64, :]), [b_pq], [b_QT])
                        for l in range(4):
                            i = 4 * go + l
                            c1, b_c1 = co[i % 2]
                            c2, b_c2 = co2[i % 2]
                            cs, b_cs = CS[i % 2]
                            T.op("dve", lambda e, i=i, c1=c1: e.tensor_scalar(
                                out=c1[:], in0=cumsp[:, 4 * i, :], scalar1=sel[:, 0:1], scalar2=None, op0=ALU.mult),
                                [b_cumsp, b_sel], [b_c1])
                            for m in range(1, 4):
                                T.op("dve", lambda e, i=i, m=m, c1=c1: e.scalar_tensor_tensor(
                                    out=c1[:], in0=cumsp[:, 4 * i + m, :], scalar=sel[:, m:m + 1], in1=c1[:],
                                    op0=ALU.mult, op1=ALU.add), [b_cumsp, b_sel, b_c1], [b_c1])
                            T.op("dve", lambda e, c1=c1: e.tensor_scalar(
                                out=c1[:], in0=c1[:], scalar1=-8.0, scalar2=None, op0=ALU.mult), [b_c1], [b_c1])
                            T.op("dve", lambda e, c1=c1, cs=cs: e.tensor_copy(out=cs[:, :, 64], in_=c1[:]), [b_c1], [b_cs])
                            T.op("dve", lambda e, c1=c1, c2=c2, cs=cs: e.tensor_tensor(
                                out=c2[:], in0=c1[:], in1=cs[:, :, 64], op=ALU.subtract), [b_c1, b_cs], [b_c2])
                            T.op("dve", lambda e, c2=c2, cs=cs: e.tensor_copy(out=cs[:, :, 65], in_=c2[:]), [b_c2], [b_cs])
                            T.op("dve", lambda e, c1=c1, c2=c2, cs=cs: e.tensor_tensor(
                                out=c1[:], in0=c2[:], in1=cs[:, :, 65], op=ALU.subtract), [b_c2, b_cs], [b_c1])
                            T.op("dve", lambda e, c1=c1, cs=cs: e.tensor_copy(out=cs[:, :, 66], in_=c1[:]), [b_c1], [b_cs])
                            for half in range(2):
                                pa, b_pa = pf[2 + half]
                                for hh in range(4):
                                    h = half * 4 + hh
                                    T.op("pe", lambda e, h=h, hh=hh, pa=pa, cs=cs: e.matmul(
                                        pa[0:67, hh * 128:(hh + 1) * 128], lhsT=cs[:, h, :], rhs=identb[:],
                                        start=True, stop=True), [b_cs, b_identb], [b_pa], signal=(hh == 3))
                                T.op("act", lambda e, half=half, pa=pa, i=i: e.copy(
                                    out=QT[64:67, half * 4:half * 4 + 4, i * 128:(i + 1) * 128],
                                    in_=pa[64:67, :].rearrange("p (h t) -> p h t", h=4)), [b_pa], [b_QT])

                    loadG(0)
                    frontG(0)
                    frontG2(0)
                    if GT > 1:
                        loadG(1)
                    transG(0)
                    for g in range(GT):
                        if g + 1 < GT:
                            frontG(g + 1)
                        if g < NGA:
                            backA(g)
                        else:
                            backB(g - NGA)
                        if g + 1 < GT:
                            frontG2(g + 1)
                        if g + 2 < GT:
                            loadG(g + 2)
                        if g < NGA:
                            backA2(g)
                        if g + 1 < GT:
                            transG(g + 1)
                        if g < NGA:
                            backA3(g)
                    if dbg:
                        T.dma("sp", lambda e: e.dma_start(out=dbgo["d_cum"], in_=cumsp[:].rearrange("p t h -> p (t h)")),
                              [b_cumsp], [], "dbg")

                T.barrier()
                psum_free()
                with ExitStack() as stC:
                    pS = [(stC.enter_context(nc.psum_tensor(f"pS{i}", [128, 512], F32)), Buf(f"pS{i}")) for i in range(4)]
                    pO = [(stC.enter_context(nc.psum_tensor(f"pO{i}", [128, 512], F32)), Buf(f"pO{i}")) for i in range(4)]
                    KT = sbr(stC, "KT", [67, S], BF16, 2)
                    PT = sbr(stC, "PT", [128, 512], BF16, 4)
                    rec = sbr(stC, "rec", [128, 4], F32, 4)
                    for kt_, b_ in KT:
                        T.op("pool", lambda e, kt_=kt_: e.memset(kt_[64:67, :], 1.0), [], [b_])
                    for k_ in ("wu", "wv2", "wg", "wbs", "wba", "wout", "wup", "wdn"):
                        precast(k_)

                    def loadK(h):
                        kt_, b_ = KT[h % 2]
                        T.dma("sp", lambda e: e.dma_start(out=kt_[0:64, :], in_=kscr[h * 64:(h + 1) * 64, :]),
                              [b_kscr], [b_], f"kt{h % 2}")

                    gsets = [[a] for a in range(NGO)]
                    items = []
                    for h in range(8):
                        for si, gs in enumerate(gsets):
                            for kt in range(16 * gs[-1] + 16):
                                items.append((h, si, kt))

                    def colstart(go, kt):
                        if kt < 16 * go:
                            return 0, None
                        l = kt // 4 - 4 * go
                        return l * 128, kt % 4

                    def active(si, kt):
                        r = []
                        for slot, go in enumerate(gsets[si]):
                            if kt <= 16 * go + 15:
                                c0, m = colstart(go, kt)
                                r.append((slot, go, c0, m))
                        return r

                    def emitS(n):
                        h, si, kt = items[n]
                        kt_, b_k = KT[h % 2]
                        ps_, b_ps = pS[n % 4]
                        for slot, go, c0, m in active(si, kt):
                            T.op("pe", lambda e, slot=slot, go=go, c0=c0: e.matmul(
                                ps_[:, slot * 512 + c0:(slot + 1) * 512], lhsT=kt_[0:67, kt * 128:(kt + 1) * 128],
                                rhs=QT[0:67, h, go * 512 + c0:(go + 1) * 512], start=True, stop=True),
                                [b_k, b_QT], [b_ps])

                    def emitE(n):
                        h, si, kt = items[n]
                        ps_, b_ps = pS[n % 4]
                        pt_, b_pt = PT[n % 4]
                        act = active(si, kt)
                        for slot, go, c0, m in act:
                            if m is not None:
                                lo = slot * 512 + c0
                                T.op("dve", lambda e, lo=lo, m=m: e.tensor_tensor(
                                    out=ps_[:, lo:lo + 128], in0=ps_[:, lo:lo + 128], in1=madd[:, m, :], op=ALU.add),
                                    [b_ps, b_madd], [b_ps])
                        lo = act[0][0] * 512 + act[0][2]
                        hi = (act[-1][0] + 1) * 512
                        T.op("act", lambda e: e.activation(
                            out=pt_[:, lo:hi], in_=ps_[:, lo:hi], func=AF.Exp, scale=0.125,
                            bias=cumsp[:, kt, h:h + 1]), [b_ps, b_cumsp], [b_pt])

                    def emitPV(n):
                        h, si, kt = items[n]
                        pt_, b_pt = PT[n % 4]
                        ring = 2 * ((h * len(gsets) + si) % 2)
                        for slot, go, c0, m in active(si, kt):
                            po, b_po = pO[ring + slot]
                            l0 = c0 // 128
                            for l in range(l0, 4):
                                last = 4 * (4 * go + l) + 3
                                T.op("pe", lambda e, l=l, last=last, slot=slot, po=po: e.matmul(
                                    po[:, l * 65:(l + 1) * 65], lhsT=pt_[:, slot * 512 + l * 128:slot * 512 + (l + 1) * 128],
                                    rhs=Vall[:, kt, h, :], start=(kt == 0 and l == 0), stop=(kt == last),
                                    skip_group_check=True),
                                    [b_pt, b_Vall], [b_po], signal=(l == 3))
                            if kt == 16 * go + 15:
                                rc, b_rc = rec[ring + slot]
                                pov = po[:, 0:260].rearrange("p (l d) -> p l d", l=4)
                                T.op("dve", lambda e, rc=rc, pov=pov: e.reciprocal(out=rc[:], in_=pov[:, :, 64]), [b_po], [b_rc])
                                for l in range(4):
                                    T.op("dve", lambda e, l=l, rc=rc, pov=pov, go=go: e.tensor_scalar(
                                        out=yatt[:, 4 * go + l, h * 64:(h + 1) * 64], in0=pov[:, l, 0:64],
                                        scalar1=rc[:, l:l + 1], scalar2=None, op0=ALU.mult), [b_po, b_rc], [b_yatt])

                    loadK(0)
                    NI = len(items)
                    per_h = NI // 8
                    for n in range(NI + 3):
                        if n < NI:
                            if n % per_h == 0:
                                hh = n // per_h
                                if hh + 1 < 8:
                                    loadK(hh + 1)
                            emitS(n)
                        if 2 <= n < NI + 2:
                            emitE(n - 2)
                        if n >= 3:
                            emitPV(n - 3)
                    if dbg:
                        T.dma("sp", lambda e: e.dma_start(out=dbgo["d_yatt"], in_=yatt[:].rearrange("p t c -> p (t c)")),
                              [b_yatt], [], "dbg")
                T.barrier()
                psum_std()

            T.barrier()
            with ExitStack() as stD:
                Wu = sb(stD, "Wu", [128, 8, 512], BF16); b_Wu = Buf()
                Wv2 = sb(stD, "Wv2", [128, 8, 512], BF16); b_Wv2 = Buf()
                Wg = sb(stD, "Wg", [128, 8, 2048], BF16); b_Wg = Buf()
                Wbs = sb(stD, "Wbs", [128, 4, D], BF16); b_Wbs = Buf()
                Wba = sb(stD, "Wba", [128, 4, D], BF16); b_Wba = Buf()
                Wout = sb(stD, "Wout", [128, 8, D], BF16); b_Wout = Buf()
                wload2(Wu[:], "wu", b_Wu)
                wload2(Wv2[:], "wv2", b_Wv2)
                gpost = sb(stD, "gpost", [128, D], F32); b_gpost = Buf()
                gsg = sb(stD, "gsg", [128, 512], F32); b_gsg = Buf()
                bsg = sb(stD, "bsg", [128, 512], F32); b_bsg = Buf()
                bsp = sb(stD, "bsp", [128, 4, 128], F32); b_bsp = Buf()
                wsTm = sb(stD, "wsTm", [128, 8, 128], BF16); b_wsTm = Buf()
                T.dma("sp", lambda e: e.dma_start(out=gpost[:], in_=g_post.partition_broadcast(128)), [], [b_gpost], "c2")
                T.dma("sp", lambda e: e.dma_start(out=gsg[:], in_=g_sgu.partition_broadcast(128)), [], [b_gsg], "c2")
                T.dma("sp", lambda e: e.dma_start(out=bsg[:], in_=b_sgu.partition_broadcast(128)), [], [b_bsg], "c2")
                T.dma("sp", lambda e: e.dma_start(out=bsp[:], in_=bsp4.rearrange("p (k t) -> p k t", k=4)[:, :, 0:128]), [], [b_bsp], "c2")
                h1t = sbr(stD, "h1t", [128, D], F32, 2)
                T.dma("sp", lambda e: e.dma_start(out=h1t[0][0][:], in_=wsT), [], [h1t[0][1]], "c2")
                T.op("dve", lambda e: e.tensor_tensor(out=wsTm[:], in0=h1t[0][0][:].rearrange("p (g t) -> p g t", g=8),
                                                      in1=triU[:].unsqueeze(1).broadcast_to([128, 8, 128]), op=ALU.mult),
                     [h1t[0][1], b_triU], [b_wsTm])

                xg = sbr(stD, "xg", [128, D], F32, 2)
                xsd = sbr(stD, "xsd", [128, D], BF16, 2)
                xsTg = sbr(stD, "xsTg", [128, 8, 512], BF16, 2)
                uT = sbr(stD, "uT", [128, 4, 512], BF16, 1)
                vg = sbr(stD, "vg", [128, 512], F32, 4)
                vnb = sbr(stD, "vnb", [128, 512], BF16, 4)
                lnst = sbr(stD, "lnst", [128, 16], F32, 2)
                t1 = sbr(stD, "t1", [128, 512], F32, 1)
                ysT = sbr(stD, "ysT", [128, 4, 512], BF16, 2)
                yaT = sbr(stD, "yaT", [128, 4, 512], BF16, 2)
                tg1 = sbr(stD, "tg1", [128, 512], F32, 2)
                tg2 = sbr(stD, "tg2", [128, 512], F32, 2)
                mT = sbr(stD, "mT", [128, 8, 512], BF16, 1)
                bank_n = [0]

                def bank():
                    r = pf[bank_n[0] % 6]
                    bank_n[0] += 1
                    return r

                def rstd_batch(s, b_s, n, src0, dst0, scale, extra_bias=0.0):
                    T.op("act", lambda e: e.activation(out=s[:, dst0:dst0 + n], in_=s[:, src0:src0 + n], func=AF.Ln,
                                                       scale=scale, bias=EPS), [b_s], [b_s])
                    T.op("act", lambda e: e.activation(out=s[:, dst0:dst0 + n], in_=s[:, dst0:dst0 + n], func=AF.Exp,
                                                       scale=-0.5, bias=extra_bias), [b_s], [b_s])

                def genX(go):
                    dT, b_dT = xsTg[go % 2]
                    for pr in range(2):
                        s, b_s = stat[fr_n[0] % 4]
                        fr_n[0] += 1
                        T.op("pool", lambda e: e.memset(s[:, 0:2], 0.0), [], [b_s])
                        for q in range(2):
                            l = 2 * pr + q
                            i = 4 * go + l
                            xt, b_x = xg[q]
                            T.dma("sp", lambda e, xt=xt, i=i: e.dma_start(out=xt[:], in_=xo[i * 128:(i + 1) * 128, :]),
                                  [], [b_x], f"xg{q}")
                            T.op("act", lambda e, xt=xt, q=q: e.activation(out=junk[:], in_=xt[:], func=AF.Square,
                                                                           accum_out=s[:, q:q + 1]), [b_x], [b_junk, b_s])
                        rstd_batch(s, b_s, 2, 0, 2, 1.0 / D)
                        for q in range(2):
                            l = 2 * pr + q
                            xt, b_x = xg[q]
                            xst, b_xs = xsd[q]
                            T.op("dve", lambda e, xt=xt, xst=xst, q=q: e.scalar_tensor_tensor(
                                out=xst[:], in0=xt[:], scalar=s[:, 2 + q:3 + q], in1=gbc[:], op0=ALU.mult, op1=ALU.mult),
                                [b_x, b_s, b_gbc], [b_xs])
                        yield True
                        for q in range(2):
                            l = 2 * pr + q
                            xst, b_xs = xsd[q]
                            transpose8(xst, b_xs, dT[:, :, l * 128:(l + 1) * 128], b_dT, q, "act" if q == 0 else "dve")
                        yield
                    ut, b_ut = uT[0]
                    for c in range(4):
                        pu, b_pu = bank()
                        for kc in range(8):
                            T.op("pe", lambda e, kc=kc, c=c, pu=pu: e.matmul(
                                pu[:], lhsT=Wu[:, kc, c * 128:(c + 1) * 128], rhs=dT[:, kc, :],
                                start=(kc == 0), stop=(kc == 7)), [b_Wu, b_dT], [b_pu], signal=(kc == 7))
                        T.op("act", lambda e, c=c, pu=pu: e.activation(out=ut[:, c, :], in_=pu[:], func=AF.Gelu_apprx_tanh),
                             [b_pu], [b_ut])
                        yield
                    ls, b_ls = lnst[go % 2]
                    T.op("pool", lambda e: e.memset(ls[:], 0.0), [], [b_ls])
                    for l in range(4):
                        pv, b_pv = bank()
                        for kc in range(8):
                            T.op("pe", lambda e, kc=kc, l=l, pv=pv: e.matmul(
                                pv[:], lhsT=dT[:, kc, l * 128:(l + 1) * 128], rhs=Wv2[:, kc, :],
                                start=(kc == 0), stop=(kc == 7)), [b_Wv2, b_dT], [b_pv], signal=(kc == 7))
                        v_, b_v = vg[l]
                        T.op("act", lambda e, l=l, v_=v_, pv=pv: e.activation(
                            out=v_[:], in_=pv[:], func=AF.Gelu_apprx_tanh, accum_out=ls[:, l:l + 1]), [b_pv], [b_v, b_ls])
                        T.op("act", lambda e, l=l, v_=v_: e.activation(out=junk[:, 0:512], in_=v_[:], func=AF.Square,
                                                                       accum_out=ls[:, 4 + l:5 + l]), [b_v], [b_junk, b_ls])
                        yield
                    T.op("dve", lambda e: e.tensor_scalar(out=ls[:, 0:4], in0=ls[:, 0:4], scalar1=1.0 / 512, scalar2=None,
                                                          op0=ALU.mult), [b_ls], [b_ls])
                    T.op("dve", lambda e: e.tensor_tensor(out=ls[:, 8:12], in0=ls[:, 0:4], in1=ls[:, 0:4], op=ALU.mult),
                         [b_ls], [b_ls])
                    T.op("dve", lambda e: e.scalar_tensor_tensor(out=ls[:, 4:8], in0=ls[:, 4:8], scalar=1.0 / 512,
                                                                 in1=ls[:, 8:12], op0=ALU.mult, op1=ALU.subtract),
                         [b_ls], [b_ls])
                    rstd_batch(ls, b_ls, 4, 4, 12, 1.0)
                    for l in range(4):
                        v_, b_v = vg[l]
                        vb, b_vb = vnb[l]
                        T.op("dve", lambda e, l=l, v_=v_: e.tensor_scalar(
                            out=v_[:], in0=v_[:], scalar1=ls[:, l:l + 1], scalar2=ls[:, 12 + l:13 + l],
                            op0=ALU.subtract, op1=ALU.mult), [b_v, b_ls], [b_v])
                        T.op("dve", lambda e, v_=v_: e.tensor_tensor(out=v_[:], in0=v_[:], in1=gsg[:], op=ALU.mult),
                             [b_v, b_gsg], [b_v])
                        T.op("pool", lambda e, v_=v_, vb=vb: e.tensor_tensor(out=vb[:], in0=v_[:], in1=bsg[:], op=ALU.add),
                             [b_v, b_bsg], [b_vb])
                    yield True
                    ys, b_ys = ysT[go % 2]
                    for k in range(4):
                        pA, b_pA = bank()
                        pB, b_pB = bank()
                        for l in range(4):
                            vb, b_vb = vnb[l]
                            T.op("pe", lambda e, l=l, k=k, vb=vb, pA=pA: e.matmul(
                                pA[:, l * 128:(l + 1) * 128], lhsT=vb[:, k * 128:(k + 1) * 128], rhs=wsTm[:, 2 * k, :],
                                start=True, stop=True), [b_vb, b_wsTm], [b_pA], signal=(l == 3))
                        for l in range(4):
                            vb, b_vb = vnb[l]
                            T.op("pe", lambda e, l=l, k=k, vb=vb, pB=pB: e.matmul(
                                pB[:, l * 128:(l + 1) * 128], lhsT=vb[:, k * 128:(k + 1) * 128], rhs=wsTm[:, 2 * k + 1, :],
                                start=True, stop=True), [b_vb, b_wsTm], [b_pB], signal=(l == 3))
                        tt, b_tt = t1[0]
                        T.op("dve", lambda e, k=k, tt=tt, pA=pA: e.tensor_tensor(
                            out=tt[0:64, :].rearrange("p (l t) -> p l t", l=4), in0=pA[0:64, :].rearrange("p (l t) -> p l t", l=4),
                            in1=bsp[0:64, k, :].unsqueeze(1).broadcast_to([64, 4, 128]), op=ALU.add), [b_pA, b_bsp], [b_tt])
                        T.op("dve", lambda e, k=k, tt=tt, pB=pB: e.tensor_tensor(
                            out=tt[64:128, :].rearrange("p (l t) -> p l t", l=4), in0=pB[64:128, :].rearrange("p (l t) -> p l t", l=4),
                            in1=bsp[64:128, k, :].unsqueeze(1).broadcast_to([64, 4, 128]), op=ALU.add), [b_pB, b_bsp], [b_tt])
                        T.op("pool", lambda e, k=k, tt=tt: e.tensor_tensor(
                            out=ys[:, k, :], in0=tt[:], in1=ut[:, k, :], op=ALU.mult), [b_tt, b_ut], [b_ys])
                        yield
                    ya, b_ya = yaT[go % 2]
                    for l in range(4):
                        i = 4 * go + l
                        transpose8(yatt[:, i, :], b_yatt, ya[:, :, l * 128:(l + 1) * 128], b_ya, l % 2,
                                   "act" if l % 2 == 0 else "dve", n=4)
                        yield

                def genY(go):
                    dT, b_dT = xsTg[go % 2]
                    ys, b_ys = ysT[go % 2]
                    ya, b_ya = yaT[go % 2]
                    mt, b_mt = mT[0]
                    for oc in range(8):
                        pG1, b_pG1 = bank()
                        pG2, b_pG2 = bank()
                        pP1, b_pP1 = bank()
                        pP2, b_pP2 = bank()
                        for kc in range(8):
                            T.op("pe", lambda e, kc=kc, oc=oc, pG1=pG1: e.matmul(
                                pG1[:], lhsT=Wg[:, kc, oc * 128:(oc + 1) * 128], rhs=dT[:, kc, :],
                                start=(kc == 0), stop=(kc == 7)), [b_Wg, b_dT], [b_pG1], signal=(kc == 7))
                        for kc in range(8):
                            T.op("pe", lambda e, kc=kc, oc=oc, pG2=pG2: e.matmul(
                                pG2[:], lhsT=Wg[:, kc, 1024 + oc * 128:1024 + (oc + 1) * 128], rhs=dT[:, kc, :],
                                start=(kc == 0), stop=(kc == 7)), [b_Wg, b_dT], [b_pG2], signal=(kc == 7))
                        for c in range(4):
                            T.op("pe", lambda e, c=c, oc=oc, pP1=pP1: e.matmul(
                                pP1[:], lhsT=Wbs[:, c, oc * 128:(oc + 1) * 128], rhs=ys[:, c, :],
                                start=(c == 0), stop=(c == 3)), [b_Wbs, b_ys], [b_pP1], signal=(c == 3))
                        for c in range(4):
                            T.op("pe", lambda e, c=c, oc=oc, pP2=pP2: e.matmul(
                                pP2[:], lhsT=Wba[:, c, oc * 128:(oc + 1) * 128], rhs=ya[:, c, :],
                                start=(c == 0), stop=(c == 3)), [b_Wba, b_ya], [b_pP2], signal=(c == 3))
                        a1, b_a1 = tg1[oc % 2]
                        a2, b_a2 = tg2[oc % 2]
                        T.op("act", lambda e, a1=a1, pG1=pG1: e.activation(out=a1[:], in_=pG1[:], func=AF.Tanh, scale=0.5),
                             [b_pG1], [b_a1])
                        T.op("act", lambda e, a2=a2, pG2=pG2: e.activation(out=a2[:], in_=pG2[:], func=AF.Tanh, scale=0.5),
                             [b_pG2], [b_a2])
                        T.op("dve", lambda e, a1=a1, pP1=pP1: e.scalar_tensor_tensor(
                            out=a1[:], in0=a1[:], scalar=1.0, in1=pP1[:], op0=ALU.add, op1=ALU.mult),
                            [b_a1, b_pP1], [b_a1])
                        T.op("dve", lambda e, a2=a2, pP2=pP2: e.scalar_tensor_tensor(
                            out=a2[:], in0=a2[:], scalar=1.0, in1=pP2[:], op0=ALU.add, op1=ALU.mult),
                            [b_a2, b_pP2], [b_a2])
                        T.op("pool", lambda e, a1=a1, a2=a2, oc=oc: e.tensor_tensor(
                            out=mt[:, oc, :], in0=a1[:], in1=a2[:], op=ALU.add), [b_a1, b_a2], [b_mt])
                        yield
                    for l in range(4):
                        i = 4 * go + l
                        s, b_s = stat[fr_n[0] % 4]
                        fr_n[0] += 1
                        T.op("pool", lambda e, s=s: e.memset(s[:, 0:2], 0.0), [], [b_s])
                        banks = []
                        for half in range(2):
                            pH, b_pH = bank()
                            banks.append((pH, b_pH))
                            for oc in range(8):
                                T.op("pe", lambda e, oc=oc, l=l, half=half, pH=pH: e.matmul(
                                    pH[:], lhsT=mt[:, oc, l * 128:(l + 1) * 128],
                                    rhs=Wout[:, oc, half * 512:(half + 1) * 512],
                                    start=(oc == 0), stop=(oc == 7)), [b_mt, b_Wout], [b_pH], signal=(oc == 7))
                            T.op("act", lambda e, s=s, pH=pH, half=half: e.activation(
                                out=junk[:, half * 512:(half + 1) * 512], in_=pH[:], func=AF.Square, scale=0.5,
                                accum_out=s[:, half:half + 1]), [b_pH], [b_junk, b_s])
                        T.op("dve", lambda e, s=s: e.tensor_tensor(out=s[:, 2:3], in0=s[:, 0:1], in1=s[:, 1:2], op=ALU.add),
                             [b_s], [b_s])
                        rstd_batch(s, b_s, 1, 2, 3, 1.0 / D, extra_bias=math.log(0.5))
                        ht, b_ht = h1t[l % 2]
                        for half in range(2):
                            pH, b_pH = banks[half]
                            T.op("dve", lambda e, s=s, ht=ht, pH=pH, half=half: e.scalar_tensor_tensor(
                                out=ht[:, half * 512:(half + 1) * 512], in0=pH[:], scalar=s[:, 3:4],
                                in1=gpost[:, half * 512:(half + 1) * 512], op0=ALU.mult, op1=ALU.mult),
                                [b_pH, b_s, b_gpost], [b_ht])
                        T.dma("sp", lambda e, ht=ht, i=i: e.dma_start(out=h1scr[i * 128:(i + 1) * 128, :], in_=ht[:]),
                              [b_ht], [b_h1scr], f"h1s{l % 2}")
                        if dbg:
                            T.dma("sp", lambda e, ht=ht, i=i: e.dma_start(out=dbgo["d_h1"][i * 128:(i + 1) * 128, :], in_=ht[:]),
                                  [b_ht], [], "dbg")
                        yield

                def run_interleaved(ga, gb, nb_per_a):
                    da = db = False
                    while not (da and db):
                        if not da:
                            try:
                                next(ga)
                            except StopIteration:
                                da = True
                        for _ in range(nb_per_a):
                            if db:
                                break
                            try:
                                if next(gb) is True:
                                    break
                            except StopIteration:
                                db = True
                        if da and not db:
                            for _ in gb:
                                pass
                            db = True

                gx0 = genX(0)
                next(gx0)
                next(gx0)
                wload2(Wg[:], "wg", b_Wg, nsplit=4)
                wload2(Wbs[:], "wbs", b_Wbs)
                wload2(Wba[:], "wba", b_Wba)
                wload2(Wout[:], "wout", b_Wout)
                for _ in gx0:
                    pass
                for go in range(NGO):
                    gy = genY(go)
                    gx = genX(go + 1) if go + 1 < NGO else iter(())
                    run_interleaved(gy, gx, 2)

        T.barrier()
        with ExitStack() as stE:
            Wup = sb(stE, "Wup", [128, 8, 8, 512], BF16)
            Wdn = sb(stE, "Wdn", [128, 32, D], BF16)
            b_Wup = [Buf() for _ in range(8)]
            b_Wdn = [Buf() for _ in range(8)]
            gpost2 = sb(stE, "gpost2", [128, D], F32); b_gpost2 = Buf()
            T.dma("sp", lambda e: e.dma_start(out=gpost2[:], in_=g_fpost.partition_broadcast(128)), [], [b_gpost2], "c3")
            T.dma("sp", lambda e: e.dma_start(out=gbc[:], in_=g_fpre.partition_broadcast(128)), [], [b_gbc], "c3")
            hg = sbr(stE, "hg", [128, D], F32, 4)
            xe = sbr(stE, "xe", [128, D], F32, 2)
            xs2 = sbr(stE, "xs2", [128, D], BF16, 2)
            x2T = sbr(stE, "x2T", [128, 8, 256], BF16, 2)
            rt = sbr(stE, "rt", [128, 256], F32, 3)
            hT = sbr(stE, "hT", [128, 32, 256], BF16, 1)
            ot = sbr(stE, "ot", [128, D], F32, 1)
            NSG = NO // 2
            bank_n = [0]

            def bank():
                r = pf[bank_n[0] % 6]
                bank_n[0] += 1
                return r

            def frontE(sg):
                dT, b_dT = x2T[sg % 2]
                s, b_s = stat[fr_n[0] % 4]
                fr_n[0] += 1
                T.op("pool", lambda e: e.memset(s[:, 0:2], 0.0), [], [b_s])
                for l in range(2):
                    i = 2 * sg + l
                    ht, b_h = hg[i % 4]
                    xt, b_x = xe[l]
                    T.dma("sp", lambda e, ht=ht, i=i: e.dma_start(out=ht[:], in_=h1scr[i * 128:(i + 1) * 128, :]),
                          [b_h1scr], [b_h], f"hg{i % 4}")
                    T.dma("sp", lambda e, xt=xt, i=i: e.dma_start(out=xt[:], in_=xo[i * 128:(i + 1) * 128, :]),
                          [], [b_x], f"xe{l}")
                    T.op("dve", lambda e, ht=ht, xt=xt: e.tensor_tensor(out=ht[:], in0=ht[:], in1=xt[:], op=ALU.add),
                         [b_h, b_x], [b_h])
                    T.op("act", lambda e, ht=ht, l=l: e.activation(out=junk[:], in_=ht[:], func=AF.Square,
                                                                   accum_out=s[:, l:l + 1]), [b_h], [b_junk, b_s])
                T.op("act", lambda e: e.activation(out=s[:, 2:4], in_=s[:, 0:2], func=AF.Ln, scale=1.0 / D, bias=EPS),
                     [b_s], [b_s])
                T.op("act", lambda e: e.activation(out=s[:, 2:4], in_=s[:, 2:4], func=AF.Exp, scale=-0.5), [b_s], [b_s])
                for l in range(2):
                    i = 2 * sg + l
                    ht, b_h = hg[i % 4]
                    xst, b_xs = xs2[l]
                    T.op("dve", lambda e, ht=ht, xst=xst, l=l: e.scalar_tensor_tensor(
                        out=xst[:], in0=ht[:], scalar=s[:, 2 + l:3 + l], in1=gbc[:], op0=ALU.mult, op1=ALU.mult),
                        [b_h, b_s, b_gbc], [b_xs])

            def frontET(sg):
                dT, b_dT = x2T[sg % 2]
                for l in range(2):
                    xst, b_xs = xs2[l]
                    transpose8(xst, b_xs, dT[:, :, l * 128:(l + 1) * 128], b_dT, l, "act" if l == 0 else "dve")

            def upE(sg, mid=None):
                dT, b_dT = x2T[sg % 2]
                h_, b_hT = hT[0]
                for j in range(32):
                    if j == 12 and mid is not None:
                        mid()
                    pu, b_pu = bank()
                    for kc in range(8):
                        T.op("pe", lambda e, kc=kc, j=j, pu=pu: e.matmul(
                            pu[:, 0:256], lhsT=Wup[:, j // 4, kc, (j % 4) * 128:(j % 4 + 1) * 128], rhs=dT[:, kc, :],
                            start=(kc == 0), stop=(kc == 7)), [b_Wup[j // 4], b_dT], [b_pu], signal=(kc == 7))
                    r_, b_r = rt[j % 3]
                    T.op("act", lambda e, r_=r_, pu=pu: e.activation(out=r_[:], in_=pu[:, 0:256], func=AF.Relu),
                         [b_pu], [b_r])
                    T.op("dve" if j % 2 == 0 else "pool", lambda e, r_=r_, j=j: e.tensor_tensor(
                        out=h_[:, j, :], in0=r_[:], in1=r_[:], op=ALU.mult), [b_r], [b_hT])

            def downE(sg):
                h_, b_hT = hT[0]
                for l in range(2):
                    i = 2 * sg + l
                    pD0, b_pD0 = bank()
                    pD1, b_pD1 = bank()
                    for half, (pD, b_pD) in enumerate(((pD0, b_pD0), (pD1, b_pD1))):
                        for j in range(32):
                            T.op("pe", lambda e, j=j, l=l, half=half, pD=pD: e.matmul(
                                pD[:], lhsT=h_[:, j, l * 128:(l + 1) * 128], rhs=Wdn[:, j, half * 512:(half + 1) * 512],
                                start=(j == 0), stop=(j == 31)), [b_hT, b_Wdn[j // 4]], [b_pD], signal=(j == 31))
                    s, b_s = stat[fr_n[0] % 4]
                    fr_n[0] += 1
                    T.op("pool", lambda e, s=s: e.memset(s[:, 0:2], 0.0), [], [b_s])
                    T.op("act", lambda e, s=s, pD0=pD0: e.activation(out=junk[:, 0:512], in_=pD0[:], func=AF.Square,
                                                                     accum_out=s[:, 0:1]), [b_pD0], [b_junk, b_s])
                    T.op("act", lambda e, s=s, pD1=pD1: e.activation(out=junk[:, 512:1024], in_=pD1[:], func=AF.Square,
                                                                     accum_out=s[:, 1:2]), [b_pD1], [b_junk, b_s])
                    T.op("dve", lambda e, s=s: e.tensor_tensor(out=s[:, 2:3], in0=s[:, 0:1], in1=s[:, 1:2], op=ALU.add),
                         [b_s], [b_s])
                    T.op("act", lambda e, s=s: e.activation(out=s[:, 3:4], in_=s[:, 2:3], func=AF.Ln, scale=1.0 / D, bias=EPS),
                         [b_s], [b_s])
                    T.op("act", lambda e, s=s: e.activation(out=s[:, 4:5], in_=s[:, 3:4], func=AF.Exp, scale=-0.5), [b_s], [b_s])
                    o_, b_o = ot[0]
                    ht, b_h = hg[i % 4]
                    T.op("dve", lambda e, s=s, o_=o_, pD0=pD0: e.scalar_tensor_tensor(
                        out=o_[:, 0:512], in0=pD0[:], scalar=s[:, 4:5], in1=gpost2[:, 0:512], op0=ALU.mult, op1=ALU.mult),
                        [b_pD0, b_s, b_gpost2], [b_o])
                    T.op("dve", lambda e, s=s, o_=o_, pD1=pD1: e.scalar_tensor_tensor(
                        out=o_[:, 512:1024], in0=pD1[:], scalar=s[:, 4:5], in1=gpost2[:, 512:1024], op0=ALU.mult, op1=ALU.mult),
                        [b_pD1, b_s, b_gpost2], [b_o])
                    T.op("dve", lambda e, o_=o_, ht=ht: e.tensor_tensor(out=o_[:], in0=o_[:], in1=ht[:], op=ALU.add),
                         [b_o, b_h], [b_o])
                    T.dma("sp", lambda e, o_=o_, i=i: e.dma_start(out=out[i * 128:(i + 1) * 128, :], in_=o_[:]),
                          [b_o], [], f"ost{i % 2}")

            wupv = wscr["wup"].rearrange("p (pc kc n) -> p pc kc n", pc=8, kc=8)
            wdnv = wscr["wdn"].rearrange("p (kc n) -> p kc n", kc=32)
            for p in range(8):
                T.dma("act", lambda e, p=p: e.dma_start(
                    out=Wdn[:, p * 4:(p + 1) * 4, :], in_=wdnv[:, p * 4:(p + 1) * 4, :]),
                    [b_wscr["wdn"]], [b_Wdn[p]], f"w_dn{p}")
            frontE(0)
            for p in range(8):
                T.dma("sp", lambda e, p=p: e.dma_start(out=Wup[:, p, :, :], in_=wupv[:, p, :, :]),
                      [b_wscr["wup"]], [b_Wup[p]], f"w_up{p}")
            frontET(0)
            for sg in range(NSG):
                upE(sg, (lambda sg=sg: frontE(sg + 1)) if sg + 1 < NSG else None)
                if sg + 1 < NSG:
                    frontET(sg + 1)
                downE(sg)
            for k in ("ost0", "ost1", "dbg"):
                if k in T.semobj:
                    nc.sync.wait_ge(T.semobj[k], T.cnt[k])
        psum_free()
        print(f"[build] ops={T.nops} waits={T.nwaits} sems={len(T.semobj)}")
    return nc


def _host_inputs(inputs, NT):
    f = lambda a: np.ascontiguousarray(np.asarray(a, dtype=np.float32))
    x = f(inputs["x"])
    w_in = f(inputs["w_in"])[0]
    wz, wq_, wk_, wv_, wf_, wg_ = np.split(w_in, [1024, 1536, 2048, 2560, 2568], axis=1)
    common = {
        "wkvf": f(np.concatenate([wk_, wv_, wf_], axis=1)),
        "wq": f(wq_),
        "wu": f(wz[:, :512]),
        "wv2": f(wz[:, 512:]),
        "wg": f(wg_),
        "wbs": f(inputs["w_branch_sgu"])[0],
        "wba": f(inputs["w_branch_attn"])[0],
        "wout": f(inputs["w_out"])[0],
        "wup": f(inputs["w_up"])[0],
        "wdn": f(inputs["w_down"])[0],
        "g_pre": f(inputs["g_mix_pre"]),
        "g_post": f(inputs["g_mix_post"]),
        "g_fpre": f(inputs["g_ffn_pre"]),
        "g_fpost": f(inputs["g_ffn_post"]),
        "g_sgu": f(inputs["g_sgu"]),
        "b_sgu": f(inputs["b_sgu"]),
        "b_fg": f(inputs["b_forget"]),
        "wsT": f(np.transpose(f(inputs["w_spatial"])[0], (2, 0, 1)).reshape(128, 8 * 128)),
        "c_id": np.eye(128, dtype=np.float32),
        "c_tri": np.triu(np.ones((128, 128), np.float32)),
        "c_one": np.ones((128, 128), np.float32),
    }
    bs = f(inputs["b_spatial"])[0]
    bsp = np.repeat(bs.reshape(4, 2, 1, 128), 64, axis=2).reshape(4, 128, 128)
    bsp4 = np.tile(bsp[:, :, None, :], (1, 1, 4, 1)).reshape(4, 128, 512)
    common["bsp4"] = f(np.transpose(bsp4, (1, 0, 2)).reshape(128, 4 * 512))
    maps = []
    s_idx = np.arange(128)[:, None]
    t_idx = np.arange(128)[None, :]
    for c in range(8):
        b, j = c // 4, c % 4
        xb = x[b]
        xo = xb.reshape(NT, 128, D)[j::4].reshape(-1, D)
        sel = np.zeros((128, 4), np.float32)
        sel[:, j] = 1.0
        madd = np.zeros((128, 4, 128), np.float32)
        for m in range(4):
            if m == j:
                madd[:, m, :] = np.where(s_idx <= t_idx, 0.0, NEG)
            elif m > j:
                madd[:, m, :] = NEG
        d = dict(common)
        d["xf"] = f(xb)
        d["xo"] = f(xo)
        d["c_sel"] = sel
        d["c_madd"] = madd.reshape(128, 512)
        maps.append(d)
    return maps


_NC_CACHE = {}


def kernel(**inputs):
    x = np.asarray(inputs["x"])
    B, S, _ = x.shape
    NT = S // 128
    if NT not in _NC_CACHE:
        _NC_CACHE[NT] = build_nc(NT)
    nc = _NC_CACHE[NT]
    maps = _host_inputs(inputs, NT)
    res = run_bass_kernel_spmd(nc, maps, core_ids=list(range(8)))
    out = np.zeros((B, NT, 128, D), np.float32)
    for c in range(8):
        b, j = c // 4, c % 4
        out[b, j::4] = np.asarray(res.results[c]["out"], dtype=np.float32).reshape(NT // 4, 128, D)
    return out.reshape(B, S, D)
```

```python
from contextlib import ExitStack
import math
import numpy as np
import concourse.bass as bass
import concourse.mybir as mybir
from concourse.bass_utils import run_bass_kernel_spmd

F32 = mybir.dt.float32
BF16 = mybir.dt.bfloat16
AF = mybir.ActivationFunctionType
ALU = mybir.AluOpType

D = 1024
NEG = -1.0e5
EPS = 1e-6


class Buf:
    def __init__(self, name=""):
        self.name = name
        self.w = None
        self.r = {}


class Tracker:
    def __init__(self, nc, stack):
        self.nc = nc
        self.stack = stack
        self.engs = {"pe": nc.tensor, "act": nc.scalar, "dve": nc.vector,
                     "pool": nc.gpsimd, "sp": nc.sync}
        self.semobj = {}
        self.cnt = {}
        for k in ["pe", "act", "dve", "pool"]:
            self.semobj[k] = stack.enter_context(nc.semaphore("s_" + k))
            self.cnt[k] = 0
        self.waited = {k: {} for k in self.engs}
        self.shared = {"c0", "c1", "c2", "c3"}
        self.nwaits = 0
        self.nops = 0

    def _wait(self, eng, ev):
        if ev is None:
            return
        key, val = ev
        if key in self.shared:
            val = self.cnt[key]
        if key == eng and eng == "pe":
            return
        if self.waited[eng].get(key, 0) >= val:
            return
        self.engs[eng].wait_ge(self.semobj[key], val)
        self.waited[eng][key] = val
        self.nwaits += 1

    def _deps(self, eng, reads, writes):
        for b in reads:
            self._wait(eng, b.w)
        for b in writes:
            self._wait(eng, b.w)
            for k, v in b.r.items():
                self._wait(eng, (k, v))

    def _record(self, ev, reads, writes):
        k, v = ev
        for b in reads:
            if b.r.get(k, 0) < v:
                b.r[k] = v
        for b in writes:
            b.w = ev
            b.r = {}

    def op(self, eng, fn, reads=(), writes=(), signal=True):
        self._deps(eng, reads, writes)
        ins = fn(self.engs[eng])
        self.nops += 1
        if signal:
            self.cnt[eng] += 1
            ins.then_inc(self.semobj[eng], 1)
            ev = (eng, self.cnt[eng])
        else:
            ev = (eng, self.cnt[eng] + 1)
        self._record(ev, reads, writes)
        return ev

    def barrier(self):
        for eng in self.engs:
            for k in list(self.semobj.keys()):
                if self.cnt[k] > 0:
                    self._wait(eng, (k, self.cnt[k]))

    def dma(self, q, fn, reads, writes, sem):
        if sem not in self.semobj:
            self.semobj[sem] = self.stack.enter_context(self.nc.semaphore("d_" + sem))
            self.cnt[sem] = 0
        self._deps(q, reads, writes)
        ins = fn(self.engs[q])
        self.nops += 1
        self.cnt[sem] += 16
        ins.then_inc(self.semobj[sem], 16)
        ev = (sem, self.cnt[sem])
        self._record(ev, reads, writes)
        return ev


def build_nc(NT, dbg=False):
    NO = NT // 4
    NGA = NT // 4
    NGO = NO // 4
    S = NT * 128
    SO = NO * 128
    nc = bass.Bass("TRN2", target_bir_lowering=False)

    def din(name, shape):
        return nc.dram_tensor(name, shape, F32, kind="ExternalInput").ap()

    xf = din("xf", [S, D])
    xo = din("xo", [SO, D])
    wkvf = din("wkvf", [D, 1032])
    wq = din("wq", [D, 512])
    wu = din("wu", [D, 512])
    wv2 = din("wv2", [D, 512])
    wg = din("wg", [D, 2048])
    wbs = din("wbs", [512, D])
    wba = din("wba", [512, D])
    wout = din("wout", [D, D])
    wup = din("wup", [D, 4096])
    wdn = din("wdn", [4096, D])
    g_pre = din("g_pre", [1, D])
    g_post = din("g_post", [1, D])
    g_fpre = din("g_fpre", [1, D])
    g_fpost = din("g_fpost", [1, D])
    g_sgu = din("g_sgu", [1, 512])
    b_sgu = din("b_sgu", [1, 512])
    b_fg = din("b_fg", [1, 8])
    wsT = din("wsT", [128, 8 * 128])
    bsp4 = din("bsp4", [128, 4 * 512])
    c_id = din("c_id", [128, 128])
    c_tri = din("c_tri", [128, 128])
    c_one = din("c_one", [128, 128])
    c_sel = din("c_sel", [128, 4])
    c_madd = din("c_madd", [128, 4 * 128])
    kscr = nc.dram_tensor("kscr", [512, S], BF16).ap()
    h1scr = nc.dram_tensor("h1scr", [SO, D], F32).ap()
    out = nc.dram_tensor("out", [SO, D], F32, kind="ExternalOutput").ap()
    dbgo = {}
    if dbg:
        dbgo["d_cum"] = nc.dram_tensor("d_cum", [128, NT * 8], F32, kind="ExternalOutput").ap()
        dbgo["d_yatt"] = nc.dram_tensor("d_yatt", [128, NO * 512], BF16, kind="ExternalOutput").ap()
        dbgo["d_h1"] = nc.dram_tensor("d_h1", [SO, D], F32, kind="ExternalOutput").ap()

    with ExitStack() as st0:
        T = Tracker(nc, st0)

        def sb(stk, name, shape, dt):
            return stk.enter_context(nc.sbuf_tensor(name, shape, dt))

        def sbr(stk, name, shape, dt, n):
            return [(sb(stk, f"{name}{i}", shape, dt), Buf(f"{name}{i}")) for i in range(n)]

        def wload(dst, src, buf, sem, kc):
            T.dma("pool", lambda e: e.dma_start(out=dst, in_=src.rearrange("(kc p) n -> p kc n", p=128)),
                  [], [buf], sem)

        wsrc = {"wu": (wu, [D, 512]), "wv2": (wv2, [D, 512]), "wg": (wg, [D, 2048]), "wbs": (wbs, [512, D]),
                "wba": (wba, [512, D]), "wout": (wout, [D, D]), "wup": (wup, [D, 4096]), "wdn": (wdn, [4096, D])}
        wscr = {k: nc.dram_tensor("scr_" + k, [128, shp[0] // 128 * shp[1]], BF16).ap() for k, (_, shp) in wsrc.items()}
        b_wscr = {k: Buf("scr_" + k) for k in wsrc}

        def precast(k):
            src, shp = wsrc[k]
            rows, n = shp
            KC = rows // 128
            if k == "wup":
                dstv = wscr[k].rearrange("p (pc kc n) -> p pc kc n", pc=8, kc=8)
                for pc in range(8):
                    T.dma("pool", lambda e, pc=pc: e.dma_start(
                        out=dstv[:, pc, :, :], in_=src[:, pc * 512:(pc + 1) * 512].rearrange("(kc p) n -> p kc n", p=128)),
                        [], [b_wscr[k]], "pc_" + k)
                return
            dstv = wscr[k].rearrange("p (kc n) -> p kc n", kc=KC)
            step = 4 if KC >= 4 else KC
            for c0 in range(0, KC, step):
                T.dma("pool", lambda e, c0=c0: e.dma_start(
                    out=dstv[:, c0:c0 + step, :],
                    in_=src[c0 * 128:(c0 + step) * 128, :].rearrange("(kc p) n -> p kc n", p=128)),
                    [], [b_wscr[k]], "pc_" + k)

        wl_n = [0]

        def wload2(dst, k, buf, nsplit=2):
            KC = dst.shape[1]
            srcv = wscr[k].rearrange("p (kc n) -> p kc n", kc=KC)
            step = max(1, KC // nsplit)
            for c0 in range(0, KC, step):
                q = "sp" if wl_n[0] % 2 == 0 else "act"
                wl_n[0] += 1
                T.dma(q, lambda e, c0=c0: e.dma_start(out=dst[:, c0:c0 + step, :], in_=srcv[:, c0:c0 + step, :]),
                      [b_wscr[k]], [buf], f"w_{k}")

        b_kscr = Buf("kscr")
        b_h1scr = Buf("h1scr")
        pf = []
        pb = []
        ps_stack = [None]
        ps_gen = [0]

        def psum_std():
            stk = ExitStack()
            ps_stack[0] = stk
            k = ps_gen[0]
            ps_gen[0] += 1
            pf[:] = [(stk.enter_context(nc.psum_tensor(f"pf{k}_{i}", [128, 512], F32)), Buf(f"pf{i}")) for i in range(6)]
            pb[:] = [(stk.enter_context(nc.psum_tensor(f"pb{k}_{i}", [128, 1024], BF16)), Buf(f"pb{i}")) for i in range(2)]

        def psum_free():
            ps_stack[0].close()
            ps_stack[0] = None

        psum_std()

        identb = sb(st0, "identb", [128, 128], BF16); b_identb = Buf()
        triU = sb(st0, "triU", [128, 128], F32); b_triU = Buf()
        onesf = sb(st0, "onesf", [128, 128], F32); b_onesf = Buf()
        gbc = sb(st0, "gbc", [128, D], F32); b_gbc = Buf()
        bfbc = sb(st0, "bfbc", [128, 8], F32); b_bfbc = Buf()
        sel = sb(st0, "sel", [128, 4], F32); b_sel = Buf()
        madd = sb(st0, "madd", [128, 4, 128], F32); b_madd = Buf()
        cumsp = sb(st0, "cumsp", [128, NT, 8], F32); b_cumsp = Buf()
        carry = sb(st0, "carry", [128, 8], F32); b_carry = Buf()
        junk = sb(st0, "junk", [128, D], BF16); b_junk = Buf()
        stat = sbr(st0, "stat", [128, 8], F32, 4)

        T.dma("pool", lambda e: e.dma_start(out=identb[:], in_=c_id), [], [b_identb], "c0")
        T.dma("sp", lambda e: e.dma_start(out=triU[:], in_=c_tri), [], [b_triU], "c1")
        T.dma("sp", lambda e: e.dma_start(out=onesf[:], in_=c_one), [], [b_onesf], "c1")
        T.dma("sp", lambda e: e.dma_start(out=gbc[:], in_=g_pre.partition_broadcast(128)), [], [b_gbc], "c1")
        T.dma("sp", lambda e: e.dma_start(out=bfbc[:], in_=b_fg.partition_broadcast(128)), [], [b_bfbc], "c1")
        T.dma("sp", lambda e: e.dma_start(out=sel[:], in_=c_sel), [], [b_sel], "c1")
        T.dma("sp", lambda e: e.dma_start(out=madd[:], in_=c_madd.rearrange("p (m t) -> p m t", m=4)),
              [], [b_madd], "c1")
        T.op("dve", lambda e: e.memset(carry[:], 0.0), [], [b_carry])

        fr_n = [0]

        def rms_scale(x_ap, b_x, xs_ap, b_xs, eng, pre_scale=1.0, extra_bias=0.0):
            s, b_s = stat[fr_n[0] % 4]
            fr_n[0] += 1
            T.op("dve", lambda e: e.memset(s[:, 0:1], 0.0), [], [b_s])
            T.op("act", lambda e: e.activation(out=junk[:], in_=x_ap, func=AF.Square, scale=pre_scale,
                                               accum_out=s[:, 0:1]), [b_x], [b_junk, b_s])
            T.op("act", lambda e: e.activation(out=s[:, 1:2], in_=s[:, 0:1], func=AF.Ln, scale=1.0 / D, bias=EPS),
                 [b_s], [b_s])
            T.op("act", lambda e: e.activation(out=s[:, 2:3], in_=s[:, 1:2], func=AF.Exp, scale=-0.5,
                                               bias=extra_bias), [b_s], [b_s])
            T.op("dve", lambda e: e.scalar_tensor_tensor(out=xs_ap, in0=x_ap, scalar=s[:, 2:3], in1=gbc[:],
                                                       op0=ALU.mult, op1=ALU.mult), [b_x, b_s, b_gbc], [b_xs])

        def transpose8(xs_t, b_xs, dst_ap, b_dst, pbi, evac_eng, n=8):
            pbt, b_pb = pb[pbi]
            for c in range(n):
                T.op("pe", lambda e, c=c: e.transpose(out=pbt[:, c * 128:(c + 1) * 128],
                                                      in_=xs_t[:, c * 128:(c + 1) * 128], identity=identb[:]),
                     [b_xs, b_identb], [b_pb], signal=(c == n - 1))
            src = pbt[:, 0:n * 128].rearrange("p (c t) -> p c t", c=n)
            if evac_eng == "act":
                T.op("act", lambda e: e.copy(out=dst_ap, in_=src), [b_pb], [b_dst])
            else:
                T.op("dve", lambda e: e.tensor_copy(out=dst_ap, in_=src), [b_pb], [b_dst])

        with ExitStack() as stCD:
            yatt = sb(stCD, "yatt", [128, NO, 512], BF16); b_yatt = Buf()

            with ExitStack() as stABC:
                Vall = sb(stABC, "Vall", [128, NT, 8, 65], BF16); b_Vall = Buf()
                QT = sb(stABC, "QT", [67, 8, SO], BF16); b_QT = Buf()
                T.op("pool", lambda e: e.memset(Vall[:, :, :, 64:65], 1.0), [], [b_Vall])

                with ExitStack() as stAB:
                    Wkvf = sb(stAB, "Wkvf", [128, 8, 1032], BF16); b_Wkvf = Buf()
                    Wq = sb(stAB, "Wq", [128, 8, 512], BF16); b_Wq = Buf()
                    wload(Wkvf[:], wkvf, b_Wkvf, "w_kvf", 8)
                    wload(Wq[:], wq, b_Wq, "w_q", 8)
                    xa = sbr(stAB, "xa", [128, D], F32, 6)
                    xs = sbr(stAB, "xs", [128, D], BF16, 4)
                    xsT = sbr(stAB, "xsT", [128, 8, 512], BF16, 2)
                    kst = sbr(stAB, "kst", [128, 4, 512], BF16, 2)
                    ftmp = sbr(stAB, "ftmp", [128, 32], F32, 2)
                    spt = sbr(stAB, "spt", [128, 32], F32, 2)
                    CS = sbr(stAB, "CS", [128, 8, 67], BF16, 2)
                    co = sbr(stAB, "co", [128, 8], F32, 2)
                    co2 = sbr(stAB, "co2", [128, 8], F32, 2)
                    for c_, b_ in CS:
                        T.op("pool", lambda e, c_=c_: e.memset(c_[:], 0.0), [], [b_])

                    tiles = [("A", t) for t in range(NT)] + [("B", t) for t in range(NO)]
                    NTT = len(tiles)

                    XR = 6
                    GT = NGA + NGO
                    fstat = {}

                    def loadG(g):
                        for l in range(4):
                            i = 4 * g + l
                            kind, t = tiles[i]
                            src = xf if kind == "A" else xo
                            xt, b_x = xa[i % XR]
                            T.dma("sp", lambda e, xt=xt, src=src, t=t: e.dma_start(out=xt[:], in_=src[t * 128:(t + 1) * 128, :]),
                                  [], [b_x], f"xa{i % XR}")

                    def frontG(g):
                        s_, b_s = stat[fr_n[0] % 4]
                        fr_n[0] += 1
                        T.op("dve", lambda e: e.memset(s_[:, 0:4], 0.0), [], [b_s])
                        for l in range(4):
                            xt, b_x = xa[(4 * g + l) % XR]
                            T.op("act", lambda e, xt=xt, l=l: e.activation(out=junk[:], in_=xt[:], func=AF.Square,
                                                                           accum_out=s_[:, l:l + 1]), [b_x], [b_junk, b_s])
                        T.op("act", lambda e: e.activation(out=s_[:, 4:8], in_=s_[:, 0:4], func=AF.Ln, scale=1.0 / D, bias=EPS),
                             [b_s], [b_s])
                        T.op("act", lambda e: e.activation(out=s_[:, 4:8], in_=s_[:, 4:8], func=AF.Exp, scale=-0.5), [b_s], [b_s])
                        fstat[g] = (s_, b_s)

                    def frontG2(g):
                        s_, b_s = fstat[g]
                        for l in range(4):
                            xt, b_x = xa[(4 * g + l) % XR]
                            xst, b_xs = xs[l]
                            T.op("dve", lambda e, xt=xt, xst=xst, l=l: e.scalar_tensor_tensor(
                                out=xst[:], in0=xt[:], scalar=s_[:, 4 + l:5 + l], in1=gbc[:], op0=ALU.mult, op1=ALU.mult),
                                [b_x, b_s, b_gbc], [b_xs])

                    def transG(g):
                        dT, b_dT = xsT[g % 2]
                        for l in range(4):
                            xst, b_xs = xs[l]
                            transpose8(xst, b_xs, dT[:, :, l * 128:(l + 1) * 128], b_dT, l % 2,
                                       "act" if l % 2 == 0 else "dve")

                    def backA(g):
                        dT, b_dT = xsT[g % 2]
                        ks, b_ks = kst[g % 2]
                        for hp in range(4):
                            pk, b_pk = pf[hp % 2]
                            for kc in range(8):
                                T.op("pe", lambda e, kc=kc, hp=hp, pk=pk: e.matmul(
                                    pk[:], lhsT=Wkvf[:, kc, hp * 128:(hp + 1) * 128], rhs=dT[:, kc, :],
                                    start=(kc == 0), stop=(kc == 7)), [b_Wkvf, b_dT], [b_pk], signal=(kc == 7))
                            T.op("dve", lambda e, hp=hp, pk=pk: e.tensor_copy(out=ks[:, hp, :], in_=pk[:]),
                                 [b_pk], [b_ks])
                        T.dma("sp", lambda e: e.dma_start(
                            out=kscr.rearrange("(hp p) s -> p hp s", p=128)[:, :, g * 512:(g + 1) * 512],
                            in_=ks[:]), [b_ks], [b_kscr], f"kst{g % 2}")

                    def backA2(g):
                        dT, b_dT = xsT[g % 2]
                        for l in range(4):
                            pv, b_pv = pf[2 + l % 2]
                            for kc in range(8):
                                T.op("pe", lambda e, kc=kc, l=l, pv=pv: e.matmul(
                                    pv[:], lhsT=dT[:, kc, l * 128:(l + 1) * 128], rhs=Wkvf[:, kc, 512:1024],
                                    start=(kc == 0), stop=(kc == 7)), [b_Wkvf, b_dT], [b_pv], signal=(kc == 7))
                            T.op("act", lambda e, l=l, pv=pv: e.copy(
                                out=Vall[:, 4 * g + l, :, 0:64], in_=pv[:].rearrange("p (h d) -> p h d", h=8)),
                                [b_pv], [b_Vall])
                        pF, b_pF = pf[4]
                        for l in range(4):
                            for kc in range(8):
                                T.op("pe", lambda e, kc=kc, l=l: e.matmul(
                                    pF[:, l * 8:(l + 1) * 8], lhsT=dT[:, kc, l * 128:(l + 1) * 128],
                                    rhs=Wkvf[:, kc, 1024:1032], start=(kc == 0), stop=(kc == 7)),
                                    [b_Wkvf, b_dT], [b_pF], signal=(kc == 7 and l == 3))
                        ft, b_ft = ftmp[g % 2]
                        sp_, b_sp = spt[g % 2]
                        T.op("dve", lambda e: e.tensor_tensor(
                            out=ft[:].rearrange("p (l h) -> p l h", l=4), in0=pF[:, 0:32].rearrange("p (l h) -> p l h", l=4),
                            in1=bfbc[:].unsqueeze(1).broadcast_to([128, 4, 8]), op=ALU.add), [b_pF, b_bfbc], [b_ft])
                        T.op("act", lambda e: e.activation(out=ft[:], in_=ft[:], func=AF.Exp, scale=-1.0), [b_ft], [b_ft])
                        T.op("act", lambda e: e.activation(out=sp_[:], in_=ft[:], func=AF.Ln, bias=1.0), [b_ft], [b_sp])

                    def backA3(g):
                        sp_, b_sp = spt[g % 2]
                        pC, b_pC = pf[5]
                        for l in range(4):
                            for l2 in range(l):
                                T.op("pe", lambda e, l=l, l2=l2: e.matmul(
                                    pC[:, l * 8:(l + 1) * 8], lhsT=onesf[:], rhs=sp_[:, l2 * 8:(l2 + 1) * 8],
                                    start=(l2 == 0), stop=False), [b_onesf, b_sp], [b_pC], signal=False)
                            T.op("pe", lambda e, l=l: e.matmul(
                                pC[:, l * 8:(l + 1) * 8], lhsT=triU[:], rhs=sp_[:, l * 8:(l + 1) * 8],
                                start=(l == 0), stop=True), [b_triU, b_sp], [b_pC], signal=False)
                        for l2 in range(4):
                            T.op("pe", lambda e, l2=l2: e.matmul(
                                pC[:, 32:40], lhsT=onesf[:], rhs=sp_[:, l2 * 8:(l2 + 1) * 8],
                                start=(l2 == 0), stop=(l2 == 3)), [b_onesf, b_sp], [b_pC], signal=(l2 == 3))
                        T.op("dve", lambda e: e.tensor_tensor(
                            out=cumsp[:, 4 * g:4 * g + 4, :], in0=pC[:, 0:32].rearrange("p (l h) -> p l h", l=4),
                            in1=carry[:].unsqueeze(1).broadcast_to([128, 4, 8]), op=ALU.add),
                            [b_pC, b_carry], [b_cumsp])
                        T.op("dve", lambda e: e.tensor_tensor(out=carry[:], in0=carry[:], in1=pC[:, 32:40], op=ALU.add),
                             [b_pC, b_carry], [b_carry])

                    def backB(go):
                        g = NGA + go
                        dT, b_dT = xsT[g % 2]
                        for h in range(8):
                            pq, b_pq = pf[h % 2]
                            for kc in range(8):
                                T.op("pe", lambda e, kc=kc, h=h, pq=pq: e.matmul(
                                    pq[0:64, :], lhsT=Wq[:, kc, h * 64:(h + 1) * 64], rhs=dT[:, kc, :],
                                    start=(kc == 0), stop=(kc == 7)), [b_Wq, b_dT], [b_pq], signal=(kc == 7))
                            if h % 2 == 0:
                                T.op("act", lambda e, h=h, pq=pq: e.copy(
                                    out=QT[0:64, h, go * 512:(go + 1) * 512], in_=pq[0:64, :]), [b_pq], [b_QT])
                            else:
                                T.op("dve", lambda e, h=h, pq=pq: e.tensor_copy(
                                    out=QT[0:64, h, go * 512:(go + 1) * 512], in_=pq[0:64, :]), [b_pq], [b_QT])
                        for l in range(4):
                            i = 4 * go + l
                            c1, b_c1 = co[i % 2]
                            c2, b_c2 = co2[i % 2]
                            cs, b_cs = CS[i % 2]
                            T.op("dve", lambda e, i=i, c1=c1: e.tensor_scalar(
                                out=c1[:], in0=cumsp[:, 4 * i, :], scalar1=sel[:, 0:1], scalar2=None, op0=ALU.mult),
                                [b_cumsp, b_sel], [b_c1])
                            for m in range(1, 4):
                                T.op("dve", lambda e, i=i, m=m, c1=c1: e.scalar_tensor_tensor(
                                    out=c1[:], in0=cumsp[:, 4 * i + m, :], scalar=sel[:, m:m + 1], in1=c1[:],
                                    op0=ALU.mult, op1=ALU.add), [b_cumsp, b_sel, b_c1], [b_c1])
                            T.op("dve", lambda e, c1=c1: e.tensor_scalar(
                                out=c1[:], in0=c1[:], scalar1=-8.0, scalar2=None, op0=ALU.mult), [b_c1], [b_c1])
                            T.op("dve", lambda e, c1=c1, cs=cs: e.tensor_copy(out=cs[:, :, 64], in_=c1[:]), [b_c1], [b_cs])
                            T.op("dve", lambda e, c1=c1, c2=c2, cs=cs: e.tensor_tensor(
                                out=c2[:], in0=c1[:], in1=cs[:, :, 64], op=ALU.subtract), [b_c1, b_cs], [b_c2])
                            T.op("dve", lambda e, c2=c2, cs=cs: e.tensor_copy(out=cs[:, :, 65], in_=c2[:]), [b_c2], [b_cs])
                            T.op("dve", lambda e, c1=c1, c2=c2, cs=cs: e.tensor_tensor(
                                out=c1[:], in0=c2[:], in1=cs[:, :, 65], op=ALU.subtract), [b_c2, b_cs], [b_c1])
                            T.op("dve", lambda e, c1=c1, cs=cs: e.tensor_copy(out=cs[:, :, 66], in_=c1[:]), [b_c1], [b_cs])
                            for half in range(2):
                                pa, b_pa = pf[2 + half]
                                for hh in range(4):
                                    h = half * 4 + hh
                                    T.op("pe", lambda e, h=h, hh=hh, pa=pa, cs=cs: e.matmul(
                                        pa[0:67, hh * 128:(hh + 1) * 128], lhsT=cs[:, h, :], rhs=identb[:],
                                        start=True, stop=True), [b_cs, b_identb], [b_pa], signal=(hh == 3))
                                T.op("act", lambda e, half=half, pa=pa, i=i: e.copy(
                                    out=QT[64:67, half * 4:half * 4 + 4, i * 128:(i + 1) * 128],
                                    in_=pa[64:67, :].rearrange("p (h t) -> p h t", h=4)), [b_pa], [b_QT])

                    loadG(0)
                    frontG(0)
                    frontG2(0)
                    if GT > 1:
                        loadG(1)
                    transG(0)
                    for g in range(GT):
                        if g + 1 < GT:
                            frontG(g + 1)
                        if g < NGA:
                            backA(g)
                        else:
                            backB(g - NGA)
                        if g + 1 < GT:
                            frontG2(g + 1)
                        if g + 2 < GT:
                            loadG(g + 2)
                        if g < NGA:
                            backA2(g)
                        if g + 1 < GT:
                            transG(g + 1)
                        if g < NGA:
                            backA3(g)
                    if dbg:
                        T.dma("sp", lambda e: e.dma_start(out=dbgo["d_cum"], in_=cumsp[:].rearrange("p t h -> p (t h)")),
                              [b_cumsp], [], "dbg")

                T.barrier()
                psum_free()
                with ExitStack() as stC:
                    pS = [(stC.enter_context(nc.psum_tensor(f"pS{i}", [128, 512], F32)), Buf(f"pS{i}")) for i in range(4)]
                    pO = [(stC.enter_context(nc.psum_tensor(f"pO{i}", [128, 512], F32)), Buf(f"pO{i}")) for i in range(4)]
                    KT = sbr(stC, "KT", [67, S], BF16, 2)
                    PT = sbr(stC, "PT", [128, 512], BF16, 4)
                    rec = sbr(stC, "rec", [128, 4], F32, 4)
                    for kt_, b_ in KT:
                        T.op("pool", lambda e, kt_=kt_: e.memset(kt_[64:67, :], 1.0), [], [b_])
                    for k_ in ("wu", "wv2", "wg", "wbs", "wba", "wout", "wup", "wdn"):
                        precast(k_)

                    def loadK(h):
                        kt_, b_ = KT[h % 2]
                        T.dma("sp", lambda e: e.dma_start(out=kt_[0:64, :], in_=kscr[h * 64:(h + 1) * 64, :]),
                              [b_kscr], [b_], f"kt{h % 2}")

                    gsets = [[a] for a in range(NGO)]
                    items = []
                    for h in range(8):
                        for si, gs in enumerate(gsets):
                            for kt in range(16 * gs[-1] + 16):
                                items.append((h, si, kt))

                    def colstart(go, kt):
                        if kt < 16 * go:
                            return 0, None
                        l = kt // 4 - 4 * go
                        return l * 128, kt % 4

                    def active(si, kt):
                        r = []
                        for slot, go in enumerate(gsets[si]):
                            if kt <= 16 * go + 15:
                                c0, m = colstart(go, kt)
                                r.append((slot, go, c0, m))
                        return r

                    def emitS(n):
                        h, si, kt = items[n]
                        kt_, b_k = KT[h % 2]
                        ps_, b_ps = pS[n % 4]
                        for slot, go, c0, m in active(si, kt):
                            T.op("pe", lambda e, slot=slot, go=go, c0=c0: e.matmul(
                                ps_[:, slot * 512 + c0:(slot + 1) * 512], lhsT=kt_[0:67, kt * 128:(kt + 1) * 128],
                                rhs=QT[0:67, h, go * 512 + c0:(go + 1) * 512], start=True, stop=True),
                                [b_k, b_QT], [b_ps])

                    def emitE(n):
                        h, si, kt = items[n]
                        ps_, b_ps = pS[n % 4]
                        pt_, b_pt = PT[n % 4]
                        act = active(si, kt)
                        for slot, go, c0, m in act:
                            if m is not None:
                                lo = slot * 512 + c0
                                T.op("dve", lambda e, lo=lo, m=m: e.tensor_tensor(
                                    out=ps_[:, lo:lo + 128], in0=ps_[:, lo:lo + 128], in1=madd[:, m, :], op=ALU.add),
                                    [b_ps, b_madd], [b_ps])
                        lo = act[0][0] * 512 + act[0][2]
                        hi = (act[-1][0] + 1) * 512
                        T.op("act", lambda e: e.activation(
                            out=pt_[:, lo:hi], in_=ps_[:, lo:hi], func=AF.Exp, scale=0.125,
                            bias=cumsp[:, kt, h:h + 1]), [b_ps, b_cumsp], [b_pt])

                    def emitPV(n):
                        h, si, kt = items[n]
                        pt_, b_pt = PT[n % 4]
                        ring = 2 * ((h * len(gsets) + si) % 2)
                        for slot, go, c0, m in active(si, kt):
                            po, b_po = pO[ring + slot]
                            l0 = c0 // 128
                            for l in range(l0, 4):
                                last = 4 * (4 * go + l) + 3
                                T.op("pe", lambda e, l=l, last=last, slot=slot, po=po: e.matmul(
                                    po[:, l * 65:(l + 1) * 65], lhsT=pt_[:, slot * 512 + l * 128:slot * 512 + (l + 1) * 128],
                                    rhs=Vall[:, kt, h, :], start=(kt == 0 and l == 0), stop=(kt == last),
                                    skip_group_check=True),
                                    [b_pt, b_Vall], [b_po], signal=(l == 3))
                            if kt == 16 * go + 15:
                                rc, b_rc = rec[ring + slot]
                                pov = po[:, 0:260].rearrange("p (l d) -> p l d", l=4)
                                T.op("dve", lambda e, rc=rc, pov=pov: e.reciprocal(out=rc[:], in_=pov[:, :, 64]), [b_po], [b_rc])
                                for l in range(4):
                                    T.op("dve", lambda e, l=l, rc=rc, pov=pov, go=go: e.tensor_scalar(
                                        out=yatt[:, 4 * go + l, h * 64:(h + 1) * 64], in0=pov[:, l, 0:64],
                                        scalar1=rc[:, l:l + 1], scalar2=None, op0=ALU.mult), [b_po, b_rc], [b_yatt])

                    loadK(0)
                    NI = len(items)
                    per_h = NI // 8
                    for n in range(NI + 3):
                        if n < NI:
                            if n % per_h == 0:
                                hh = n // per_h
                                if hh + 1 < 8:
                                    loadK(hh + 1)
                            emitS(n)
                        if 2 <= n < NI + 2:
                            emitE(n - 2)
                        if n >= 3:
                            emitPV(n - 3)
                    if dbg:
                        T.dma("sp", lambda e: e.dma_start(out=dbgo["d_yatt"], in_=yatt[:].rearrange("p t c -> p (t c)")),
                              [b_yatt], [], "dbg")
                T.barrier()
                psum_std()

            T.barrier()
            with ExitStack() as stD:
                Wu = sb(stD, "Wu", [128, 8, 512], BF16); b_Wu = Buf()
                Wv2 = sb(stD, "Wv2", [128, 8, 512], BF16); b_Wv2 = Buf()
                Wg = sb(stD, "Wg", [128, 8, 2048], BF16); b_Wg = Buf()
                Wbs = sb(stD, "Wbs", [128, 4, D], BF16); b_Wbs = Buf()
                Wba = sb(stD, "Wba", [128, 4, D], BF16); b_Wba = Buf()
                Wout = sb(stD, "Wout", [128, 8, D], BF16); b_Wout = Buf()
                wload2(Wu[:], "wu", b_Wu)
                wload2(Wv2[:], "wv2", b_Wv2)
                gpost = sb(stD, "gpost", [128, D], F32); b_gpost = Buf()
                gsg = sb(stD, "gsg", [128, 512], F32); b_gsg = Buf()
                bsg = sb(stD, "bsg", [128, 512], F32); b_bsg = Buf()
                bsp = sb(stD, "bsp", [128, 4, 128], F32); b_bsp = Buf()
                wsTm = sb(stD, "wsTm", [128, 8, 128], BF16); b_wsTm = Buf()
                T.dma("sp", lambda e: e.dma_start(out=gpost[:], in_=g_post.partition_broadcast(128)), [], [b_gpost], "c2")
                T.dma("sp", lambda e: e.dma_start(out=gsg[:], in_=g_sgu.partition_broadcast(128)), [], [b_gsg], "c2")
                T.dma("sp", lambda e: e.dma_start(out=bsg[:], in_=b_sgu.partition_broadcast(128)), [], [b_bsg], "c2")
                T.dma("sp", lambda e: e.dma_start(out=bsp[:], in_=bsp4.rearrange("p (k t) -> p k t", k=4)[:, :, 0:128]), [], [b_bsp], "c2")
                h1t = sbr(stD, "h1t", [128, D], F32, 2)
                T.dma("sp", lambda e: e.dma_start(out=h1t[0][0][:], in_=wsT), [], [h1t[0][1]], "c2")
                T.op("dve", lambda e: e.tensor_tensor(out=wsTm[:], in0=h1t[0][0][:].rearrange("p (g t) -> p g t", g=8),
                                                      in1=triU[:].unsqueeze(1).broadcast_to([128, 8, 128]), op=ALU.mult),
                     [h1t[0][1], b_triU], [b_wsTm])

                xg = sbr(stD, "xg", [128, D], F32, 2)
                xsd = sbr(stD, "xsd", [128, D], BF16, 2)
                xsTg = sbr(stD, "xsTg", [128, 8, 512], BF16, 2)
                uT = sbr(stD, "uT", [128, 4, 512], BF16, 1)
                vg = sbr(stD, "vg", [128, 512], F32, 4)
                vnb = sbr(stD, "vnb", [128, 512], BF16, 4)
                lnst = sbr(stD, "lnst", [128, 16], F32, 2)
                t1 = sbr(stD, "t1", [128, 512], F32, 1)
                ysT = sbr(stD, "ysT", [128, 4, 512], BF16, 2)
                yaT = sbr(stD, "yaT", [128, 4, 512], BF16, 2)
                tg1 = sbr(stD, "tg1", [128, 512], F32, 2)
                tg2 = sbr(stD, "tg2", [128, 512], F32, 2)
                mT = sbr(stD, "mT", [128, 8, 512], BF16, 1)
                bank_n = [0]

                def bank():
                    r = pf[bank_n[0] % 6]
                    bank_n[0] += 1
                    return r

                def rstd_batch(s, b_s, n, src0, dst0, scale, extra_bias=0.0):
                    T.op("act", lambda e: e.activation(out=s[:, dst0:dst0 + n], in_=s[:, src0:src0 + n], func=AF.Ln,
                                                       scale=scale, bias=EPS), [b_s], [b_s])
                    T.op("act", lambda e: e.activation(out=s[:, dst0:dst0 + n], in_=s[:, dst0:dst0 + n], func=AF.Exp,
                                                       scale=-0.5, bias=extra_bias), [b_s], [b_s])

                def genX(go):
                    dT, b_dT = xsTg[go % 2]
                    for pr in range(2):
                        s, b_s = stat[fr_n[0] % 4]
                        fr_n[0] += 1
                        T.op("pool", lambda e: e.memset(s[:, 0:2], 0.0), [], [b_s])
                        for q in range(2):
                            l = 2 * pr + q
                            i = 4 * go + l
                            xt, b_x = xg[q]
                            T.dma("sp", lambda e, xt=xt, i=i: e.dma_start(out=xt[:], in_=xo[i * 128:(i + 1) * 128, :]),
                                  [], [b_x], f"xg{q}")
                            T.op("act", lambda e, xt=xt, q=q: e.activation(out=junk[:], in_=xt[:], func=AF.Square,
                                                                           accum_out=s[:, q:q + 1]), [b_x], [b_junk, b_s])
                        rstd_batch(s, b_s, 2, 0, 2, 1.0 / D)
                        for q in range(2):
                            l = 2 * pr + q
                            xt, b_x = xg[q]
                            xst, b_xs = xsd[q]
                            T.op("dve", lambda e, xt=xt, xst=xst, q=q: e.scalar_tensor_tensor(
                                out=xst[:], in0=xt[:], scalar=s[:, 2 + q:3 + q], in1=gbc[:], op0=ALU.mult, op1=ALU.mult),
                                [b_x, b_s, b_gbc], [b_xs])
                        yield True
                        for q in range(2):
                            l = 2 * pr + q
                            xst, b_xs = xsd[q]
                            transpose8(xst, b_xs, dT[:, :, l * 128:(l + 1) * 128], b_dT, q, "act" if q == 0 else "dve")
                        yield
                    ut, b_ut = uT[0]
                    for c in range(4):
                        pu, b_pu = bank()
                        for kc in range(8):
                            T.op("pe", lambda e, kc=kc, c=c, pu=pu: e.matmul(
                                pu[:], lhsT=Wu[:, kc, c * 128:(c + 1) * 128], rhs=dT[:, kc, :],
                                start=(kc == 0), stop=(kc == 7)), [b_Wu, b_dT], [b_pu], signal=(kc == 7))
                        T.op("act", lambda e, c=c, pu=pu: e.activation(out=ut[:, c, :], in_=pu[:], func=AF.Gelu_apprx_tanh),
                             [b_pu], [b_ut])
                        yield
                    ls, b_ls = lnst[go % 2]
                    T.op("pool", lambda e: e.memset(ls[:], 0.0), [], [b_ls])
                    for l in range(4):
                        pv, b_pv = bank()
                        for kc in range(8):
                            T.op("pe", lambda e, kc=kc, l=l, pv=pv: e.matmul(
                                pv[:], lhsT=dT[:, kc, l * 128:(l + 1) * 128], rhs=Wv2[:, kc, :],
                                start=(kc == 0), stop=(kc == 7)), [b_Wv2, b_dT], [b_pv], signal=(kc == 7))
                        v_, b_v = vg[l]
                        T.op("act", lambda e, l=l, v_=v_, pv=pv: e.activation(
                            out=v_[:], in_=pv[:], func=AF.Gelu_apprx_tanh, accum_out=ls[:, l:l + 1]), [b_pv], [b_v, b_ls])
                        T.op("act", lambda e, l=l, v_=v_: e.activation(out=junk[:, 0:512], in_=v_[:], func=AF.Square,
                                                                       accum_out=ls[:, 4 + l:5 + l]), [b_v], [b_junk, b_ls])
                        yield
                    T.op("dve", lambda e: e.tensor_scalar(out=ls[:, 0:4], in0=ls[:, 0:4], scalar1=1.0 / 512, scalar2=None,
                                                          op0=ALU.mult), [b_ls], [b_ls])
                    T.op("dve", lambda e: e.tensor_tensor(out=ls[:, 8:12], in0=ls[:, 0:4], in1=ls[:, 0:4], op=ALU.mult),
                         [b_ls], [b_ls])
                    T.op("dve", lambda e: e.scalar_tensor_tensor(out=ls[:, 4:8], in0=ls[:, 4:8], scalar=1.0 / 512,
                                                                 in1=ls[:, 8:12], op0=ALU.mult, op1=ALU.subtract),
                         [b_ls], [b_ls])
                    rstd_batch(ls, b_ls, 4, 4, 12, 1.0)
                    for l in range(4):
                        v_, b_v = vg[l]
                        vb, b_vb = vnb[l]
                        T.op("dve", lambda e, l=l, v_=v_: e.tensor_scalar(
                            out=v_[:], in0=v_[:], scalar1=ls[:, l:l + 1], scalar2=ls[:, 12 + l:13 + l],
                            op0=ALU.subtract, op1=ALU.mult), [b_v, b_ls], [b_v])
                        T.op("dve", lambda e, v_=v_: e.tensor_tensor(out=v_[:], in0=v_[:], in1=gsg[:], op=ALU.mult),
                             [b_v, b_gsg], [b_v])
                        T.op("pool", lambda e, v_=v_, vb=vb: e.tensor_tensor(out=vb[:], in0=v_[:], in1=bsg[:], op=ALU.add),
                             [b_v, b_bsg], [b_vb])
                    yield True
                    ys, b_ys = ysT[go % 2]
                    for k in range(4):
                        pA, b_pA = bank()
                        pB, b_pB = bank()
                        for l in range(4):
                            vb, b_vb = vnb[l]
                            T.op("pe", lambda e, l=l, k=k, vb=vb, pA=pA: e.matmul(
                                pA[:, l * 128:(l + 1) * 128], lhsT=vb[:, k * 128:(k + 1) * 128], rhs=wsTm[:, 2 * k, :],
                                start=True, stop=True), [b_vb, b_wsTm], [b_pA], signal=(l == 3))
                        for l in range(4):
                            vb, b_vb = vnb[l]
                            T.op("pe", lambda e, l=l, k=k, vb=vb, pB=pB: e.matmul(
                                pB[:, l * 128:(l + 1) * 128], lhsT=vb[:, k * 128:(k + 1) * 128], rhs=wsTm[:, 2 * k + 1, :],
                                start=True, stop=True), [b_vb, b_wsTm], [b_pB], signal=(l == 3))
                        tt, b_tt = t1[0]
                        T.op("dve", lambda e, k=k, tt=tt, pA=pA: e.tensor_tensor(
                            out=tt[0:64, :].rearrange("p (l t) -> p l t", l=4), in0=pA[0:64, :].rearrange("p (l t) -> p l t", l=4),
                            in1=bsp[0:64, k, :].unsqueeze(1).broadcast_to([64, 4, 128]), op=ALU.add), [b_pA, b_bsp], [b_tt])
                        T.op("dve", lambda e, k=k, tt=tt, pB=pB: e.tensor_tensor(
                            out=tt[64:128, :].rearrange("p (l t) -> p l t", l=4), in0=pB[64:128, :].rearrange("p (l t) -> p l t", l=4),
                            in1=bsp[64:128, k, :].unsqueeze(1).broadcast_to([64, 4, 128]), op=ALU.add), [b_pB, b_bsp], [b_tt])
                        T.op("pool", lambda e, k=k, tt=tt: e.tensor_tensor(
                            out=ys[:, k, :], in0=tt[:], in1=ut[:, k, :], op=ALU.mult), [b_tt, b_ut], [b_ys])
                        yield
                    ya, b_ya = yaT[go % 2]
                    for l in range(4):
                        i = 4 * go + l
                        transpose8(yatt[:, i, :], b_yatt, ya[:, :, l * 128:(l + 1) * 128], b_ya, l % 2,
                                   "act" if l % 2 == 0 else "dve", n=4)
                        yield

                def genY(go):
                    dT, b_dT = xsTg[go % 2]
                    ys, b_ys = ysT[go % 2]
                    ya, b_ya = yaT[go % 2]
                    mt, b_mt = mT[0]
                    for oc in range(8):
                        pG1, b_pG1 = bank()
                        pG2, b_pG2 = bank()
                        pP1, b_pP1 = bank()
                        pP2, b_pP2 = bank()
                        for kc in range(8):
                            T.op("pe", lambda e, kc=kc, oc=oc, pG1=pG1: e.matmul(
                                pG1[:], lhsT=Wg[:, kc, oc * 128:(oc + 1) * 128], rhs=dT[:, kc, :],
                                start=(kc == 0), stop=(kc == 7)), [b_Wg, b_dT], [b_pG1], signal=(kc == 7))
                        for kc in range(8):
                            T.op("pe", lambda e, kc=kc, oc=oc, pG2=pG2: e.matmul(
                                pG2[:], lhsT=Wg[:, kc, 1024 + oc * 128:1024 + (oc + 1) * 128], rhs=dT[:, kc, :],
                                start=(kc == 0), stop=(kc == 7)), [b_Wg, b_dT], [b_pG2], signal=(kc == 7))
                        for c in range(4):
                            T.op("pe", lambda e, c=c, oc=oc, pP1=pP1: e.matmul(
                                pP1[:], lhsT=Wbs[:, c, oc * 128:(oc + 1) * 128], rhs=ys[:, c, :],
                                start=(c == 0), stop=(c == 3)), [b_Wbs, b_ys], [b_pP1], signal=(c == 3))
                        for c in range(4):
                            T.op("pe", lambda e, c=c, oc=oc, pP2=pP2: e.matmul(
                                pP2[:], lhsT=Wba[:, c, oc * 128:(oc + 1) * 128], rhs=ya[:, c, :],
                                start=(c == 0), stop=(c == 3)), [b_Wba, b_ya], [b_pP2], signal=(c == 3))
                        a1, b_a1 = tg1[oc % 2]
                        a2, b_a2 = tg2[oc % 2]
                        T.op("act", lambda e, a1=a1, pG1=pG1: e.activation(out=a1[:], in_=pG1[:], func=AF.Tanh, scale=0.5),
                             [b_pG1], [b_a1])
                        T.op("act", lambda e, a2=a2, pG2=pG2: e.activation(out=a2[:], in_=pG2[:], func=AF.Tanh, scale=0.5),
                             [b_pG2], [b_a2])
                        T.op("dve", lambda e, a1=a1, pP1=pP1: e.scalar_tensor_tensor(
                            out=a1[:], in0=a1[:], scalar=1.0, in1=pP1[:], op0=ALU.add, op1=ALU.mult),
                            [b_a1, b_pP1], [b_a1])
                        T.op("dve", lambda e, a2=a2, pP2=pP2: e.scalar_tensor_tensor(
                            out=a2[:], in0=a2[:], scalar=1.0, in1=pP2[:], op0=ALU.add, op1=ALU.mult),
                            [b_a2, b_pP2], [b_a2])
                        T.op("pool", lambda e, a1=a1, a2=a2, oc=oc: e.tensor_tensor(
                            out=mt[:, oc, :], in0=a1[:], in1=a2[:], op=ALU.add), [b_a1, b_a2], [b_mt])
                        yield
                    for l in range(4):
                        i = 4 * go + l
                        s, b_s = stat[fr_n[0] % 4]
                        fr_n[0] += 1
                        T.op("pool", lambda e, s=s: e.memset(s[:, 0:2], 0.0), [], [b_s])
                        banks = []
                        for half in range(2):
                            pH, b_pH = bank()
                            banks.append((pH, b_pH))
                            for oc in range(8):
                                T.op("pe", lambda e, oc=oc, l=l, half=half, pH=pH: e.matmul(
                                    pH[:], lhsT=mt[:, oc, l * 128:(l + 1) * 128],
                                    rhs=Wout[:, oc, half * 512:(half + 1) * 512],
                                    start=(oc == 0), stop=(oc == 7)), [b_mt, b_Wout], [b_pH], signal=(oc == 7))
                            T.op("act", lambda e, s=s, pH=pH, half=half: e.activation(
                                out=junk[:, half * 512:(half + 1) * 512], in_=pH[:], func=AF.Square, scale=0.5,
                                accum_out=s[:, half:half + 1]), [b_pH], [b_junk, b_s])
                        T.op("dve", lambda e, s=s: e.tensor_tensor(out=s[:, 2:3], in0=s[:, 0:1], in1=s[:, 1:2], op=ALU.add),
                             [b_s], [b_s])
                        rstd_batch(s, b_s, 1, 2, 3, 1.0 / D, extra_bias=math.log(0.5))
                        ht, b_ht = h1t[l % 2]
                        for half in range(2):
                            pH, b_pH = banks[half]
                            T.op("dve", lambda e, s=s, ht=ht, pH=pH, half=half: e.scalar_tensor_tensor(
                                out=ht[:, half * 512:(half + 1) * 512], in0=pH[:], scalar=s[:, 3:4],
                                in1=gpost[:, half * 512:(half + 1) * 512], op0=ALU.mult, op1=ALU.mult),
                                [b_pH, b_s, b_gpost], [b_ht])
                        T.dma("sp", lambda e, ht=ht, i=i: e.dma_start(out=h1scr[i * 128:(i + 1) * 128, :], in_=ht[:]),
                              [b_ht], [b_h1scr], f"h1s{l % 2}")
                        if dbg:
                            T.dma("sp", lambda e, ht=ht, i=i: e.dma_start(out=dbgo["d_h1"][i * 128:(i + 1) * 128, :], in_=ht[:]),
                                  [b_ht], [], "dbg")
                        yield

                def run_interleaved(ga, gb, nb_per_a):
                    da = db = False
                    while not (da and db):
                        if not da:
                            try:
                                next(ga)
                            except StopIteration:
                                da = True
                        for _ in range(nb_per_a):
                            if db:
                                break
                            try:
                                if next(gb) is True:
                                    break
                            except StopIteration:
                                db = True
                        if da and not db:
                            for _ in gb:
                                pass
                            db = True

                gx0 = genX(0)
                next(gx0)
                next(gx0)
                wload2(Wg[:], "wg", b_Wg, nsplit=4)
                wload2(Wbs[:], "wbs", b_Wbs)
                wload2(Wba[:], "wba", b_Wba)
                wload2(Wout[:], "wout", b_Wout)
                for _ in gx0:
                    pass
                for go in range(NGO):
                    gy = genY(go)
                    gx = genX(go + 1) if go + 1 < NGO else iter(())
                    run_interleaved(gy, gx, 3)

        T.barrier()
        with ExitStack() as stE:
            Wup = sb(stE, "Wup", [128, 8, 8, 512], BF16)
            Wdn = sb(stE, "Wdn", [128, 32, D], BF16)
            b_Wup = [Buf() for _ in range(8)]
            b_Wdn = [Buf() for _ in range(8)]
            gpost2 = sb(stE, "gpost2", [128, D], F32); b_gpost2 = Buf()
            T.dma("sp", lambda e: e.dma_start(out=gpost2[:], in_=g_fpost.partition_broadcast(128)), [], [b_gpost2], "c3")
            T.dma("sp", lambda e: e.dma_start(out=gbc[:], in_=g_fpre.partition_broadcast(128)), [], [b_gbc], "c3")
            hg = sbr(stE, "hg", [128, D], F32, 4)
            xe = sbr(stE, "xe", [128, D], F32, 2)
            xs2 = sbr(stE, "xs2", [128, D], BF16, 2)
            x2T = sbr(stE, "x2T", [128, 8, 256], BF16, 2)
            rt = sbr(stE, "rt", [128, 256], F32, 3)
            hT = sbr(stE, "hT", [128, 32, 256], BF16, 1)
            ot = sbr(stE, "ot", [128, D], F32, 1)
            NSG = NO // 2
            bank_n = [0]

            def bank():
                r = pf[bank_n[0] % 6]
                bank_n[0] += 1
                return r

            def frontE(sg):
                dT, b_dT = x2T[sg % 2]
                s, b_s = stat[fr_n[0] % 4]
                fr_n[0] += 1
                T.op("pool", lambda e: e.memset(s[:, 0:2], 0.0), [], [b_s])
                for l in range(2):
                    i = 2 * sg + l
                    ht, b_h = hg[i % 4]
                    xt, b_x = xe[l]
                    T.dma("sp", lambda e, ht=ht, i=i: e.dma_start(out=ht[:], in_=h1scr[i * 128:(i + 1) * 128, :]),
                          [b_h1scr], [b_h], f"hg{i % 4}")
                    T.dma("sp", lambda e, xt=xt, i=i: e.dma_start(out=xt[:], in_=xo[i * 128:(i + 1) * 128, :]),
                          [], [b_x], f"xe{l}")
                    T.op("dve", lambda e, ht=ht, xt=xt: e.tensor_tensor(out=ht[:], in0=ht[:], in1=xt[:], op=ALU.add),
                         [b_h, b_x], [b_h])
                    T.op("act", lambda e, ht=ht, l=l: e.activation(out=junk[:], in_=ht[:], func=AF.Square,
                                                                   accum_out=s[:, l:l + 1]), [b_h], [b_junk, b_s])
                T.op("act", lambda e: e.activation(out=s[:, 2:4], in_=s[:, 0:2], func=AF.Ln, scale=1.0 / D, bias=EPS),
                     [b_s], [b_s])
                T.op("act", lambda e: e.activation(out=s[:, 2:4], in_=s[:, 2:4], func=AF.Exp, scale=-0.5), [b_s], [b_s])
                for l in range(2):
                    i = 2 * sg + l
                    ht, b_h = hg[i % 4]
                    xst, b_xs = xs2[l]
                    T.op("dve", lambda e, ht=ht, xst=xst, l=l: e.scalar_tensor_tensor(
                        out=xst[:], in0=ht[:], scalar=s[:, 2 + l:3 + l], in1=gbc[:], op0=ALU.mult, op1=ALU.mult),
                        [b_h, b_s, b_gbc], [b_xs])

            def frontET(sg):
                dT, b_dT = x2T[sg % 2]
                for l in range(2):
                    xst, b_xs = xs2[l]
                    transpose8(xst, b_xs, dT[:, :, l * 128:(l + 1) * 128], b_dT, l, "act" if l == 0 else "dve")

            def upE(sg, mid=None):
                dT, b_dT = x2T[sg % 2]
                h_, b_hT = hT[0]
                for j in range(32):
                    if j == 12 and mid is not None:
                        mid()
                    pu, b_pu = bank()
                    for kc in range(8):
                        T.op("pe", lambda e, kc=kc, j=j, pu=pu: e.matmul(
                            pu[:, 0:256], lhsT=Wup[:, j // 4, kc, (j % 4) * 128:(j % 4 + 1) * 128], rhs=dT[:, kc, :],
                            start=(kc == 0), stop=(kc == 7)), [b_Wup[j // 4], b_dT], [b_pu], signal=(kc == 7))
                    r_, b_r = rt[j % 3]
                    T.op("act", lambda e, r_=r_, pu=pu: e.activation(out=r_[:], in_=pu[:, 0:256], func=AF.Relu),
                         [b_pu], [b_r])
                    T.op("dve" if j % 2 == 0 else "pool", lambda e, r_=r_, j=j: e.tensor_tensor(
                        out=h_[:, j, :], in0=r_[:], in1=r_[:], op=ALU.mult), [b_r], [b_hT])

            def downE(sg):
                h_, b_hT = hT[0]
                for l in range(2):
                    i = 2 * sg + l
                    pD0, b_pD0 = bank()
                    pD1, b_pD1 = bank()
                    for half, (pD, b_pD) in enumerate(((pD0, b_pD0), (pD1, b_pD1))):
                        for j in range(32):
                            T.op("pe", lambda e, j=j, l=l, half=half, pD=pD: e.matmul(
                                pD[:], lhsT=h_[:, j, l * 128:(l + 1) * 128], rhs=Wdn[:, j, half * 512:(half + 1) * 512],
                                start=(j == 0), stop=(j == 31)), [b_hT, b_Wdn[j // 4]], [b_pD], signal=(j == 31))
                    s, b_s = stat[fr_n[0] % 4]
                    fr_n[0] += 1
                    T.op("pool", lambda e, s=s: e.memset(s[:, 0:2], 0.0), [], [b_s])
                    T.op("act", lambda e, s=s, pD0=pD0: e.activation(out=junk[:, 0:512], in_=pD0[:], func=AF.Square,
                                                                     accum_out=s[:, 0:1]), [b_pD0], [b_junk, b_s])
                    T.op("act", lambda e, s=s, pD1=pD1: e.activation(out=junk[:, 512:1024], in_=pD1[:], func=AF.Square,
                                                                     accum_out=s[:, 1:2]), [b_pD1], [b_junk, b_s])
                    T.op("dve", lambda e, s=s: e.tensor_tensor(out=s[:, 2:3], in0=s[:, 0:1], in1=s[:, 1:2], op=ALU.add),
                         [b_s], [b_s])
                    T.op("act", lambda e, s=s: e.activation(out=s[:, 3:4], in_=s[:, 2:3], func=AF.Ln, scale=1.0 / D, bias=EPS),
                         [b_s], [b_s])
                    T.op("act", lambda e, s=s: e.activation(out=s[:, 4:5], in_=s[:, 3:4], func=AF.Exp, scale=-0.5), [b_s], [b_s])
                    o_, b_o = ot[0]
                    ht, b_h = hg[i % 4]
                    T.op("dve", lambda e, s=s, o_=o_, pD0=pD0: e.scalar_tensor_tensor(
                        out=o_[:, 0:512], in0=pD0[:], scalar=s[:, 4:5], in1=gpost2[:, 0:512], op0=ALU.mult, op1=ALU.mult),
                        [b_pD0, b_s, b_gpost2], [b_o])
                    T.op("dve", lambda e, s=s, o_=o_, pD1=pD1: e.scalar_tensor_tensor(
                        out=o_[:, 512:1024], in0=pD1[:], scalar=s[:, 4:5], in1=gpost2[:, 512:1024], op0=ALU.mult, op1=ALU.mult),
                        [b_pD1, b_s, b_gpost2], [b_o])
                    T.op("dve", lambda e, o_=o_, ht=ht: e.tensor_tensor(out=o_[:], in0=o_[:], in1=ht[:], op=ALU.add),
                         [b_o, b_h], [b_o])
                    T.dma("sp", lambda e, o_=o_, i=i: e.dma_start(out=out[i * 128:(i + 1) * 128, :], in_=o_[:]),
                          [b_o], [], f"ost{i % 2}")

            wupv = wscr["wup"].rearrange("p (pc kc n) -> p pc kc n", pc=8, kc=8)
            wdnv = wscr["wdn"].rearrange("p (kc n) -> p kc n", kc=32)
            for p in range(8):
                T.dma("act", lambda e, p=p: e.dma_start(
                    out=Wdn[:, p * 4:(p + 1) * 4, :], in_=wdnv[:, p * 4:(p + 1) * 4, :]),
                    [b_wscr["wdn"]], [b_Wdn[p]], f"w_dn{p}")
            frontE(0)
            for p in range(8):
                T.dma("sp", lambda e, p=p: e.dma_start(out=Wup[:, p, :, :], in_=wupv[:, p, :, :]),
                      [b_wscr["wup"]], [b_Wup[p]], f"w_up{p}")
            frontET(0)
            for sg in range(NSG):
                upE(sg, (lambda sg=sg: frontE(sg + 1)) if sg + 1 < NSG else None)
                if sg + 1 < NSG:
                    frontET(sg + 1)
                downE(sg)
            for k in ("ost0", "ost1", "dbg"):
                if k in T.semobj:
                    nc.sync.wait_ge(T.semobj[k], T.cnt[k])
        psum_free()
        print(f"[build] ops={T.nops} waits={T.nwaits} sems={len(T.semobj)}")
    return nc


def _host_inputs(inputs, NT):
    f = lambda a: np.ascontiguousarray(np.asarray(a, dtype=np.float32))
    x = f(inputs["x"])
    w_in = f(inputs["w_in"])[0]
    wz, wq_, wk_, wv_, wf_, wg_ = np.split(w_in, [1024, 1536, 2048, 2560, 2568], axis=1)
    common = {
        "wkvf": f(np.concatenate([wk_, wv_, wf_], axis=1)),
        "wq": f(wq_),
        "wu": f(wz[:, :512]),
        "wv2": f(wz[:, 512:]),
        "wg": f(wg_),
        "wbs": f(inputs["w_branch_sgu"])[0],
        "wba": f(inputs["w_branch_attn"])[0],
        "wout": f(inputs["w_out"])[0],
        "wup": f(inputs["w_up"])[0],
        "wdn": f(inputs["w_down"])[0],
        "g_pre": f(inputs["g_mix_pre"]),
        "g_post": f(inputs["g_mix_post"]),
        "g_fpre": f(inputs["g_ffn_pre"]),
        "g_fpost": f(inputs["g_ffn_post"]),
        "g_sgu": f(inputs["g_sgu"]),
        "b_sgu": f(inputs["b_sgu"]),
        "b_fg": f(inputs["b_forget"]),
        "wsT": f(np.transpose(f(inputs["w_spatial"])[0], (2, 0, 1)).reshape(128, 8 * 128)),
        "c_id": np.eye(128, dtype=np.float32),
        "c_tri": np.triu(np.ones((128, 128), np.float32)),
        "c_one": np.ones((128, 128), np.float32),
    }
    bs = f(inputs["b_spatial"])[0]
    bsp = np.repeat(bs.reshape(4, 2, 1, 128), 64, axis=2).reshape(4, 128, 128)
    bsp4 = np.tile(bsp[:, :, None, :], (1, 1, 4, 1)).reshape(4, 128, 512)
    common["bsp4"] = f(np.transpose(bsp4, (1, 0, 2)).reshape(128, 4 * 512))
    maps = []
    s_idx = np.arange(128)[:, None]
    t_idx = np.arange(128)[None, :]
    for c in range(8):
        b, j = c // 4, c % 4
        xb = x[b]
        xo = xb.reshape(NT, 128, D)[j::4].reshape(-1, D)
        sel = np.zeros((128, 4), np.float32)
        sel[:, j] = 1.0
        madd = np.zeros((128, 4, 128), np.float32)
        for m in range(4):
            if m == j:
                madd[:, m, :] = np.where(s_idx <= t_idx, 0.0, NEG)
            elif m > j:
                madd[:, m, :] = NEG
        d = dict(common)
        d["xf"] = f(xb)
        d["xo"] = f(xo)
        d["c_sel"] = sel
        d["c_madd"] = madd.reshape(128, 512)
        maps.append(d)
    return maps


_NC_CACHE = {}


def kernel(**inputs):
    x = np.asarray(inputs["x"])
    B, S, _ = x.shape
    NT = S // 128
    if NT not in _NC_CACHE:
        _NC_CACHE[NT] = build_nc(NT)
    nc = _NC_CACHE[NT]
    maps = _host_inputs(inputs, NT)
    res = run_bass_kernel_spmd(nc, maps, core_ids=list(range(8)))
    out = np.zeros((B, NT, 128, D), np.float32)
    for c in range(8):
        b, j = c // 4, c % 4
        out[b, j::4] = np.asarray(res.results[c]["out"], dtype=np.float32).reshape(NT // 4, 128, D)
    return out.reshape(B, S, D)
```
